# Optimizing a Trainium2 kernel written in Bass

```python
import math
import jax, jax.numpy as jnp
from jax import lax
import numpy as np

D_MODEL = 2048
BATCH = 4
SEQ = 4096
DEPTH = 2

HEAD_DIM_A = 64
N_Q_A = 32
N_KV_A = 4
GROUP_A = N_Q_A // N_KV_A
WINDOW = 128
BLOCK_A = WINDOW
HEAD_DIM_B = 128
N_HEADS_B = D_MODEL // (2 * HEAD_DIM_B)
Q_BLOCK_B = 128
SUBLN_EPS = 1e-5
ROPE_THETA = 10000.0
D_FF = ((8 * D_MODEL // 3 + 255) // 256) * 256
CONV_WIDTH = 3
LN_EPS = 1e-5
ALPHA = (2 * DEPTH) ** 0.25
BETA = (8 * DEPTH) ** -0.25
NEG_INF = -1e30

W_QA = N_Q_A * HEAD_DIM_A
W_KA = N_KV_A * HEAD_DIM_A
W_VA = N_KV_A * HEAD_DIM_A
W_QB = N_HEADS_B * 2 * HEAD_DIM_B
W_KB = N_HEADS_B * 2 * HEAD_DIM_B
W_VB = N_HEADS_B * 2 * HEAD_DIM_B
W_GATES = 2 * D_MODEL
IN_SIZES = (W_QA, W_KA, W_VA, W_QB, W_KB, W_VB, W_GATES)
IN_WIDTH = sum(IN_SIZES)
SPLIT_POINTS = tuple(int(v) for v in np.cumsum(IN_SIZES)[:-1])
OUT_A = N_Q_A * HEAD_DIM_A
OUT_B = N_HEADS_B * 2 * HEAD_DIM_B

kernel_name = "hybrid_swa_sink_diffattn_convffn_deepnorm"


def rope_cos_sin(seq, dim):
    inv = 1.0 / (ROPE_THETA ** (jnp.arange(0, dim, 2, dtype=jnp.float32) / dim))
    ang = jnp.arange(seq, dtype=jnp.float32)[:, None] * inv[None, :]
    return jnp.cos(ang), jnp.sin(ang)


def apply_rope(x, cos, sin):
    half = x.shape[-1] // 2
    x1, x2 = x[..., :half], x[..., half:]
    c = cos.astype(x.dtype)
    s = sin.astype(x.dtype)
    return jnp.concatenate([x1 * c - x2 * s, x2 * c + x1 * s], axis=-1)


def layer_norm(x, g, b):
    x32 = x.astype(jnp.float32)
    mu = jnp.mean(x32, axis=-1, keepdims=True)
    var = jnp.mean(jnp.square(x32 - mu), axis=-1, keepdims=True)
    y = (x32 - mu) * lax.rsqrt(var + LN_EPS) * g.astype(jnp.float32) + b.astype(jnp.float32)
    return y.astype(x.dtype)


def sliding_window_sink_attention(q, k, v, sinks):
    B, S = q.shape[0], q.shape[1]
    nb = S // BLOCK_A
    qb = q.reshape(B, nb, BLOCK_A, N_KV_A, GROUP_A, HEAD_DIM_A)
    kb = k.reshape(B, nb, BLOCK_A, N_KV_A, HEAD_DIM_A)
    vb = v.reshape(B, nb, BLOCK_A, N_KV_A, HEAD_DIM_A)
    pad = ((0, 0), (1, 0), (0, 0), (0, 0), (0, 0))
    k_band = jnp.concatenate([jnp.pad(kb, pad)[:, :-1], kb], axis=2)
    v_band = jnp.concatenate([jnp.pad(vb, pad)[:, :-1], vb], axis=2)
    scale = HEAD_DIM_A ** -0.5
    s = jnp.einsum('bnqhgd,bnkhd->bnhgqk', qb, k_band).astype(jnp.float32) * scale
    qi = jnp.arange(BLOCK_A)[:, None]
    kj = jnp.arange(2 * BLOCK_A)[None, :]
    in_window = (kj <= BLOCK_A + qi) & (kj > BLOCK_A + qi - WINDOW)
    key_pos = jnp.arange(nb)[:, None] * BLOCK_A + jnp.arange(2 * BLOCK_A)[None, :] - BLOCK_A
    valid = key_pos >= 0
    mask = in_window[None, :, :] & valid[:, None, :]
    s = jnp.where(mask[None, :, None, None], s, NEG_INF)
    sink = sinks.astype(jnp.float32).reshape(N_KV_A, GROUP_A)[None, None, :, :, None, None]
    m = jnp.maximum(jnp.max(s, axis=-1, keepdims=True), sink)
    p = jnp.exp(s - m)
    denom = jnp.sum(p, axis=-1, keepdims=True) + jnp.exp(sink - m)
    w = (p / denom).astype(v.dtype)
    o = jnp.einsum('bnhgqk,bnkhd->bnqhgd', w, v_band)
    return o.reshape(B, S, OUT_A)


def differential_attention(q, k, v, lam, subln_g, lam_init):
    B, S = q.shape[0], q.shape[1]
    nb = S // Q_BLOCK_B
    q_blocks = q.reshape(B, nb, Q_BLOCK_B, N_HEADS_B, 2, HEAD_DIM_B).transpose(1, 0, 2, 3, 4, 5)
    key_pos = jnp.arange(S)
    scale = HEAD_DIM_B ** -0.5

    def one_block(args):
        qb, n = args
        s = jnp.einsum('bqhcd,bkhcd->bhcqk', qb, k).astype(jnp.float32) * scale
        q_pos = n * Q_BLOCK_B + jnp.arange(Q_BLOCK_B)
        causal = key_pos[None, :] <= q_pos[:, None]
        a = jax.nn.softmax(jnp.where(causal, s, NEG_INF), axis=-1)
        w = a[:, :, 0] - lam * a[:, :, 1]
        return jnp.einsum('bhqk,bkhe->bqhe', w.astype(v.dtype), v)

    o = lax.map(one_block, (q_blocks, jnp.arange(nb)))
    o = o.transpose(1, 0, 2, 3, 4).reshape(B, S, N_HEADS_B, 2 * HEAD_DIM_B)
    o32 = o.astype(jnp.float32)
    o32 = o32 * lax.rsqrt(jnp.mean(jnp.square(o32), axis=-1, keepdims=True) + SUBLN_EPS)
    o32 = o32 * subln_g.astype(jnp.float32) * (1.0 - lam_init)
    return o32.reshape(B, S, OUT_B).astype(q.dtype)


def conv_ffn(h, w_up, conv_w, conv_b, w_down):
    S = h.shape[1]
    u = jnp.einsum('bsd,df->bsf', h, w_up)
    up = jnp.pad(u, ((0, 0), (CONV_WIDTH - 1, 0), (0, 0)))
    u = conv_b + sum(conv_w[t] * up[:, t:t + S] for t in range(CONV_WIDTH))
    gate, val = jnp.split(u, 2, axis=-1)
    return jnp.einsum('bsf,fd->bsd', jax.nn.silu(gate) * val, w_down)


def setup_inputs(seed: int = 0) -> dict:
    key = jax.random.key(seed)
    ks = jax.random.split(key, 19)
    f32 = jnp.float32
    nrm = lambda k, shape, scale: jax.random.normal(k, shape, f32) * scale
    return {
        "x": nrm(ks[0], (BATCH, SEQ, D_MODEL), 1.0),
        "w_in": nrm(ks[1], (DEPTH, D_MODEL, IN_WIDTH), D_MODEL ** -0.5),
        "sinks": nrm(ks[2], (DEPTH, N_Q_A), 1.0),
        "lambda_q1": nrm(ks[3], (DEPTH, HEAD_DIM_B), 0.1),
        "lambda_k1": nrm(ks[4], (DEPTH, HEAD_DIM_B), 0.1),
        "lambda_q2": nrm(ks[5], (DEPTH, HEAD_DIM_B), 0.1),
        "lambda_k2": nrm(ks[6], (DEPTH, HEAD_DIM_B), 0.1),
        "subln_g": 1.0 + nrm(ks[7], (DEPTH, 2 * HEAD_DIM_B), 0.02),
        "w_proj_a": nrm(ks[8], (DEPTH, OUT_A, D_MODEL), OUT_A ** -0.5),
        "w_proj_b": nrm(ks[9], (DEPTH, OUT_B, D_MODEL), OUT_B ** -0.5),
        "w_out": nrm(ks[10], (DEPTH, D_MODEL, D_MODEL), BETA * D_MODEL ** -0.5),
        "ln1_g": 1.0 + nrm(ks[11], (DEPTH, D_MODEL), 0.02),
        "ln1_b": nrm(ks[12], (DEPTH, D_MODEL), 0.02),
        "w_up": nrm(ks[13], (DEPTH, D_MODEL, 2 * D_FF), D_MODEL ** -0.5),
        "conv_w": nrm(ks[14], (DEPTH, CONV_WIDTH, 2 * D_FF), CONV_WIDTH ** -0.5),
        "conv_b": nrm(ks[15], (DEPTH, 2 * D_FF), 0.02),
        "w_down": nrm(ks[16], (DEPTH, D_FF, D_MODEL), BETA * D_FF ** -0.5),
        "ln2_g": 1.0 + nrm(ks[17], (DEPTH, D_MODEL), 0.02),
        "ln2_b": nrm(ks[18], (DEPTH, D_MODEL), 0.02),
    }


def reference(x, w_in, sinks, lambda_q1, lambda_k1, lambda_q2, lambda_k2, subln_g,
              w_proj_a, w_proj_b, w_out, ln1_g, ln1_b, w_up, conv_w, conv_b, w_down,
              ln2_g, ln2_b):
    B, S = x.shape[0], x.shape[1]
    cos_a, sin_a = rope_cos_sin(S, HEAD_DIM_A)
    cos_b, sin_b = rope_cos_sin(S, HEAD_DIM_B)
    for l in range(DEPTH):
        proj = jnp.einsum('bsd,de->bse', x, w_in[l])
        qa, ka, va, qb, kb, vb, gates = jnp.split(proj, SPLIT_POINTS, axis=-1)
        qa = apply_rope(qa.reshape(B, S, N_Q_A, HEAD_DIM_A), cos_a[:, None, :], sin_a[:, None, :])
        ka = apply_rope(ka.reshape(B, S, N_KV_A, HEAD_DIM_A), cos_a[:, None, :], sin_a[:, None, :])
        va = va.reshape(B, S, N_KV_A, HEAD_DIM_A)
        o_a = sliding_window_sink_attention(qa, ka, va, sinks[l])

        qb = apply_rope(qb.reshape(B, S, N_HEADS_B, 2, HEAD_DIM_B), cos_b[:, None, None, :], sin_b[:, None, None, :])
        kb = apply_rope(kb.reshape(B, S, N_HEADS_B, 2, HEAD_DIM_B), cos_b[:, None, None, :], sin_b[:, None, None, :])
        vb = vb.reshape(B, S, N_HEADS_B, 2 * HEAD_DIM_B)
        lam_init = 0.8 - 0.6 * math.exp(-0.3 * l)
        lam = (jnp.exp(jnp.sum(lambda_q1[l].astype(jnp.float32) * lambda_k1[l].astype(jnp.float32)))
               - jnp.exp(jnp.sum(lambda_q2[l].astype(jnp.float32) * lambda_k2[l].astype(jnp.float32)))
               + lam_init)
        o_b = differential_attention(qb, kb, vb, lam, subln_g[l], lam_init)

        g_a, g_b = jnp.split(jax.nn.sigmoid(gates), 2, axis=-1)
        merged = (g_a * jnp.einsum('bse,ed->bsd', o_a, w_proj_a[l])
                  + g_b * jnp.einsum('bse,ed->bsd', o_b, w_proj_b[l]))
        mix_out = jnp.einsum('bsd,de->bse', merged, w_out[l])
        x = layer_norm(ALPHA * x + mix_out, ln1_g[l], ln1_b[l])
        f = conv_ffn(x, w_up[l], conv_w[l], conv_b[l], w_down[l])
        x = layer_norm(ALPHA * x + f, ln2_g[l], ln2_b[l])
    return x
```

```python
import math
from contextlib import ExitStack

import numpy as np
import ml_dtypes

import concourse.bass as bass
import concourse.mybir as mybir
from concourse.bass_utils import run_bass_kernel_spmd

F32 = mybir.dt.float32
BF16 = mybir.dt.bfloat16
AF = mybir.ActivationFunctionType
ALU = mybir.AluOpType

D = 2048
SEQ = 4096
BATCH = 4
DEPTH = 2
HA, NQA, NKVA, GA = 64, 32, 4, 8
HB, NHB = 128, 8
DFF = 5632
INW = 12800
LN_EPS = 1e-5
SUB_EPS = 1e-5
ALPHA = (2 * DEPTH) ** 0.25
THETA = 10000.0
T = 2048
NH = SEQ // T
KC = D // 128
NEG = -30000.0

ENGS = ("pe", "act", "dve", "pool", "sp")
NCORES = 4


class Buf:
    __slots__ = ("name", "w", "r", "g")

    def __init__(self, name):
        self.name = name
        self.w = {}
        self.r = {}
        self.g = {}


def _merge(dst, src):
    for s, v in src.items():
        if dst.get(s, 0) < v:
            dst[s] = v


class Prog:
    DRING = 8

    def __init__(self, nc, stack):
        self.nc = nc
        self.st = stack
        self.q = {e: [] for e in ENGS}
        self.semh = {}
        self.cnt = {}
        self.seen = {e: {} for e in ENGS}
        for e in ("pe", "act", "dve", "pool"):
            self.newsem("P_" + e)
        self.dring = {}
        self.dpos = {}
        for e in ("sp", "act", "pool"):
            self.dring[e] = [self.newsem("D_%s_%d" % (e, i)) for i in range(self.DRING)]
            self.dpos[e] = 0
        self.nops = 0

    def newsem(self, name):
        self.semh[name] = self.st.enter_context(self.nc.semaphore(name))
        self.cnt[name] = 0
        return name

    def _need(self, eng, reads, writes, pwrites):
        need = {}
        for b in reads:
            _merge(need, b.w)
        for b in writes:
            _merge(need, b.g)
            _merge(need, b.w)
            _merge(need, b.r)
        for b in pwrites:
            if b.r:
                _merge(b.g, b.w)
                _merge(b.g, b.r)
                b.w = {}
                b.r = {}
            _merge(need, b.g)
        for s, v in need.items():
            if eng == "pe" and s == "P_pe":
                continue
            if self.seen[eng].get(s, 0) >= v:
                continue
            self.seen[eng][s] = v
            self.q[eng].append(("w", s, v))

    def _post(self, s, v, reads, writes, pwrites):
        for b in reads:
            if b.r.get(s, 0) < v:
                b.r[s] = v
        for b in writes:
            b.g = {}
            b.r = {}
            b.w = {s: v}
        for b in pwrites:
            if b.w.get(s, 0) < v:
                b.w[s] = v

    def op(self, eng, fn, reads=(), writes=(), pwrites=()):
        self._need(eng, reads, writes, pwrites)
        s = "P_" + eng
        self.cnt[s] += 1
        v = self.cnt[s]
        self.q[eng].append(("op", fn, s, 1))
        self._post(s, v, reads, writes, pwrites)
        self.nops += 1

    def pe(self, fns, reads=(), writes=(), pwrites=()):
        self._need("pe", reads, writes, pwrites)
        s = "P_pe"
        self.cnt[s] += 1
        v = self.cnt[s]
        for f in fns[:-1]:
            self.q["pe"].append(("op", f, None, 0))
        self.q["pe"].append(("op", fns[-1], s, 1))
        self._post(s, v, reads, writes, pwrites)
        self.nops += len(fns)

    def dma(self, eng, out, in_, reads=(), writes=(), pwrites=(), **kw):
        self._need(eng, reads, writes, pwrites)
        ring = self.dring[eng]
        s = ring[self.dpos[eng] % len(ring)]
        self.dpos[eng] += 1
        prev = self.cnt[s]
        if prev and self.seen[eng].get(s, 0) < prev:
            self.seen[eng][s] = prev
            self.q[eng].append(("w", s, prev))
        self.cnt[s] += 16
        v = self.cnt[s]
        self.q[eng].append(("op", lambda e, o=out, i=in_, k=kw: e.dma_start(out=o, in_=i, **k), s, 16))
        self._post(s, v, reads, writes, pwrites)
        self.nops += 1

    def finish(self, final_bufs):
        need = {}
        for b in final_bufs:
            _merge(need, b.w)
            _merge(need, b.g)
        for s, v in need.items():
            self.q["sp"].append(("w", s, v))

    def replay(self):
        nc = self.nc
        semh = self.semh

        def run(items, e):
            for it in items:
                if it[0] == "w":
                    e.wait_ge(semh[it[1]], it[2])
                else:
                    ins = it[1](e)
                    if it[2] is not None:
                        ins.then_inc(semh[it[2]], it[3])

        with nc.Block() as block:
            @block.sync
            def _(e):
                run(self.q["sp"], e)

            @block.tensor
            def _(e):
                run(self.q["pe"], e)

            @block.scalar
            def _(e):
                run(self.q["act"], e)

            @block.vector
            def _(e):
                run(self.q["dve"], e)

            @block.gpsimd
            def _(e):
                run(self.q["pool"], e)


def _win_perm():
    cols = []
    for i in range(8):
        for half in range(2):
            for hl in range(4):
                for d in range(32):
                    cols.append((4 * i + hl) * 64 + half * 32 + d)
    for half in range(2):
        for g in range(4):
            for d in range(32):
                cols.append(2048 + g * 64 + half * 32 + d)
    for base in (2560, 4608):
        for i in range(8):
            for half in range(2):
                for c in range(2):
                    for d in range(64):
                        cols.append(base + i * 256 + c * 128 + half * 64 + d)
    cols.extend(range(2304, 2560))
    cols.extend(range(6656, 8704))
    cols.extend(range(8704, 12800))
    assert len(cols) == INW and len(set(cols)) == INW
    return np.asarray(cols, dtype=np.int64)


def _wup_perm():
    cols = []
    for c in range(DFF // 128):
        cols.extend(range(c * 128, (c + 1) * 128))
        cols.extend(range(DFF + c * 128, DFF + (c + 1) * 128))
    return np.asarray(cols, dtype=np.int64)


def _rope_tables():
    out = {}
    t = np.arange(SEQ, dtype=np.float32)[None, :]
    for name, dim in (("A", HA), ("B", HB)):
        inv = (1.0 / (np.float32(THETA) ** (np.arange(0, dim, 2, dtype=np.float32) / np.float32(dim)))).astype(np.float32)
        p = np.arange(128) % (dim // 2)
        ang = (inv[p][:, None] * t).astype(np.float32)
        out["cos" + name] = np.cos(ang).astype(np.float32)
        out["sin" + name] = np.sin(ang).astype(np.float32)
    return out


BLK_QA0, BLK_KA, BLK_QB0, BLK_KB0, BLK_VA, BLK_VB0, BLK_G0 = 0, 8, 9, 17, 25, 26, 34
NBLK_IN = 50


class Kern:
    def __init__(self, debug=None, nlayers=DEPTH, nhalves=NH, stop=None):
        self.debug = debug or ()
        self.nlayers = nlayers
        self.nhalves = nhalves
        self.stop = stop
        self.nc = bass.Bass("TRN2", target_bir_lowering=False)
        self.bufs = {}

    def buf(self, name):
        b = self.bufs.get(name)
        if b is None:
            b = self.bufs[name] = Buf(name)
        return b

    def dram(self, name, shape, dt, kind="Internal"):
        if name in self.debug:
            kind = "ExternalOutput"
        return self.nc.dram_tensor(name, list(shape), dt, kind=kind).ap()

    def phase_begin(self):
        self.aoff = self.persist_end

    def alloc(self, nbytes, dt, shape=None, parts=128):
        nb0 = nbytes
        nbytes = (nbytes + 63) // 64 * 64
        off = self.aoff
        self.aoff += nbytes
        assert self.aoff <= self.ARENA, ("arena overflow", self.aoff)
        v = self.arena[0:parts, off // 4:(off + nbytes) // 4]
        if dt != F32:
            v = v.bitcast(dt)
            v = v[:, 0:nb0 // 2]
        else:
            v = v[:, 0:nb0 // 4]
        if shape is not None:
            names = " ".join("d%d" % i for i in range(len(shape)))
            kw = {"d%d" % i: int(s) for i, s in enumerate(shape)}
            v = v.rearrange("p (%s) -> p %s" % (names, names), **kw)
        return v

    def build(self):
        nc = self.nc
        with ExitStack() as st:
            self.st = st
            self.p = Prog(nc, st)
            self.ARENA = 168 * 1024
            self.arena = st.enter_context(nc.sbuf_tensor("arena", [128, self.ARENA // 4], F32))
            self.persist_end = 0
            self.aoff = 0
            self.tp_i = 0
            self.psf = [st.enter_context(nc.psum_tensor("psf%d" % i, [128, 512], F32)) for i in range(7)]
            self.psb = st.enter_context(nc.psum_tensor("psb", [128, 1024], BF16))
            self.declare()
            self.emit()
            self.p.replay()
        return nc

    def declare(self):
        nc = self.nc
        ext = lambda n, s, dt=F32: nc.dram_tensor(n, list(s), dt, kind="ExternalInput").ap()
        self.x = ext("x", [SEQ, D])
        self.w_in = ext("w_in", [DEPTH, D, INW])
        self.w_pa = ext("w_proj_a", [DEPTH, D, D])
        self.w_pb = ext("w_proj_b", [DEPTH, D, D])
        self.w_out = ext("w_out", [DEPTH, D, D])
        self.w_up = ext("w_up", [DEPTH, D, 2 * DFF])
        self.w_down = ext("w_down", [DEPTH, DFF, D])
        self.sinks = ext("sinks", [DEPTH, NQA])
        self.lam = ext("lam", [DEPTH, 4, HB])
        self.subg = ext("subln_g", [DEPTH, 2 * HB])
        self.ln1g = ext("ln1_g", [DEPTH, D])
        self.ln1b = ext("ln1_b", [DEPTH, D])
        self.ln2g = ext("ln2_g", [DEPTH, D])
        self.ln2b = ext("ln2_b", [DEPTH, D])
        self.convw = ext("conv_w", [DEPTH, 128, 88, 3])
        self.convb = ext("conv_b", [DEPTH, 128, 88])
        self.cosA = ext("cosA", [128, SEQ])
        self.sinA = ext("sinA", [128, SEQ])
        self.cosB = ext("cosB", [128, SEQ])
        self.sinB = ext("sinB", [128, SEQ])
        self.masks = ext("masks", [128, 4, 128], BF16)
        self.ident = ext("ident", [128, 128], BF16)
        self.y = nc.dram_tensor("y", [SEQ, D], F32, kind="ExternalOutput").ap()

        self.wb_in = [self.dram("wb_in%d" % l, [NBLK_IN, 128, KC, 256], BF16) for l in range(DEPTH)]
        self.wb_pa = [self.dram("wb_pa%d" % l, [16, 128, KC, 128], BF16) for l in range(DEPTH)]
        self.wb_pb = [self.dram("wb_pb%d" % l, [16, 128, KC, 128], BF16) for l in range(DEPTH)]
        self.wb_out = [self.dram("wb_out%d" % l, [128, KC, D], BF16) for l in range(DEPTH)]
        self.wb_up = [self.dram("wb_up%d" % l, [44, 128, KC, 256], BF16) for l in range(DEPTH)]
        self.wb_down = [self.dram("wb_down%d" % l, [128, DFF // 128, D], BF16) for l in range(DEPTH)]
        self.QTa = self.dram("QTa", [NQA, HA, T], BF16)
        self.QTb = self.dram("QTb", [2 * NHB, HB, T], BF16)
        self.KTa = [self.dram("KTa%d" % l, [NKVA, HA, SEQ], BF16) for l in range(DEPTH)]
        self.KTb = [self.dram("KTb%d" % l, [2 * NHB, HB, SEQ], BF16) for l in range(DEPTH)]
        self.Va = [self.dram("Va%d" % l, [SEQ, NKVA * HA], BF16) for l in range(DEPTH)]
        self.Vb = [self.dram("Vb%d" % l, [SEQ, NHB * 2 * HB], BF16) for l in range(DEPTH)]
        self.GT = self.dram("GT", [2 * D, T], BF16)
        self.OTa = self.dram("OTa", [D, T], BF16)
        self.OTb = self.dram("OTb", [D, T], BF16)
        self.MT = self.dram("MT", [D, T], BF16)
        self.Hres = self.dram("Hres", [T, D], F32)
        self.GF = self.dram("GF", [DFF, T], BF16)
        self.Y1 = self.dram("Y1", [T, D], F32)
        self.Xres = self.dram("Xres", [T, D], F32)
        self.XTd = self.dram("XTd", [D, T], BF16)
        self.HTd = self.dram("HTd", [D, T], BF16)

    def barrier(self):
        p = self.p
        for eng in ("pe", "act", "dve", "sp"):
            for s, v in p.cnt.items():
                if s == "P_pool" or s.startswith("D_pool"):
                    continue
                if eng == "pe" and s == "P_pe":
                    continue
                if v and p.seen[eng].get(s, 0) < v:
                    p.seen[eng][s] = v
                    p.q[eng].append(("w", s, v))
        self.phase_begin()

    def wbuf(self, name, l, i):
        return self.buf("W_%s_%d_%d" % (name, l, i))

    def conv_setup(self):
        self.cv_f = [self.st.enter_context(self.nc.sbuf_tensor("cvf%d" % i, [128, 2048], F32)) for i in range(2)]
        self.cv_b = [self.st.enter_context(self.nc.sbuf_tensor("cvb%d" % i, [128, 2048], BF16)) for i in range(2)]
        self.cv_i = 0

    def conv_piece(self, src, dst, dst_shape3, wb):
        p = self.p
        i = self.cv_i % 2
        self.cv_i += 1
        n = src.shape[-1]
        f = self.cv_f[i][:, 0:n]
        b = self.cv_b[i][:, 0:n]
        bf_, bb_ = self.buf("cvf%d" % i), self.buf("cvb%d" % i)
        p.dma("pool", f, src, writes=[bf_])
        p.op("pool", lambda e, o=b, a=f: e.tensor_copy(out=o, in_=a), reads=[bf_], writes=[bb_])
        bsrc = b
        if dst_shape3 is not None:
            bsrc = b.rearrange("p (a n) -> p a n", a=dst_shape3)
        p.dma("pool", dst, bsrc, reads=[bb_], pwrites=[wb])

    def convert_layer(self, l):
        wsrc = self.w_in[l]
        for cg in range(7):
            c0 = cg * 2048
            n = min(2048, INW - c0)
            nb = n // 256
            for kc in range(KC):
                self.conv_piece(wsrc[kc * 128:(kc + 1) * 128, c0:c0 + n],
                                self.wb_in[l][8 * cg:8 * cg + nb, :, kc, :].rearrange("b p n -> p b n"),
                                nb, self.wbuf("in", l, cg))
        for nm, wsrc, wdst in (("pa", self.w_pa[l], self.wb_pa[l]), ("pb", self.w_pb[l], self.wb_pb[l])):
            for kc in range(KC):
                self.conv_piece(wsrc[kc * 128:(kc + 1) * 128, :],
                                wdst[:, :, kc, :].rearrange("c p n -> p c n"), 16, self.wbuf(nm, l, 0))
        for kc in range(KC):
            self.conv_piece(self.w_out[l][kc * 128:(kc + 1) * 128, :], self.wb_out[l][:, kc, :], None,
                            self.wbuf("out", l, 0))
        for cg in range(6):
            c0 = cg * 2048
            n = min(2048, 2 * DFF - c0)
            nb = n // 256
            for kc in range(KC):
                self.conv_piece(self.w_up[l][kc * 128:(kc + 1) * 128, c0:c0 + n],
                                self.wb_up[l][8 * cg:8 * cg + nb, :, kc, :].rearrange("b p n -> p b n"),
                                nb, self.wbuf("up", l, cg))
        for kc in range(DFF // 128):
            self.conv_piece(self.w_down[l][kc * 128:(kc + 1) * 128, :], self.wb_down[l][:, kc, :], None,
                            self.wbuf("down", l, kc // 4))

    def transpose_tile(self, src_bf, src_buf, dst3, dst_buf, tok_off, nchunks=16, pw=True):
        p = self.p
        psb, psbB = self.psb, self.buf("psb")
        ident, identB = self.ident_sb, self.buf("ident")
        for g0 in range(0, nchunks, 8):
            ng = min(8, nchunks - g0)
            fns = []
            for j in range(ng):
                c = g0 + j
                fns.append(lambda e, j=j, c=c: e.transpose(out=psb[:, j * 128:(j + 1) * 128],
                                                           in_=src_bf[:, c * 128:(c + 1) * 128], identity=ident))
            p.pe(fns, reads=[src_buf, identB], writes=[psbB])
            src = psb[:, 0:ng * 128].rearrange("p (c t) -> p c t", c=ng)
            dst = dst3[:, g0:g0 + ng, tok_off:tok_off + 128]
            kw = dict(reads=[psbB], pwrites=[dst_buf]) if pw else dict(reads=[psbB], writes=[dst_buf])
            eng = "act" if (g0 // 8) % 2 == 0 else "dve"
            if eng == "act":
                p.op("act", lambda e, o=dst, a=src: e.copy(out=o, in_=a), **kw)
            else:
                p.op("dve", lambda e, o=dst, a=src: e.tensor_copy(out=o, in_=a), **kw)

    def phase_A(self, l, h):
        p = self.p
        tok0 = h * T
        self.barrier()
        AT = self.alloc(KC * T * 2, BF16, [KC, T])
        ATb = self.buf("A_AT")
        wblk = [self.alloc(KC * 256 * 2, BF16, [KC, 256]) for _ in range(3)]
        wblkB = [self.buf("A_w%d" % i) for i in range(3)]
        tabs = {}
        for nm in ("cosA", "sinA", "cosB", "sinB"):
            tabs[nm] = self.alloc(T * 4, F32)
        tabB = self.buf("A_tabs")
        tmp = [[self.alloc(512 * 4, F32) for _ in range(4)] for _ in range(2)]
        tmpB = [[self.buf("A_tmp%d_%d" % (s, i)) for i in range(4)] for s in range(2)]
        stg = [self.alloc(T * 2, BF16) for _ in range(4)]
        stgB = [self.buf("A_stg%d" % i) for i in range(4)]
        nstg = [0]

        def next_stg():
            i = nstg[0] % 4
            nstg[0] += 1
            return stg[i], stgB[i]

        if l == 0:
            xf = [tabs["cosA"], tabs["sinA"]]
            xfB = [self.buf("A_xf%d" % i) for i in range(2)]
            xbv = tabs["cosB"].bitcast(BF16)
            xb = [xbv[:, 0:D], xbv[:, D:2 * D]]
            xbB = [self.buf("A_xb%d" % i) for i in range(2)]
            for tb in range(T // 128):
                i = tb % 2
                p.dma("sp", xf[i], self.x[tok0 + tb * 128:tok0 + (tb + 1) * 128, :], writes=[xfB[i]])
                if tb % 2 == 0:
                    p.op("dve", lambda e, o=xb[i], a=xf[i]: e.tensor_copy(out=o, in_=a), reads=[xfB[i]], writes=[xbB[i]])
                else:
                    p.op("act", lambda e, o=xb[i], a=xf[i]: e.copy(out=o, in_=a), reads=[xfB[i]], writes=[xbB[i]])
                self.transpose_tile(xb[i], xbB[i], AT, ATb, tb * 128)
        else:
            src = self.XTd.rearrange("(c p) t -> p c t", p=128)
            for c0 in range(0, KC, 4):
                p.dma("sp", AT[:, c0:c0 + 4, :], src[:, c0:c0 + 4, :], reads=[self.buf("XTd")], pwrites=[ATb])

        ov = {"cosA": ["A_xf0"], "sinA": ["A_xf1"], "cosB": ["A_xb0", "A_xb1"], "sinB": []}
        for nm in ("cosA", "sinA", "cosB", "sinB"):
            p.dma("sp", tabs[nm], getattr(self, nm)[:, tok0:tok0 + T], writes=[self.buf(n) for n in ov[nm]], pwrites=[tabB])

        psf = self.psf
        psB = [self.buf("ps%d" % i) for i in range(7)]
        pscur = [0]

        def next_ps():
            i = pscur[0] % 4
            pscur[0] += 1
            return psf[i], psB[i]

        wsrc = self.wb_in[l]

        def load_w(b):
            if b < NBLK_IN:
                p.dma("sp", wblk[b % 3], wsrc[b], reads=[self.wbuf("in", l, b // 8)], writes=[wblkB[b % 3]])

        load_w(0)
        load_w(1)
        for blk in range(NBLK_IN):
            wi = blk % 3
            W = wblk[wi]
            load_w(blk + 2)
            if blk < BLK_VA:
                if blk < BLK_KA:
                    cosT, sinT = tabs["cosA"], tabs["sinA"]
                elif blk == BLK_KA:
                    cosT, sinT = tabs["cosA"], tabs["sinA"]
                else:
                    cosT, sinT = tabs["cosB"], tabs["sinB"]
                s1, s1B = next_stg()
                s2, s2B = next_stg()
                for tt in range(4):
                    tsl = slice(tt * 512, (tt + 1) * 512)
                    pp = []
                    for ch in range(2):
                        ps, psb_ = next_ps()
                        fns = [lambda e, kc=kc, ps=ps, ch=ch, W=W, tsl=tsl: e.matmul(
                            ps[:, :], lhsT=W[:, kc, ch * 128:(ch + 1) * 128], rhs=AT[:, kc, tsl],
                            start=(kc == 0), stop=(kc == KC - 1)) for kc in range(KC)]
                        p.pe(fns, reads=[wblkB[wi], ATb], writes=[psb_])
                        pp.append((ps, psb_))
                    (P1, P1B), (P2, P2B) = pp
                    tm, tmB = tmp[tt % 2], tmpB[tt % 2]
                    TT = lambda o, a, b, op: (lambda e: e.tensor_tensor(out=o, in0=a, in1=b, op=op))
                    p.op("dve", TT(tm[0], P1[:, :], cosT[:, tsl], ALU.mult), reads=[P1B, tabB], writes=[tmB[0]])
                    p.op("dve", TT(tm[3], P1[:, :], sinT[:, tsl], ALU.mult), reads=[P1B, tabB], writes=[tmB[3]])
                    p.op("dve", TT(tm[1], P2[:, :], sinT[:, tsl], ALU.mult), reads=[P2B, tabB], writes=[tmB[1]])
                    p.op("dve", TT(tm[2], P2[:, :], cosT[:, tsl], ALU.mult), reads=[P2B, tabB], writes=[tmB[2]])
                    p.op("dve", TT(s1[:, tsl], tm[0], tm[1], ALU.subtract), reads=[tmB[0], tmB[1]], pwrites=[s1B])
                    p.op("dve", TT(s2[:, tsl], tm[2], tm[3], ALU.add), reads=[tmB[2], tmB[3]], pwrites=[s2B])
                if blk < BLK_KA:
                    i = blk
                    for hl in range(4):
                        p.dma("sp", self.QTa[4 * i + hl, 0:32, :], s1[32 * hl:32 * hl + 32, :], reads=[s1B], pwrites=[self.buf("QTa")])
                        p.dma("sp", self.QTa[4 * i + hl, 32:64, :], s2[32 * hl:32 * hl + 32, :], reads=[s2B], pwrites=[self.buf("QTa")])
                elif blk == BLK_KA:
                    for g in range(4):
                        p.dma("sp", self.KTa[l][g, 0:32, tok0:tok0 + T], s1[32 * g:32 * g + 32, :], reads=[s1B], pwrites=[self.buf("KTa%d" % l)])
                        p.dma("sp", self.KTa[l][g, 32:64, tok0:tok0 + T], s2[32 * g:32 * g + 32, :], reads=[s2B], pwrites=[self.buf("KTa%d" % l)])
                else:
                    isq = blk < BLK_KB0
                    i = blk - (BLK_QB0 if isq else BLK_KB0)
                    for c in range(2):
                        if isq:
                            d1, d2, db = self.QTb[2 * i + c, 0:64, :], self.QTb[2 * i + c, 64:128, :], self.buf("QTb")
                        else:
                            d1 = self.KTb[l][2 * i + c, 0:64, tok0:tok0 + T]
                            d2 = self.KTb[l][2 * i + c, 64:128, tok0:tok0 + T]
                            db = self.buf("KTb%d" % l)
                        p.dma("sp", d1, s1[64 * c:64 * c + 64, :], reads=[s1B], pwrites=[db])
                        p.dma("sp", d2, s2[64 * c:64 * c + 64, :], reads=[s2B], pwrites=[db])
            elif blk < BLK_G0:
                if blk == BLK_VA:
                    vdst, vb_, c0 = self.Va[l], self.buf("Va%d" % l), 0
                else:
                    vdst, vb_, c0 = self.Vb[l], self.buf("Vb%d" % l), (blk - BLK_VB0) * 256
                for hh in range(2):
                    s, sB = next_stg()
                    s3 = s.rearrange("p (a n) -> p a n", a=8)
                    for j in range(8):
                        tb = hh * 8 + j
                        ps, psb_ = next_ps()
                        fns = [lambda e, kc=kc, ps=ps, tb=tb, W=W: e.matmul(
                            ps[:, 0:256], lhsT=AT[:, kc, tb * 128:(tb + 1) * 128], rhs=W[:, kc, :],
                            start=(kc == 0), stop=(kc == KC - 1)) for kc in range(KC)]
                        p.pe(fns, reads=[wblkB[wi], ATb], writes=[psb_])
                        p.op("act", lambda e, o=s3[:, j, :], a=ps[:, 0:256]: e.copy(out=o, in_=a), reads=[psb_], pwrites=[sB])
                    t0 = tok0 + hh * 1024
                    p.dma("sp", vdst[t0:t0 + 1024, c0:c0 + 256].rearrange("(a p) n -> p a n", p=128), s3,
                          reads=[sB], pwrites=[vb_])
            else:
                for ch in range(2):
                    s, sB = next_stg()
                    for tt in range(4):
                        tsl = slice(tt * 512, (tt + 1) * 512)
                        ps, psb_ = next_ps()
                        fns = [lambda e, kc=kc, ps=ps, ch=ch, W=W, tsl=tsl: e.matmul(
                            ps[:, :], lhsT=W[:, kc, ch * 128:(ch + 1) * 128], rhs=AT[:, kc, tsl],
                            start=(kc == 0), stop=(kc == KC - 1)) for kc in range(KC)]
                        p.pe(fns, reads=[wblkB[wi], ATb], writes=[psb_])
                        p.op("act", lambda e, o=s[:, tsl], a=ps[:, :]: e.activation(out=o, in_=a, func=AF.Sigmoid),
                             reads=[psb_], pwrites=[sB])
                    r0 = (blk - BLK_G0) * 256 + ch * 128
                    p.dma("sp", self.GT[r0:r0 + 128, :], s, reads=[sB], pwrites=[self.buf("GT")])

    def phase_B(self, l, h):
        p = self.p
        tok0 = h * T
        self.barrier()
        NKB = 17
        k0 = tok0 - 128
        QT = [self.alloc(8 * T * 2, BF16, [8, T], parts=64) for _ in range(2)]
        KT = [self.alloc(NKB * 128 * 2, BF16, parts=64) for _ in range(2)]
        QTB = [self.buf("B_QT%d" % i) for i in range(2)]
        KTB = [self.buf("B_KT%d" % i) for i in range(2)]
        V = self.alloc(NKB * 4 * 65 * 2, BF16, [NKB, 4, 65])
        VB = self.buf("B_V")
        PT = [[self.alloc(1024 * 2, BF16) for _ in range(2)] for _ in range(2)]
        PTB = [[self.buf("B_PT%d_%d" % (r, k)) for k in range(2)] for r in range(2)]
        ot = [self.alloc(512 * 2, BF16) for _ in range(2)]
        otB = [self.buf("B_ot%d" % i) for i in range(2)]
        OTs = [self.alloc(4 * T * 2, BF16, [4, T]) for _ in range(2)]
        OTsB = [self.buf("B_OTs%d" % i) for i in range(2)]
        esink = self.alloc(32 * 4, F32)
        esB = self.buf("B_esink")
        den = [self.alloc(8 * 4, F32) for _ in range(2)]
        denB = [self.buf("B_den%d" % i) for i in range(2)]
        psf = self.psf
        psB = [self.buf("ps%d" % i) for i in range(7)]
        scale = HA ** -0.5

        p.dma("sp", esink, self.sinks[l:l + 1, :].broadcast_to([128, NQA]), writes=[esB])
        p.op("act", lambda e: e.activation(out=esink, in_=esink, func=AF.Exp), reads=[esB], writes=[esB])
        p.op("dve", lambda e: e.memset(V[:, :, :, 64:65], 1.0), pwrites=[VB])
        kb_lo = 1 if h == 0 else 0
        for g in range(NKVA):
            src = self.Va[l][max(k0, 0):tok0 + T, g * 64:(g + 1) * 64].rearrange("(a p) n -> p a n", p=128)
            p.dma("sp", V[:, kb_lo:NKB, g, 0:64], src, reads=[self.buf("Va%d" % l)], pwrites=[VB])

        sring = [0]
        pend = []

        pend_t = []

        def emit_pv(item):
            g, j, kbs, r, gi = item
            Oa, Ob_ = psf[4], psf[5]
            for i in range(8):
                bank = Oa if i < 4 else Ob_
                out = bank[:, (i % 4) * 65:(i % 4) * 65 + 65]
                fns = []
                for n_, kb in enumerate(kbs):
                    kblk = j + kb
                    fns.append(lambda e, out=out, kb=kb, kblk=kblk, i=i, first=(n_ == 0), last=(n_ == len(kbs) - 1):
                               e.matmul(out, lhsT=PT[r][kb][:, i * 128:(i + 1) * 128], rhs=V[:, kblk, g, :],
                                        start=first, stop=last))
                p.pe(fns, reads=[PTB[r][kb] for kb in kbs] + [VB], pwrites=[psB[4 if i < 4 else 5]])
            flush_t()
            o = ot[r]
            for bi, bank in enumerate((Oa, Ob_)):
                O3 = bank[:, 0:260].rearrange("p (a n) -> p a n", a=4)
                dn = den[r][:, bi * 4:(bi + 1) * 4]
                hs = 8 * g + bi * 4
                p.op("dve", lambda e, dn=dn, O3=O3, hs=hs: e.tensor_tensor(
                    out=dn, in0=O3[:, :, 64], in1=esink[:, hs:hs + 4], op=ALU.add),
                    reads=[psB[4 + bi], esB], pwrites=[denB[r]])
                p.op("dve", lambda e, dn=dn: e.reciprocal(out=dn, in_=dn), reads=[denB[r]], pwrites=[denB[r]])
                o3 = o[:, bi * 256:(bi + 1) * 256].rearrange("p (a n) -> p a n", a=4)
                p.op("dve", lambda e, dn=dn, O3=O3, o3=o3: e.tensor_tensor(
                    out=o3, in0=O3[:, :, 0:64], in1=dn.unsqueeze(2).broadcast_to([128, 4, 64]), op=ALU.mult),
                    reads=[psB[4 + bi], denB[r]], pwrites=[otB[r]])
            pend_t.append((o, otB[r], OTs[gi], OTsB[gi], j * 128))

        def flush_t():
            while pend_t:
                a = pend_t.pop(0)
                self.transpose_tile(a[0], a[1], a[2], a[3], a[4], nchunks=4)

        def load_grp(g):
            if g >= NKVA:
                return
            gi = g % 2
            p.dma("sp", QT[gi], self.QTa[8 * g:8 * g + 8, :, :].rearrange("h d t -> d h t"),
                  reads=[self.buf("QTa")], writes=[QTB[gi]])
            if h == 0:
                p.dma("sp", KT[gi][:, 128:NKB * 128], self.KTa[l][g, :, 0:T], reads=[self.buf("KTa%d" % l)], writes=[KTB[gi]])
            else:
                p.dma("sp", KT[gi], self.KTa[l][g, :, k0:tok0 + T], reads=[self.buf("KTa%d" % l)], writes=[KTB[gi]])

        load_grp(0)
        for g in range(NKVA):
            gi = g % 2
            load_grp(g + 1)
            for j in range(T // 128):
                gq = h * 16 + j
                kbs = [1] if gq == 0 else [0, 1]
                r = j % 2
                for kb in kbs:
                    for hh in range(2):
                        bi = sring[0] % 4
                        sring[0] += 1
                        bank = psf[bi]
                        kcol = (j + kb) * 128
                        fns = [
                            lambda e, bank=bank, kcol=kcol, hh=hh, j=j, gi=gi: e.matmul(
                                bank[:, :], lhsT=KT[gi][:, kcol:kcol + 128],
                                rhs=QT[gi][:, 4 * hh:4 * hh + 4, j * 128:(j + 1) * 128], start=True, stop=False),
                            lambda e, bank=bank, kb=kb: e.matmul(
                                bank[:, :], lhsT=self.ident_sb, rhs=self.maskrep[:, kb, :], start=False, stop=True),
                        ]
                        p.pe(fns, reads=[KTB[gi], QTB[gi], self.buf("ident"), self.buf("maskrep")], writes=[psB[bi]])
                        p.op("act", lambda e, bank=bank, r=r, kb=kb, hh=hh: e.activation(
                            out=PT[r][kb][:, hh * 512:(hh + 1) * 512], in_=bank[:, :], func=AF.Exp, scale=scale),
                            reads=[psB[bi]], pwrites=[PTB[r][kb]])
                pend.append((g, j, kbs, r, gi))
                if len(pend) > 1:
                    emit_pv(pend.pop(0))
            while pend:
                emit_pv(pend.pop(0))
            flush_t()
            p.dma("sp", self.OTa[g * 512:(g + 1) * 512, :].rearrange("(c p) t -> p c t", p=128), OTs[gi],
                  reads=[OTsB[gi]], pwrites=[self.buf("OTa")])

    def phase_C(self, l, h):
        p = self.p
        tok0 = h * T
        self.barrier()
        nk = tok0 + T
        nkb = nk // 128
        KT = [self.alloc(2 * SEQ * 2, BF16, [2, SEQ]) for _ in range(2)]
        QT = [self.alloc(2 * T * 2, BF16, [2, T]) for _ in range(2)]
        V = [self.alloc(32 * 257 * 2, BF16, [32, 257]) for _ in range(2)]
        KTB = [self.buf("C_KT%d" % i) for i in range(2)]
        QTB = [self.buf("C_QT%d" % i) for i in range(2)]
        VB = [self.buf("C_V%d" % i) for i in range(2)]
        PT = [self.alloc(512 * 2, BF16) for _ in range(3)]
        PTB = [self.buf("C_PT%d" % i) for i in range(3)]
        OTs = [self.alloc(2 * T * 2, BF16, [2, T]) for _ in range(2)]
        OTsB = [self.buf("C_OTs%d" % i) for i in range(2)]
        tf = [self.alloc(256 * 4, F32) for _ in range(2)]
        tfB = [self.buf("C_tf%d" % i) for i in range(2)]
        of = [self.alloc(256 * 4, F32) for _ in range(2)]
        ofB = [self.buf("C_of%d" % i) for i in range(2)]
        junk = self.alloc(256 * 4, F32)
        junkB = self.buf("C_junk")
        ob = [self.alloc(256 * 2, BF16) for _ in range(2)]
        obB = [self.buf("C_ob%d" % i) for i in range(2)]
        sm = [self.alloc(8 * 4, F32) for _ in range(2)]
        OC = [[[self.alloc(257 * 4, F32) for _ in range(2)] for _ in range(2)] for _ in range(2)]
        OCB = [[[self.buf("C_OC%d_%d_%d" % (s, c, q)) for q in range(2)] for c in range(2)] for s in range(2)]
        OFs = [self.alloc(16 * 256 * 4, F32, [16, 256]) for _ in range(2)]
        OFBs = [self.buf("C_OF%d" % i) for i in range(2)]
        sss = [self.alloc(64 * 4, F32) for _ in range(2)]
        ssBs = [self.buf("C_ss%d" % i) for i in range(2)]
        smB = [self.buf("C_sm%d" % i) for i in range(2)]
        lamt = self.alloc(4 * HB * 4, F32, [4, HB])
        lamB = self.buf("C_lamt")
        lsc = self.alloc(8 * 4, F32)
        lscB = self.buf("C_lsc")
        subg = self.alloc(256 * 4, F32)
        subgB = self.buf("C_subg")
        psf = self.psf
        psB = [self.buf("ps%d" % i) for i in range(7)]
        scale = HB ** -0.5
        lam_init = 0.8 - 0.6 * math.exp(-0.3 * l)

        p.dma("sp", lamt, self.lam[l:l + 1, :, :].broadcast_to([128, 4, HB]), writes=[lamB])
        p.dma("sp", subg, self.subg[l:l + 1, :].broadcast_to([128, 2 * HB]), writes=[subgB])
        for i in range(2):
            p.op("dve", lambda e, i=i: e.scalar_tensor_tensor(
                out=junk[:, 0:HB], in0=lamt[:, 2 * i, :], scalar=1.0, in1=lamt[:, 2 * i + 1, :],
                op0=ALU.mult, op1=ALU.mult, accum_out=lsc[:, i:i + 1]),
                reads=[lamB], writes=[junkB], pwrites=[lscB])
        p.op("act", lambda e: e.activation(out=lsc[:, 0:2], in_=lsc[:, 0:2], func=AF.Exp), reads=[lscB], writes=[lscB])
        p.op("dve", lambda e: e.scalar_tensor_tensor(
            out=lsc[:, 2:3], in0=lsc[:, 1:2], scalar=-lam_init, in1=lsc[:, 0:1], op0=ALU.add, op1=ALU.subtract),
            reads=[lscB], writes=[lscB])
        for i in range(2):
            p.op("dve", lambda e, i=i: e.memset(V[i][:, :, 256:257], 1.0), pwrites=[VB[i]])
        p.op("dve", lambda e: e.tensor_scalar(out=subg, in0=subg, scalar1=(1.0 - lam_init), scalar2=None, op0=ALU.mult),
             reads=[subgB], writes=[subgB])

        sring = [0]
        Oacc = [[psf[2], psf[3]], [psf[4], psf[5]]]
        OaccB = [[psB[2], psB[3]], [psB[4], psB[5]]]
        SB = [0, 1, 6]
        fin = [0]

        def load_head(hd):
            if hd >= NHB:
                return
            hi = hd % 2
            for c in range(2):
                p.dma("sp", KT[hi][:, c, 0:nk], self.KTb[l][2 * hd + c, :, 0:nk], reads=[self.buf("KTb%d" % l)], pwrites=[KTB[hi]])
                p.dma("sp", QT[hi][:, c, :], self.QTb[2 * hd + c, :, :], reads=[self.buf("QTb")], pwrites=[QTB[hi]])
            p.dma("sp", V[hi][:, 0:nkb, 0:256],
                  self.Vb[l][0:nk, hd * 256:(hd + 1) * 256].rearrange("(a p) n -> p a n", p=128),
                  reads=[self.buf("Vb%d" % l)], pwrites=[VB[hi]])

        ep_steps = []

        def run_steps(n):
            for _ in range(n):
                if ep_steps:
                    ep_steps.pop(0)()

        def epilogue(hd):
            run_steps(len(ep_steps))
            hi = hd % 2
            OF, OFB, ss, ssB = OFs[hi], OFBs[hi], sss[hi], ssBs[hi]

            def s0():
                p.op("dve", lambda e: e.tensor_scalar(out=ss[:, 16:32], in0=ss[:, 0:16], scalar1=1.0 / 256.0, scalar2=SUB_EPS,
                                                      op0=ALU.mult, op1=ALU.add), reads=[ssB], pwrites=[ssB])
                self.rsqrt(ss[:, 16:32], ss[:, 32:48], ss[:, 48:64], ssB)
            ep_steps.append(s0)
            for jb in range(T // 128):
                def sj(jb=jb):
                    f = jb % 2
                    p.op("dve", lambda e, f=f, jb=jb: e.scalar_tensor_tensor(
                        out=ob[f], in0=OF[:, jb, :], scalar=ss[:, 32 + jb:33 + jb], in1=subg, op0=ALU.mult, op1=ALU.mult),
                        reads=[OFB, ssB, subgB], writes=[obB[f]])
                    self.transpose_tile(ob[f], obB[f], OTs[hi], OTsB[hi], jb * 128, nchunks=2)
                ep_steps.append(sj)

            def sl():
                p.dma("sp", self.OTb[hd * 256:(hd + 1) * 256, :].rearrange("(c p) t -> p c t", p=128), OTs[hi],
                      reads=[OTsB[hi]], pwrites=[self.buf("OTb")])
            ep_steps.append(sl)

        load_head(0)
        for hd in range(NHB):
            hi = hd % 2
            OF, OFB, ss, ssB = OFs[hi], OFBs[hi], sss[hi], ssBs[hi]
            load_head(hd + 1)
            for jp in range(T // 256):
                gq0 = h * 16 + 2 * jp
                nkbs = gq0 + 2
                pend = []

                def emit_pv(item):
                    kb, pi = item
                    for c in range(2):
                        for qq in range(2):
                            if kb == gq0 + 1 and qq == 0:
                                continue
                            last = (kb == gq0 + qq)
                            p.pe([lambda e, c=c, qq=qq, kb=kb, pi=pi, last=last, hi=hi: e.matmul(
                                Oacc[c][qq][:, 0:257], lhsT=PT[pi][:, c * 256 + qq * 128:c * 256 + qq * 128 + 128],
                                rhs=V[hi][:, kb, :], start=(kb == 0), stop=last)],
                                reads=[PTB[pi], VB[hi]], pwrites=[OaccB[c][qq]])

                for kb in range(nkbs):
                    bi = SB[sring[0] % 3]
                    pi = sring[0] % 3
                    sring[0] += 1
                    bank = psf[bi]
                    masked = kb >= gq0
                    fns = []
                    for c in range(2):
                        fns.append(lambda e, bank=bank, c=c, kb=kb, masked=masked, hi=hi, jp=jp: e.matmul(
                            bank[:, c * 256:(c + 1) * 256], lhsT=KT[hi][:, c, kb * 128:(kb + 1) * 128],
                            rhs=QT[hi][:, c, jp * 256:(jp + 1) * 256], start=True, stop=(not masked)))
                        if masked:
                            mt = kb - gq0
                            fns.append(lambda e, bank=bank, c=c, mt=mt: e.matmul(
                                bank[:, c * 256:(c + 1) * 256], lhsT=self.ident_sb, rhs=self.maskpair[:, mt, :],
                                start=False, stop=True))
                    p.pe(fns, reads=[KTB[hi], QTB[hi], self.buf("ident"), self.buf("maskpair")], writes=[psB[bi]])
                    p.op("act", lambda e, bank=bank, pi=pi: e.activation(out=PT[pi], in_=bank[:, :], func=AF.Exp, scale=scale),
                         reads=[psB[bi]], writes=[PTB[pi]])
                    pend.append((kb, pi))
                    if len(pend) > 2:
                        emit_pv(pend.pop(0))
                    if kb == nkbs // 2:
                        run_steps(3)

                while pend:
                    emit_pv(pend.pop(0))
                oset = (hd * (T // 256) + jp) % 2
                for c in range(2):
                    for qq in range(2):
                        src_ = Oacc[c][qq][:, 0:257]
                        dst_ = OC[oset][c][qq]
                        if qq == 0:
                            p.op("act", lambda e, o=dst_, a=src_: e.copy(out=o, in_=a), reads=[OaccB[c][qq]], writes=[OCB[oset][c][qq]])
                        else:
                            p.op("dve", lambda e, o=dst_, a=src_: e.tensor_copy(out=o, in_=a), reads=[OaccB[c][qq]], writes=[OCB[oset][c][qq]])
                for qq in range(2):
                    f = fin[0] % 2
                    fin[0] += 1
                    O1, O2 = OC[oset][0][qq], OC[oset][1][qq]
                    O1B, O2B = OCB[oset][0][qq], OCB[oset][1][qq]
                    s_ = sm[f]
                    p.op("dve", lambda e, s_=s_, O1=O1: e.reciprocal(out=s_[:, 0:1], in_=O1[:, 256:257]), reads=[O1B], pwrites=[smB[f]])
                    p.op("dve", lambda e, s_=s_, O2=O2: e.reciprocal(out=s_[:, 1:2], in_=O2[:, 256:257]), reads=[O2B], pwrites=[smB[f]])
                    p.op("dve", lambda e, s_=s_: e.tensor_tensor(out=s_[:, 1:2], in0=s_[:, 1:2], in1=lsc[:, 2:3], op=ALU.mult),
                         reads=[smB[f], lscB], pwrites=[smB[f]])
                    p.op("dve", lambda e, s_=s_, O1=O1, f=f: e.tensor_scalar(
                        out=tf[f], in0=O1[:, 0:256], scalar1=s_[:, 0:1], scalar2=None, op0=ALU.mult),
                        reads=[O1B, smB[f]], writes=[tfB[f]])
                    jb = 2 * jp + qq
                    p.op("dve", lambda e, s_=s_, O2=O2, f=f, jb=jb, OF=OF: e.scalar_tensor_tensor(
                        out=OF[:, jb, :], in0=O2[:, 0:256], scalar=s_[:, 1:2], in1=tf[f], op0=ALU.mult, op1=ALU.add),
                        reads=[O2B, smB[f], tfB[f]], pwrites=[OFB])
                    p.op("act", lambda e, jb=jb, OF=OF, ss=ss: e.activation(out=junk, in_=OF[:, jb, :], func=AF.Square, accum_out=ss[:, jb:jb + 1]),
                         reads=[OFB], writes=[junkB], pwrites=[ssB])
                if jp == 0 and hd > 0:
                    epilogue(hd - 1)
        epilogue(NHB - 1)
        run_steps(len(ep_steps))

    def phase_D(self, l, h):
        p = self.p
        self.barrier()
        TS = 1024
        OA = self.alloc(KC * TS * 2, BF16, [KC, TS])
        OB = self.alloc(KC * TS * 2, BF16, [KC, TS])
        OAB, OBB = self.buf("D_OA"), self.buf("D_OB")
        wa = [self.alloc(KC * 128 * 2, BF16, [KC, 128]) for _ in range(3)]
        wb = [self.alloc(KC * 128 * 2, BF16, [KC, 128]) for _ in range(3)]
        waB = [self.buf("D_wa%d" % i) for i in range(3)]
        wbB = [self.buf("D_wb%d" % i) for i in range(3)]
        ga = [self.alloc(TS * 2, BF16) for _ in range(3)]
        gb = [self.alloc(TS * 2, BF16) for _ in range(3)]
        gaB = [self.buf("D_ga%d" % i) for i in range(3)]
        gbB = [self.buf("D_gb%d" % i) for i in range(3)]
        t1 = [self.alloc(512 * 4, F32) for _ in range(2)]
        t2 = [self.alloc(512 * 4, F32) for _ in range(2)]
        t1B = [self.buf("D_t1%d" % i) for i in range(2)]
        t2B = [self.buf("D_t2%d" % i) for i in range(2)]
        ms = [self.alloc(TS * 2, BF16) for _ in range(3)]
        msB = [self.buf("D_ms%d" % i) for i in range(3)]
        psf = self.psf
        psB = [self.buf("ps%d" % i) for i in range(7)]
        pr = [0]
        it = [0]
        NIT = (T // TS) * KC

        def load_it(k):
            if k >= NIT:
                return
            s_, c_ = k // KC, k % KC
            i_ = k % 3
            t_ = s_ * TS
            p.dma("sp", wa[i_], self.wb_pa[l][c_], reads=[self.wbuf("pa", l, 0)], writes=[waB[i_]])
            p.dma("sp", wb[i_], self.wb_pb[l][c_], reads=[self.wbuf("pb", l, 0)], writes=[wbB[i_]])
            p.dma("sp", ga[i_], self.GT[c_ * 128:(c_ + 1) * 128, t_:t_ + TS], reads=[self.buf("GT")], writes=[gaB[i_]])
            p.dma("sp", gb[i_], self.GT[D + c_ * 128:D + (c_ + 1) * 128, t_:t_ + TS], reads=[self.buf("GT")], writes=[gbB[i_]])

        load_it(0)
        load_it(1)
        for sub in range(T // TS):
            ts0 = sub * TS
            for c0 in range(0, KC, 4):
                p.dma("sp", OA[:, c0:c0 + 4, :], self.OTa[c0 * 128:(c0 + 4) * 128, ts0:ts0 + TS].rearrange("(c p) t -> p c t", p=128),
                      reads=[self.buf("OTa")], pwrites=[OAB])
                p.dma("sp", OB[:, c0:c0 + 4, :], self.OTb[c0 * 128:(c0 + 4) * 128, ts0:ts0 + TS].rearrange("(c p) t -> p c t", p=128),
                      reads=[self.buf("OTb")], pwrites=[OBB])
            for c in range(KC):
                i = it[0] % 3
                load_it(it[0] + 2)
                it[0] += 1
                for tt in range(TS // 512):
                    tsl = slice(tt * 512, (tt + 1) * 512)
                    banks = []
                    for (W, WB, O, OBf) in ((wa[i], waB[i], OA, OAB), (wb[i], wbB[i], OB, OBB)):
                        bi = pr[0] % 7
                        pr[0] += 1
                        ps = psf[bi]
                        fns = [lambda e, kc=kc, ps=ps, W=W, O=O, tsl=tsl: e.matmul(
                            ps[:, :], lhsT=W[:, kc, :], rhs=O[:, kc, tsl], start=(kc == 0), stop=(kc == KC - 1))
                            for kc in range(KC)]
                        p.pe(fns, reads=[WB, OBf], writes=[psB[bi]])
                        banks.append((ps, psB[bi]))
                    (pa, paB), (pb_, pbB) = banks
                    k = tt % 2
                    p.op("dve", lambda e, k=k, pa=pa, i=i, tsl=tsl: e.tensor_tensor(out=t1[k], in0=pa[:, :], in1=ga[i][:, tsl], op=ALU.mult),
                         reads=[paB, gaB[i]], writes=[t1B[k]])
                    p.op("dve", lambda e, k=k, pb_=pb_, i=i, tsl=tsl: e.tensor_tensor(out=t2[k], in0=pb_[:, :], in1=gb[i][:, tsl], op=ALU.mult),
                         reads=[pbB, gbB[i]], writes=[t2B[k]])
                    p.op("dve", lambda e, k=k, i=i, tsl=tsl: e.tensor_tensor(out=ms[i][:, tsl], in0=t1[k], in1=t2[k], op=ALU.add),
                         reads=[t1B[k], t2B[k]], pwrites=[msB[i]])
                p.dma("sp", self.MT[c * 128:(c + 1) * 128, ts0:ts0 + TS], ms[i], reads=[msB[i]], pwrites=[self.buf("MT")])

    def rsqrt(self, v, r, t, B, iters=2):
        p = self.p
        p.op("act", lambda e: e.activation(out=r, in_=v, func=AF.Sqrt), reads=[B], pwrites=[B])
        p.op("dve", lambda e: e.reciprocal(out=r, in_=r), reads=[B], pwrites=[B])
        for _ in range(iters):
            p.op("dve", lambda e: e.tensor_tensor(out=t, in0=r, in1=r, op=ALU.mult), reads=[B], pwrites=[B])
            p.op("dve", lambda e: e.tensor_tensor(out=t, in0=t, in1=v, op=ALU.mult), reads=[B], pwrites=[B])
            p.op("dve", lambda e: e.tensor_scalar(out=t, in0=t, scalar1=-0.5, scalar2=1.5, op0=ALU.mult, op1=ALU.add),
                 reads=[B], pwrites=[B])
            p.op("dve", lambda e: e.tensor_tensor(out=r, in0=r, in1=t, op=ALU.mult), reads=[B], pwrites=[B])

    def rsqrt1(self, v, r, t, B):
        p = self.p
        p.op("act", lambda e: e.activation(out=r, in_=v, func=AF.Sqrt), reads=[B], pwrites=[B])
        p.op("dve", lambda e: e.reciprocal(out=r, in_=r), reads=[B], pwrites=[B])
        for _ in range(2):
            p.op("dve", lambda e: e.scalar_tensor_tensor(out=t, in0=r, scalar=v, in1=r, op0=ALU.mult, op1=ALU.mult),
                 reads=[B], pwrites=[B])
            p.op("dve", lambda e: e.tensor_scalar(out=t, in0=t, scalar1=-0.5, scalar2=1.5, op0=ALU.mult, op1=ALU.add),
                 reads=[B], pwrites=[B])
            p.op("dve", lambda e: e.tensor_tensor(out=r, in0=r, in1=t, op=ALU.mult), reads=[B], pwrites=[B])

    def ln_tile(self, y, yB, gt, bt, gbB, st, stB, hbf, hbfB):
        p = self.p
        stats = st[:, 0:24].rearrange("p (a n) -> p a n", a=4)
        mv = st[:, 24:26]
        rstd = st[:, 26:27]
        nb = st[:, 27:28]
        for n in range(4):
            p.op("dve", lambda e, n=n: e.bn_stats(out=stats[:, n, :], in_=y[:, n * 512:(n + 1) * 512]), reads=[yB], pwrites=[stB])
        p.op("dve", lambda e: e.bn_aggr(out=mv, in_=st[:, 0:24]), reads=[stB], pwrites=[stB])
        p.op("dve", lambda e: e.tensor_scalar(out=st[:, 28:29], in0=st[:, 25:26], scalar1=LN_EPS, scalar2=None, op0=ALU.add),
             reads=[stB], pwrites=[stB])
        self.rsqrt1(st[:, 28:29], rstd, st[:, 29:30], stB)
        p.op("dve", lambda e: e.scalar_tensor_tensor(out=nb, in0=st[:, 24:25], scalar=-1.0, in1=rstd, op0=ALU.mult, op1=ALU.mult),
             reads=[stB], pwrites=[stB])
        p.op("act", lambda e: e.activation(out=y, in_=y, func=AF.Identity, bias=nb, scale=rstd), reads=[yB, stB], writes=[yB])
        p.op("dve", lambda e: e.tensor_tensor(out=y, in0=y, in1=gt, op=ALU.mult), reads=[yB, gbB], writes=[yB])
        p.op("dve", lambda e: e.tensor_tensor(out=y, in0=y, in1=bt, op=ALU.add), reads=[yB, gbB], writes=[yB])
        if hbf is not None:
            p.op("act", lambda e: e.copy(out=hbf, in_=y), reads=[yB], writes=[hbfB])

    def phase_E(self, l, h):
        p = self.p
        tok0 = h * T
        self.barrier()
        TS = 512
        Wo = self.alloc(KC * D * 2, BF16, [KC, D])
        WoB = self.buf("E_Wo")
        Ms = [self.alloc(KC * TS * 2, BF16, [KC, TS]) for _ in range(2)]
        MBs = [self.buf("E_M%d" % i) for i in range(2)]
        gt = self.alloc(D * 4, F32)
        bt = self.alloc(D * 4, F32)
        gbB = self.buf("E_gb")
        yt = [self.alloc(D * 4, F32) for _ in range(3)]
        ytB = [self.buf("E_y%d" % i) for i in range(3)]
        hb = [self.alloc(D * 2, BF16) for _ in range(2)]
        hbB = [self.buf("E_hb%d" % i) for i in range(2)]
        st = [self.alloc(32 * 4, F32) for _ in range(2)]
        stB = [self.buf("E_st%d" % i) for i in range(2)]
        HTs = self.alloc(KC * 256 * 2, BF16, [KC, 256])
        HTsB = self.buf("E_HTs")
        psf = self.psf
        psB = [self.buf("ps%d" % i) for i in range(7)]
        pr = [0]
        xsrc = self.x if l == 0 else self.Xres
        xoff = tok0 if l == 0 else 0
        xsrcB = [] if l == 0 else [self.buf("Xres")]
        for c0 in range(0, KC, 4):
            p.dma("sp", Wo[:, c0:c0 + 4, :], self.wb_out[l][:, c0:c0 + 4, :], reads=[self.wbuf("out", l, 0)], pwrites=[WoB])
        p.dma("sp", gt, self.ln1g[l:l + 1, :].broadcast_to([128, D]), pwrites=[gbB])
        p.dma("sp", bt, self.ln1b[l:l + 1, :].broadcast_to([128, D]), pwrites=[gbB])
        def load_y(k):
            if k < T // 128:
                p.dma("sp", yt[k % 3], xsrc[xoff + k * 128:xoff + (k + 1) * 128, :], reads=xsrcB, writes=[ytB[k % 3]])

        def load_m(s_):
            if s_ < T // TS:
                for c0 in range(0, KC, 8):
                    p.dma("sp", Ms[s_ % 2][:, c0:c0 + 8, :],
                          self.MT[c0 * 128:(c0 + 8) * 128, s_ * TS:(s_ + 1) * TS].rearrange("(c p) t -> p c t", p=128),
                          reads=[self.buf("MT")], pwrites=[MBs[s_ % 2]])

        load_m(0)
        NB_ = T // 128
        banks = {}

        def mm(gidx):
            sub, tb = divmod(gidx, TS // 128)
            if tb == 0:
                load_m(sub + 1)
            if gidx == 0:
                load_y(0)
            load_y(gidx + 1)
            M, MB = Ms[sub % 2], MBs[sub % 2]
            bl = []
            for n in range(4):
                bi = pr[0] % 7
                pr[0] += 1
                ps = psf[bi]
                fns = [lambda e, kc=kc, ps=ps, tb=tb, n=n, M=M: e.matmul(
                    ps[:, :], lhsT=M[:, kc, tb * 128:(tb + 1) * 128], rhs=Wo[:, kc, n * 512:(n + 1) * 512],
                    start=(kc == 0), stop=(kc == KC - 1)) for kc in range(KC)]
                p.pe(fns, reads=[MB, WoB], writes=[psB[bi]])
                bl.append(bi)
            banks[gidx] = bl

        def adds(gidx):
            y, yB = yt[gidx % 3], ytB[gidx % 3]
            for n, bi in enumerate(banks.pop(gidx)):
                ps = psf[bi]
                p.op("dve", lambda e, y=y, ps=ps, n=n: e.scalar_tensor_tensor(
                    out=y[:, n * 512:(n + 1) * 512], in0=y[:, n * 512:(n + 1) * 512], scalar=ALPHA, in1=ps[:, :],
                    op0=ALU.mult, op1=ALU.add), reads=[psB[bi]], writes=[yB])

        def tail(gidx):
            y, yB = yt[gidx % 3], ytB[gidx % 3]
            i = gidx % 2
            tl = gidx * 128
            self.ln_tile(y, yB, gt, bt, gbB, st[i], stB[i], hb[i], hbB[i])
            p.dma("sp", self.Hres[tl:tl + 128, :], y, reads=[yB], pwrites=[self.buf("Hres")])
            self.transpose_tile(hb[i], hbB[i], HTs, HTsB, (gidx % 2) * 128)
            if gidx % 2 == 1:
                t0 = tl - 128
                p.dma("sp", self.HTd.rearrange("(c p) t -> p c t", p=128)[:, :, t0:t0 + 256], HTs,
                      reads=[HTsB], pwrites=[self.buf("HTd")])

        for gidx in range(NB_):
            mm(gidx)
            if gidx > 0:
                tail(gidx - 1)
            adds(gidx)
        tail(NB_ - 1)

    def phase_F(self, l, h):
        p = self.p
        self.barrier()
        HT = self.alloc(KC * T * 2, BF16, [KC, T])
        HTB = self.buf("F_HT")
        wblk = [self.alloc(KC * 256 * 2, BF16, [KC, 256]) for _ in range(3)]
        wblkB = [self.buf("F_w%d" % i) for i in range(3)]
        U = [[self.alloc((T + 2) * 4, F32) for _ in range(2)] for _ in range(2)]
        UB = [[self.buf("F_U%d_%d" % (r, c)) for c in range(2)] for r in range(2)]
        cv = [[self.alloc(512 * 4, F32) for _ in range(2)] for _ in range(2)]
        cvB = [[self.buf("F_cv%d_%d" % (r, c)) for c in range(2)] for r in range(2)]
        sl = [self.alloc(512 * 4, F32) for _ in range(2)]
        slB = [self.buf("F_sl%d" % i) for i in range(2)]
        gs = [self.alloc(T * 2, BF16) for _ in range(2)]
        gsB = [self.buf("F_gs%d" % i) for i in range(2)]
        cw = self.alloc(88 * 3 * 4, F32, [88, 3])
        cb = self.alloc(88 * 4, F32)
        cwB = self.buf("F_cw")
        psf = self.psf
        psB = [self.buf("ps%d" % i) for i in range(7)]
        pr = [0]
        uh = self.uhalo
        uhB = self.buf("uhalo")
        for c0 in range(0, KC, 4):
            p.dma("sp", HT[:, c0:c0 + 4, :], self.HTd[c0 * 128:(c0 + 4) * 128, :].rearrange("(c p) t -> p c t", p=128),
                  reads=[self.buf("HTd")], pwrites=[HTB])
        p.dma("sp", cw, self.convw[l], pwrites=[cwB])
        p.dma("sp", cb, self.convb[l], pwrites=[cwB])
        def load_w(b):
            if b < DFF // 128:
                p.dma("sp", wblk[b % 3], self.wb_up[l][b], reads=[self.wbuf("up", l, b // 8)], writes=[wblkB[b % 3]])

        load_w(0)
        load_w(1)
        for blk in range(DFF // 128):
            wi = blk % 3
            W = wblk[wi]
            r = blk % 2
            load_w(blk + 2)
            for ch in range(2):
                p.op("act", lambda e, r=r, ch=ch, blk=blk: e.copy(out=U[r][ch][:, 0:2], in_=uh[:, l, 2 * blk + ch, :]),
                     reads=[uhB], pwrites=[UB[r][ch]])
            g_, gB_ = gs[r], gsB[r]
            for tt in range(4):
                tsl = slice(tt * 512, (tt + 1) * 512)
                k = tt % 2
                for ch in range(2):
                    bi = pr[0] % 7
                    pr[0] += 1
                    ps = psf[bi]
                    fns = [lambda e, kc=kc, ps=ps, ch=ch, W=W, tsl=tsl: e.matmul(
                        ps[:, :], lhsT=W[:, kc, ch * 128:(ch + 1) * 128], rhs=HT[:, kc, tsl],
                        start=(kc == 0), stop=(kc == KC - 1)) for kc in range(KC)]
                    p.pe(fns, reads=[wblkB[wi], HTB], writes=[psB[bi]])
                    u = U[r][ch]
                    p.op("act", lambda e, u=u, ps=ps, tt=tt: e.copy(out=u[:, 2 + tt * 512:2 + (tt + 1) * 512], in_=ps[:, :]),
                         reads=[psB[bi]], pwrites=[UB[r][ch]])
                    a = cv[k][ch]
                    ci = 2 * blk + ch
                    t0 = tt * 512
                    p.op("dve", lambda e, a=a, u=u, ci=ci, t0=t0: e.tensor_scalar(
                        out=a, in0=u[:, t0 + 2:t0 + 514], scalar1=cw[:, ci, 2:3], scalar2=cb[:, ci:ci + 1], op0=ALU.mult, op1=ALU.add),
                        reads=[UB[r][ch], cwB], writes=[cvB[k][ch]])
                    p.op("dve", lambda e, a=a, u=u, ci=ci, t0=t0: e.scalar_tensor_tensor(
                        out=a, in0=u[:, t0 + 1:t0 + 513], scalar=cw[:, ci, 1:2], in1=a, op0=ALU.mult, op1=ALU.add),
                        reads=[UB[r][ch], cwB, cvB[k][ch]], writes=[cvB[k][ch]])
                    p.op("dve", lambda e, a=a, u=u, ci=ci, t0=t0: e.scalar_tensor_tensor(
                        out=a, in0=u[:, t0:t0 + 512], scalar=cw[:, ci, 0:1], in1=a, op0=ALU.mult, op1=ALU.add),
                        reads=[UB[r][ch], cwB, cvB[k][ch]], writes=[cvB[k][ch]])
                p.op("act", lambda e, k=k: e.activation(out=sl[k], in_=cv[k][0], func=AF.Silu), reads=[cvB[k][0]], writes=[slB[k]])
                p.op("dve", lambda e, k=k, g_=g_, tsl=tsl: e.tensor_tensor(out=g_[:, tsl], in0=sl[k], in1=cv[k][1], op=ALU.mult),
                     reads=[slB[k], cvB[k][1]], pwrites=[gB_])
            p.dma("sp", self.GF[blk * 128:(blk + 1) * 128, :], g_, reads=[gB_], pwrites=[self.buf("GF")])
            if h == 0:
                for ch in range(2):
                    p.op("act", lambda e, r=r, ch=ch, blk=blk: e.copy(out=uh[:, l, 2 * blk + ch, :], in_=U[r][ch][:, T:T + 2]),
                         reads=[UB[r][ch]], pwrites=[uhB])

    def phase_G(self, l, h):
        p = self.p
        tok0 = h * T
        psf = self.psf
        psB = [self.buf("ps%d" % i) for i in range(7)]
        last = (l == self.nlayers - 1)
        for ps_i, (ka, kb_) in enumerate(((0, 24), (24, 44))):
            nkc = kb_ - ka
            self.barrier()
            Wd = self.alloc(nkc * D * 2, BF16, [nkc, D])
            WdB = self.buf("G_Wd%d" % ps_i)
            gT = [self.alloc(nkc * 256 * 2, BF16, [nkc, 256]) for _ in range(2)]
            gTB = [self.buf("G_gT%d_%d" % (ps_i, i)) for i in range(2)]
            yt = [self.alloc(D * 4, F32) for _ in range(3)]
            ytB = [self.buf("G_y%d_%d" % (ps_i, i)) for i in range(3)]
            if ps_i == 1:
                gt = self.alloc(D * 4, F32)
                bt = self.alloc(D * 4, F32)
                gbB = self.buf("G_gb")
                hb = [self.alloc(D * 2, BF16) for _ in range(2)]
                hbB = [self.buf("G_hb%d" % i) for i in range(2)]
                st = [self.alloc(32 * 4, F32) for _ in range(2)]
                stB = [self.buf("G_st%d" % i) for i in range(2)]
                XTs = self.alloc(KC * 256 * 2, BF16, [KC, 256])
                XTsB = self.buf("G_XTs")
                p.dma("sp", gt, self.ln2g[l:l + 1, :].broadcast_to([128, D]), pwrites=[gbB])
                p.dma("sp", bt, self.ln2b[l:l + 1, :].broadcast_to([128, D]), pwrites=[gbB])
            for c0 in range(0, nkc, 4):
                p.dma("sp", Wd[:, c0:c0 + 4, :], self.wb_down[l][:, ka + c0:ka + c0 + 4, :],
                      reads=[self.wbuf("down", l, (ka + c0) // 4)], pwrites=[WdB])
            pr = [0]

            def load_g(k, gT=gT, gTB=gTB, ka=ka, kb_=kb_):
                if k < T // 256:
                    p.dma("sp", gT[k % 2], self.GF[ka * 128:kb_ * 128, k * 256:(k + 1) * 256].rearrange("(c p) t -> p c t", p=128),
                          reads=[self.buf("GF")], writes=[gTB[k % 2]])

            def load_y(k, yt=yt, ytB=ytB, ps_i=ps_i):
                if k < T // 128:
                    if ps_i == 0:
                        p.dma("sp", yt[k % 3], self.Hres[k * 128:(k + 1) * 128, :], reads=[self.buf("Hres")], writes=[ytB[k % 3]])
                    else:
                        p.dma("sp", yt[k % 3], self.Y1[k * 128:(k + 1) * 128, :], reads=[self.buf("Y1")], writes=[ytB[k % 3]])

            load_g(0)
            load_y(0)
            banks = {}
            NB_ = T // 128

            def mm(tb, gT=gT, gTB=gTB, Wd=Wd, WdB=WdB, nkc=nkc, banks=banks, load_g=load_g, load_y=load_y, pr=pr):
                gi = (tb // 2) % 2
                if tb % 2 == 0:
                    load_g(tb // 2 + 1)
                load_y(tb + 1)
                bl = []
                for n in range(4):
                    bi = pr[0] % 7
                    pr[0] += 1
                    ps = psf[bi]
                    fns = [lambda e, kc=kc, ps=ps, tb=tb, n=n, g_=gT[gi], Wd=Wd, nkc=nkc: e.matmul(
                        ps[:, :], lhsT=g_[:, kc, (tb % 2) * 128:(tb % 2) * 128 + 128], rhs=Wd[:, kc, n * 512:(n + 1) * 512],
                        start=(kc == 0), stop=(kc == nkc - 1)) for kc in range(nkc)]
                    p.pe(fns, reads=[gTB[gi], WdB], writes=[psB[bi]])
                    bl.append(bi)
                banks[tb] = bl

            def adds(tb, yt=yt, ytB=ytB, banks=banks, ps_i=ps_i):
                y, yB = yt[tb % 3], ytB[tb % 3]
                sc = ALPHA if ps_i == 0 else 1.0
                for n, bi in enumerate(banks.pop(tb)):
                    ps = psf[bi]
                    p.op("dve", lambda e, y=y, ps=ps, n=n, sc=sc: e.scalar_tensor_tensor(
                        out=y[:, n * 512:(n + 1) * 512], in0=y[:, n * 512:(n + 1) * 512], scalar=sc, in1=ps[:, :],
                        op0=ALU.mult, op1=ALU.add), reads=[psB[bi]], writes=[yB])

            if ps_i == 0:
                def tail(tb, yt=yt, ytB=ytB):
                    y, yB = yt[tb % 3], ytB[tb % 3]
                    tl = tb * 128
                    p.dma("sp", self.Y1[tl:tl + 128, :], y, reads=[yB], pwrites=[self.buf("Y1")])
            else:
                def tail(tb, yt=yt, ytB=ytB, gt=gt, bt=bt, gbB=gbB, st=st, stB=stB, hb=hb, hbB=hbB, XTs=XTs, XTsB=XTsB):
                    y, yB = yt[tb % 3], ytB[tb % 3]
                    i = tb % 2
                    tl = tb * 128
                    self.ln_tile(y, yB, gt, bt, gbB, st[i], stB[i], None if last else hb[i], None if last else hbB[i])
                    if last:
                        p.dma("sp", self.y[tok0 + tl:tok0 + tl + 128, :], y, reads=[yB], pwrites=[self.buf("y_out")])
                    else:
                        p.dma("sp", self.Xres[tl:tl + 128, :], y, reads=[yB], pwrites=[self.buf("Xres")])
                        self.transpose_tile(hb[i], hbB[i], XTs, XTsB, (tb % 2) * 128)
                        if tb % 2 == 1:
                            t0 = tl - 128
                            p.dma("sp", self.XTd.rearrange("(c p) t -> p c t", p=128)[:, :, t0:t0 + 256], XTs,
                                  reads=[XTsB], pwrites=[self.buf("XTd")])

            for tb in range(NB_):
                mm(tb)
                if tb > 0:
                    tail(tb - 1)
                adds(tb)
            tail(NB_ - 1)

    def load_consts(self):
        p = self.p
        self.persist_end = 0
        self.aoff = 0
        self.ident_sb = self.alloc(128 * 2, BF16)
        self.masks_sb = self.alloc(4 * 128 * 2, BF16, [4, 128])
        p.dma("sp", self.ident_sb, self.ident, writes=[self.buf("ident")])
        p.dma("sp", self.masks_sb, self.masks, writes=[self.buf("masks")])
        self.maskrep = self.alloc(2 * 512 * 2, BF16, [2, 512])
        for kb, mi in ((0, 1), (1, 0)):
            for hh in range(4):
                p.op("dve", lambda e, kb=kb, mi=mi, hh=hh: e.tensor_copy(
                    out=self.maskrep[:, kb, hh * 128:(hh + 1) * 128], in_=self.masks_sb[:, mi, :]),
                    reads=[self.buf("masks")], pwrites=[self.buf("maskrep")])
        self.maskpair = self.alloc(2 * 256 * 2, BF16, [2, 256])
        for (t_, half, mi) in ((0, 0, 0), (0, 1, 3), (1, 0, 2), (1, 1, 0)):
            p.op("dve", lambda e, t_=t_, half=half, mi=mi: e.tensor_copy(
                out=self.maskpair[:, t_, half * 128:(half + 1) * 128], in_=self.masks_sb[:, mi, :]),
                reads=[self.buf("masks")], pwrites=[self.buf("maskpair")])
        self.uhalo = self.alloc(DEPTH * 88 * 2 * 4, F32, [DEPTH, 88, 2])
        p.op("dve", lambda e: e.memset(self.uhalo, 0.0), writes=[self.buf("uhalo")])
        self.persist_end = self.aoff

    def emit(self):
        self.conv_setup()
        self.load_consts()
        final = []
        for l in range(self.nlayers):
            self.convert_layer(l)
        for h in range(self.nhalves):
            for l in range(self.nlayers):
                for nm in "ABCDEFG":
                    getattr(self, "phase_" + nm)(l, h)
                    if self.stop == nm:
                        break
        allb = [b for b in self.bufs.values()]
        self.p.finish(allb)


_CACHE = {}


def _host_consts():
    if "c" in _CACHE:
        return _CACHE["c"]
    c = dict(_rope_tables())
    k = np.arange(128)[:, None]
    q = np.arange(128)[None, :]
    m = np.zeros((128, 4, 128), dtype=np.float32)
    m[:, 0, :] = np.where(k <= q, 0.0, NEG)
    m[:, 1, :] = np.where(k > q, 0.0, NEG)
    m[:, 2, :] = NEG
    c["masks"] = m.astype(ml_dtypes.bfloat16)
    c["ident"] = np.eye(128, dtype=np.float32).astype(ml_dtypes.bfloat16)
    _CACHE["c"] = c
    return c


def prepare_shared(inp):
    f = lambda a: np.ascontiguousarray(np.asarray(a, dtype=np.float32))
    sh = {}
    sh["w_in"] = np.ascontiguousarray(f(inp["w_in"])[:, :, _win_perm()])
    up = _wup_perm()
    sh["w_up"] = np.ascontiguousarray(f(inp["w_up"])[:, :, up])
    cw = f(inp["conv_w"])[:, :, up]
    sh["conv_w"] = np.ascontiguousarray(cw.reshape(DEPTH, 3, 88, 128).transpose(0, 3, 2, 1))
    cb = f(inp["conv_b"])[:, up]
    sh["conv_b"] = np.ascontiguousarray(cb.reshape(DEPTH, 88, 128).transpose(0, 2, 1))
    for k in ("w_proj_a", "w_proj_b", "w_out", "w_down", "sinks", "subln_g", "ln1_g", "ln1_b", "ln2_g", "ln2_b"):
        sh[k] = f(inp[k])
    sh["lam"] = np.ascontiguousarray(np.stack([f(inp["lambda_q1"]), f(inp["lambda_k1"]),
                                               f(inp["lambda_q2"]), f(inp["lambda_k2"])], axis=1))
    sh.update(_host_consts())
    return sh


def kernel(**inputs):
    x = np.asarray(inputs["x"], dtype=np.float32)
    sh = prepare_shared(inputs)
    kern = Kern()
    nc = kern.build()
    in_maps = []
    for c in range(NCORES):
        m = dict(sh)
        m["x"] = np.ascontiguousarray(x[c % BATCH])
        in_maps.append(m)
    res = run_bass_kernel_spmd(nc, in_maps, core_ids=list(range(NCORES)))
    out = np.stack([np.asarray(res.results[b]["y"], dtype=np.float32) for b in range(BATCH)], axis=0)
    return out
```

```python
import math
from contextlib import ExitStack

import numpy as np
import ml_dtypes

import concourse.bass as bass
import concourse.mybir as mybir
from concourse.bass_utils import run_bass_kernel_spmd

F32 = mybir.dt.float32
BF16 = mybir.dt.bfloat16
AF = mybir.ActivationFunctionType
ALU = mybir.AluOpType

D = 2048
SEQ = 4096
BATCH = 4
DEPTH = 2
HA, NQA, NKVA, GA = 64, 32, 4, 8
HB, NHB = 128, 8
DFF = 5632
INW = 12800
LN_EPS = 1e-5
SUB_EPS = 1e-5
ALPHA = (2 * DEPTH) ** 0.25
THETA = 10000.0
T = 2048
NH = SEQ // T
KC = D // 128
NEG = -30000.0

ENGS = ("pe", "act", "dve", "pool", "sp")
NCORES = 4


class Buf:
    __slots__ = ("name", "w", "r", "g")

    def __init__(self, name):
        self.name = name
        self.w = {}
        self.r = {}
        self.g = {}


def _merge(dst, src):
    for s, v in src.items():
        if dst.get(s, 0) < v:
            dst[s] = v


class Prog:
    DRING = 8

    def __init__(self, nc, stack):
        self.nc = nc
        self.st = stack
        self.q = {e: [] for e in ENGS}
        self.semh = {}
        self.cnt = {}
        self.seen = {e: {} for e in ENGS}
        for e in ("pe", "act", "dve", "pool"):
            self.newsem("P_" + e)
        self.dring = {}
        self.dpos = {}
        for e in ("sp", "act", "pool"):
            self.dring[e] = [self.newsem("D_%s_%d" % (e, i)) for i in range(self.DRING)]
            self.dpos[e] = 0
        self.nops = 0

    def newsem(self, name):
        self.semh[name] = self.st.enter_context(self.nc.semaphore(name))
        self.cnt[name] = 0
        return name

    def _need(self, eng, reads, writes, pwrites):
        need = {}
        for b in reads:
            _merge(need, b.w)
        for b in writes:
            _merge(need, b.g)
            _merge(need, b.w)
            _merge(need, b.r)
        for b in pwrites:
            if b.r:
                _merge(b.g, b.w)
                _merge(b.g, b.r)
                b.w = {}
                b.r = {}
            _merge(need, b.g)
        for s, v in need.items():
            if eng == "pe" and s == "P_pe":
                continue
            if self.seen[eng].get(s, 0) >= v:
                continue
            self.seen[eng][s] = v
            self.q[eng].append(("w", s, v))

    def _post(self, s, v, reads, writes, pwrites):
        for b in reads:
            if b.r.get(s, 0) < v:
                b.r[s] = v
        for b in writes:
            b.g = {}
            b.r = {}
            b.w = {s: v}
        for b in pwrites:
            if b.w.get(s, 0) < v:
                b.w[s] = v

    def op(self, eng, fn, reads=(), writes=(), pwrites=()):
        self._need(eng, reads, writes, pwrites)
        s = "P_" + eng
        self.cnt[s] += 1
        v = self.cnt[s]
        self.q[eng].append(("op", fn, s, 1))
        self._post(s, v, reads, writes, pwrites)
        self.nops += 1

    def pe(self, fns, reads=(), writes=(), pwrites=()):
        self._need("pe", reads, writes, pwrites)
        s = "P_pe"
        self.cnt[s] += 1
        v = self.cnt[s]
        for f in fns[:-1]:
            self.q["pe"].append(("op", f, None, 0))
        self.q["pe"].append(("op", fns[-1], s, 1))
        self._post(s, v, reads, writes, pwrites)
        self.nops += len(fns)

    def dma(self, eng, out, in_, reads=(), writes=(), pwrites=(), **kw):
        self._need(eng, reads, writes, pwrites)
        ring = self.dring[eng]
        s = ring[self.dpos[eng] % len(ring)]
        self.dpos[eng] += 1
        prev = self.cnt[s]
        if prev and self.seen[eng].get(s, 0) < prev:
            self.seen[eng][s] = prev
            self.q[eng].append(("w", s, prev))
        self.cnt[s] += 16
        v = self.cnt[s]
        self.q[eng].append(("op", lambda e, o=out, i=in_, k=kw: e.dma_start(out=o, in_=i, **k), s, 16))
        self._post(s, v, reads, writes, pwrites)
        self.nops += 1

    def finish(self, final_bufs):
        need = {}
        for b in final_bufs:
            _merge(need, b.w)
            _merge(need, b.g)
        for s, v in need.items():
            self.q["sp"].append(("w", s, v))

    def replay(self):
        nc = self.nc
        semh = self.semh

        def run(items, e):
            for it in items:
                if it[0] == "w":
                    e.wait_ge(semh[it[1]], it[2])
                else:
                    ins = it[1](e)
                    if it[2] is not None:
                        ins.then_inc(semh[it[2]], it[3])

        with nc.Block() as block:
            @block.sync
            def _(e):
                run(self.q["sp"], e)

            @block.tensor
            def _(e):
                run(self.q["pe"], e)

            @block.scalar
            def _(e):
                run(self.q["act"], e)

            @block.vector
            def _(e):
                run(self.q["dve"], e)

            @block.gpsimd
            def _(e):
                run(self.q["pool"], e)


def _win_perm():
    cols = []
    for i in range(8):
        for half in range(2):
            for hl in range(4):
                for d in range(32):
                    cols.append((4 * i + hl) * 64 + half * 32 + d)
    for half in range(2):
        for g in range(4):
            for d in range(32):
                cols.append(2048 + g * 64 + half * 32 + d)
    for base in (2560, 4608):
        for i in range(8):
            for half in range(2):
                for c in range(2):
                    for d in range(64):
                        cols.append(base + i * 256 + c * 128 + half * 64 + d)
    cols.extend(range(2304, 2560))
    cols.extend(range(6656, 8704))
    cols.extend(range(8704, 12800))
    assert len(cols) == INW and len(set(cols)) == INW
    return np.asarray(cols, dtype=np.int64)


def _wup_perm():
    cols = []
    for c in range(DFF // 128):
        cols.extend(range(c * 128, (c + 1) * 128))
        cols.extend(range(DFF + c * 128, DFF + (c + 1) * 128))
    return np.asarray(cols, dtype=np.int64)


def _rope_tables():
    out = {}
    t = np.arange(SEQ, dtype=np.float32)[None, :]
    for name, dim in (("A", HA), ("B", HB)):
        inv = (1.0 / (np.float32(THETA) ** (np.arange(0, dim, 2, dtype=np.float32) / np.float32(dim)))).astype(np.float32)
        p = np.arange(128) % (dim // 2)
        ang = (inv[p][:, None] * t).astype(np.float32)
        out["cos" + name] = np.cos(ang).astype(np.float32)
        out["sin" + name] = np.sin(ang).astype(np.float32)
    return out


BLK_QA0, BLK_KA, BLK_QB0, BLK_KB0, BLK_VA, BLK_VB0, BLK_G0 = 0, 8, 9, 17, 25, 26, 34
NBLK_IN = 50


class Kern:
    def __init__(self, debug=None, nlayers=DEPTH, nhalves=NH, stop=None):
        self.debug = debug or ()
        self.nlayers = nlayers
        self.nhalves = nhalves
        self.stop = stop
        self.nc = bass.Bass("TRN2", target_bir_lowering=False)
        self.bufs = {}

    def buf(self, name):
        b = self.bufs.get(name)
        if b is None:
            b = self.bufs[name] = Buf(name)
        return b

    def dram(self, name, shape, dt, kind="Internal"):
        if name in self.debug:
            kind = "ExternalOutput"
        return self.nc.dram_tensor(name, list(shape), dt, kind=kind).ap()

    def phase_begin(self):
        self.aoff = self.persist_end

    def alloc(self, nbytes, dt, shape=None, parts=128):
        nb0 = nbytes
        nbytes = (nbytes + 63) // 64 * 64
        off = self.aoff
        self.aoff += nbytes
        assert self.aoff <= self.ARENA, ("arena overflow", self.aoff)
        v = self.arena[0:parts, off // 4:(off + nbytes) // 4]
        if dt != F32:
            v = v.bitcast(dt)
            v = v[:, 0:nb0 // 2]
        else:
            v = v[:, 0:nb0 // 4]
        if shape is not None:
            names = " ".join("d%d" % i for i in range(len(shape)))
            kw = {"d%d" % i: int(s) for i, s in enumerate(shape)}
            v = v.rearrange("p (%s) -> p %s" % (names, names), **kw)
        return v

    def build(self):
        nc = self.nc
        with ExitStack() as st:
            self.st = st
            self.p = Prog(nc, st)
            self.ARENA = 168 * 1024
            self.arena = st.enter_context(nc.sbuf_tensor("arena", [128, self.ARENA // 4], F32))
            self.persist_end = 0
            self.aoff = 0
            self.tp_i = 0
            self.tp_two = True
            self.psf = [st.enter_context(nc.psum_tensor("psf%d" % i, [128, 512], F32)) for i in range(7)]
            self.psb = st.enter_context(nc.psum_tensor("psb", [128, 1024], BF16))
            self.psf6b = self.psf[6][:, :].bitcast(BF16)
            self.declare()
            self.emit()
            self.p.replay()
        return nc

    def declare(self):
        nc = self.nc
        ext = lambda n, s, dt=F32: nc.dram_tensor(n, list(s), dt, kind="ExternalInput").ap()
        self.x = ext("x", [SEQ, D])
        self.w_in = ext("w_in", [DEPTH, D, INW])
        self.w_pa = ext("w_proj_a", [DEPTH, D, D])
        self.w_pb = ext("w_proj_b", [DEPTH, D, D])
        self.w_out = ext("w_out", [DEPTH, D, D])
        self.w_up = ext("w_up", [DEPTH, D, 2 * DFF])
        self.w_down = ext("w_down", [DEPTH, DFF, D])
        self.sinks = ext("sinks", [DEPTH, NQA])
        self.lam = ext("lam", [DEPTH, 4, HB])
        self.subg = ext("subln_g", [DEPTH, 2 * HB])
        self.ln1g = ext("ln1_g", [DEPTH, D])
        self.ln1b = ext("ln1_b", [DEPTH, D])
        self.ln2g = ext("ln2_g", [DEPTH, D])
        self.ln2b = ext("ln2_b", [DEPTH, D])
        self.convw = ext("conv_w", [DEPTH, 128, 88, 3])
        self.convb = ext("conv_b", [DEPTH, 128, 88])
        self.cosA = ext("cosA", [128, SEQ])
        self.sinA = ext("sinA", [128, SEQ])
        self.cosB = ext("cosB", [128, SEQ])
        self.sinB = ext("sinB", [128, SEQ])
        self.masks = ext("masks", [128, 4, 128], BF16)
        self.ident = ext("ident", [128, 128], BF16)
        self.y = nc.dram_tensor("y", [SEQ, D], F32, kind="ExternalOutput").ap()

        self.wb_in = [self.dram("wb_in%d" % l, [NBLK_IN, 128, KC, 256], BF16) for l in range(DEPTH)]
        self.wb_pa = [self.dram("wb_pa%d" % l, [16, 128, KC, 128], BF16) for l in range(DEPTH)]
        self.wb_pb = [self.dram("wb_pb%d" % l, [16, 128, KC, 128], BF16) for l in range(DEPTH)]
        self.wb_out = [self.dram("wb_out%d" % l, [128, KC, D], BF16) for l in range(DEPTH)]
        self.wb_up = [self.dram("wb_up%d" % l, [44, 128, KC, 256], BF16) for l in range(DEPTH)]
        self.wb_down = [self.dram("wb_down%d" % l, [128, DFF // 128, D], BF16) for l in range(DEPTH)]
        self.QTa = self.dram("QTa", [NQA, HA, T], BF16)
        self.QTb = self.dram("QTb", [2 * NHB, HB, T], BF16)
        self.KTa = [self.dram("KTa%d" % l, [NKVA, HA, SEQ], BF16) for l in range(DEPTH)]
        self.KTb = [self.dram("KTb%d" % l, [2 * NHB, HB, SEQ], BF16) for l in range(DEPTH)]
        self.Va = [self.dram("Va%d" % l, [SEQ, NKVA * HA], BF16) for l in range(DEPTH)]
        self.Vb = [self.dram("Vb%d" % l, [SEQ, NHB * 2 * HB], BF16) for l in range(DEPTH)]
        self.GT = self.dram("GT", [2 * D, T], BF16)
        self.OTa = self.dram("OTa", [D, T], BF16)
        self.OTb = self.dram("OTb", [D, T], BF16)
        self.MT = self.dram("MT", [D, T], BF16)
        self.Hres = self.dram("Hres", [T, D], F32)
        self.GF = self.dram("GF", [DFF, T], BF16)
        self.Y1 = self.dram("Y1", [T, D], F32)
        self.Xres = self.dram("Xres", [T, D], F32)
        self.XTd = self.dram("XTd", [D, T], BF16)
        self.HTd = self.dram("HTd", [D, T], BF16)

    def barrier(self):
        p = self.p
        for eng in ("pe", "act", "dve", "sp"):
            for s, v in p.cnt.items():
                if s == "P_pool" or s.startswith("D_pool"):
                    continue
                if eng == "pe" and s == "P_pe":
                    continue
                if v and p.seen[eng].get(s, 0) < v:
                    p.seen[eng][s] = v
                    p.q[eng].append(("w", s, v))
        self.phase_begin()

    def wbuf(self, name, l, i):
        return self.buf("W_%s_%d_%d" % (name, l, i))

    def conv_setup(self):
        self.cv_f = [self.st.enter_context(self.nc.sbuf_tensor("cvf%d" % i, [128, 2048], F32)) for i in range(2)]
        self.cv_b = [self.st.enter_context(self.nc.sbuf_tensor("cvb%d" % i, [128, 2048], BF16)) for i in range(2)]
        self.cv_i = 0

    def conv_piece(self, src, dst, dst_shape3, wb):
        p = self.p
        i = self.cv_i % 2
        self.cv_i += 1
        n = src.shape[-1]
        f = self.cv_f[i][:, 0:n]
        b = self.cv_b[i][:, 0:n]
        bf_, bb_ = self.buf("cvf%d" % i), self.buf("cvb%d" % i)
        p.dma("pool", f, src, writes=[bf_])
        p.op("pool", lambda e, o=b, a=f: e.tensor_copy(out=o, in_=a), reads=[bf_], writes=[bb_])
        bsrc = b
        if dst_shape3 is not None:
            bsrc = b.rearrange("p (a n) -> p a n", a=dst_shape3)
        p.dma("pool", dst, bsrc, reads=[bb_], pwrites=[wb])

    def convert_layer(self, l):
        wsrc = self.w_in[l]
        for cg in range(7):
            c0 = cg * 2048
            n = min(2048, INW - c0)
            nb = n // 256
            for kc in range(KC):
                self.conv_piece(wsrc[kc * 128:(kc + 1) * 128, c0:c0 + n],
                                self.wb_in[l][8 * cg:8 * cg + nb, :, kc, :].rearrange("b p n -> p b n"),
                                nb, self.wbuf("in", l, cg))
        for nm, wsrc, wdst in (("pa", self.w_pa[l], self.wb_pa[l]), ("pb", self.w_pb[l], self.wb_pb[l])):
            for kc in range(KC):
                self.conv_piece(wsrc[kc * 128:(kc + 1) * 128, :],
                                wdst[:, :, kc, :].rearrange("c p n -> p c n"), 16, self.wbuf(nm, l, 0))
        for kc in range(KC):
            self.conv_piece(self.w_out[l][kc * 128:(kc + 1) * 128, :], self.wb_out[l][:, kc, :], None,
                            self.wbuf("out", l, 0))
        for cg in range(6):
            c0 = cg * 2048
            n = min(2048, 2 * DFF - c0)
            nb = n // 256
            for kc in range(KC):
                self.conv_piece(self.w_up[l][kc * 128:(kc + 1) * 128, c0:c0 + n],
                                self.wb_up[l][8 * cg:8 * cg + nb, :, kc, :].rearrange("b p n -> p b n"),
                                nb, self.wbuf("up", l, cg))
        for kc in range(DFF // 128):
            self.conv_piece(self.w_down[l][kc * 128:(kc + 1) * 128, :], self.wb_down[l][:, kc, :], None,
                            self.wbuf("down", l, kc // 4))

    def transpose_tile(self, src_bf, src_buf, dst3, dst_buf, tok_off, nchunks=16, pw=True):
        p = self.p
        ident, identB = self.ident_sb, self.buf("ident")
        for g0 in range(0, nchunks, 8):
            ng = min(8, nchunks - g0)
            if self.tp_two and self.tp_i % 2 == 1:
                psb, psbB = self.psf6b, self.buf("ps6")
            else:
                psb, psbB = self.psb, self.buf("psb")
            self.tp_i += 1
            fns = []
            for j in range(ng):
                c = g0 + j
                fns.append(lambda e, j=j, c=c, psb=psb: e.transpose(out=psb[:, j * 128:(j + 1) * 128],
                                                           in_=src_bf[:, c * 128:(c + 1) * 128], identity=ident))
            p.pe(fns, reads=[src_buf, identB], writes=[psbB])
            src = psb[:, 0:ng * 128].rearrange("p (c t) -> p c t", c=ng)
            dst = dst3[:, g0:g0 + ng, tok_off:tok_off + 128]
            kw = dict(reads=[psbB], pwrites=[dst_buf]) if pw else dict(reads=[psbB], writes=[dst_buf])
            eng = "act" if (g0 // 8) % 2 == 0 else "dve"
            if eng == "act":
                p.op("act", lambda e, o=dst, a=src: e.copy(out=o, in_=a), **kw)
            else:
                p.op("dve", lambda e, o=dst, a=src: e.tensor_copy(out=o, in_=a), **kw)

    def phase_A(self, l, h):
        p = self.p
        tok0 = h * T
        self.barrier()
        AT = self.alloc(KC * T * 2, BF16, [KC, T])
        ATb = self.buf("A_AT")
        wblk = [self.alloc(KC * 256 * 2, BF16, [KC, 256]) for _ in range(3)]
        wblkB = [self.buf("A_w%d" % i) for i in range(3)]
        tabs = {}
        for nm in ("cosA", "sinA", "cosB", "sinB"):
            tabs[nm] = self.alloc(T * 4, F32)
        tabB = self.buf("A_tabs")
        tmp = [[self.alloc(512 * 4, F32) for _ in range(4)] for _ in range(2)]
        tmpB = [[self.buf("A_tmp%d_%d" % (s, i)) for i in range(4)] for s in range(2)]
        stg = [self.alloc(T * 2, BF16) for _ in range(4)]
        stgB = [self.buf("A_stg%d" % i) for i in range(4)]
        nstg = [0]

        def next_stg():
            i = nstg[0] % 4
            nstg[0] += 1
            return stg[i], stgB[i]

        if l == 0:
            xf = [tabs["cosA"], tabs["sinA"]]
            xfB = [self.buf("A_xf%d" % i) for i in range(2)]
            xbv = tabs["cosB"].bitcast(BF16)
            xb = [xbv[:, 0:D], xbv[:, D:2 * D]]
            xbB = [self.buf("A_xb%d" % i) for i in range(2)]
            for tb in range(T // 128):
                i = tb % 2
                p.dma("sp", xf[i], self.x[tok0 + tb * 128:tok0 + (tb + 1) * 128, :], writes=[xfB[i]])
                if tb % 2 == 0:
                    p.op("dve", lambda e, o=xb[i], a=xf[i]: e.tensor_copy(out=o, in_=a), reads=[xfB[i]], writes=[xbB[i]])
                else:
                    p.op("act", lambda e, o=xb[i], a=xf[i]: e.copy(out=o, in_=a), reads=[xfB[i]], writes=[xbB[i]])
                self.transpose_tile(xb[i], xbB[i], AT, ATb, tb * 128)
        else:
            src = self.XTd.rearrange("(c p) t -> p c t", p=128)
            for c0 in range(0, KC, 4):
                p.dma("sp", AT[:, c0:c0 + 4, :], src[:, c0:c0 + 4, :], reads=[self.buf("XTd")], pwrites=[ATb])

        ov = {"cosA": ["A_xf0"], "sinA": ["A_xf1"], "cosB": ["A_xb0", "A_xb1"], "sinB": []}
        for nm in ("cosA", "sinA", "cosB", "sinB"):
            p.dma("sp", tabs[nm], getattr(self, nm)[:, tok0:tok0 + T], writes=[self.buf(n) for n in ov[nm]], pwrites=[tabB])

        psf = self.psf
        psB = [self.buf("ps%d" % i) for i in range(7)]
        pscur = [0]

        def next_ps():
            i = pscur[0] % 4
            pscur[0] += 1
            return psf[i], psB[i]

        wsrc = self.wb_in[l]

        def load_w(b):
            if b < NBLK_IN:
                p.dma("sp", wblk[b % 3], wsrc[b], reads=[self.wbuf("in", l, b // 8)], writes=[wblkB[b % 3]])

        load_w(0)
        load_w(1)
        for blk in range(NBLK_IN):
            wi = blk % 3
            W = wblk[wi]
            load_w(blk + 2)
            if blk < BLK_VA:
                if blk < BLK_KA:
                    cosT, sinT = tabs["cosA"], tabs["sinA"]
                elif blk == BLK_KA:
                    cosT, sinT = tabs["cosA"], tabs["sinA"]
                else:
                    cosT, sinT = tabs["cosB"], tabs["sinB"]
                s1, s1B = next_stg()
                s2, s2B = next_stg()
                for tt in range(4):
                    tsl = slice(tt * 512, (tt + 1) * 512)
                    pp = []
                    for ch in range(2):
                        ps, psb_ = next_ps()
                        fns = [lambda e, kc=kc, ps=ps, ch=ch, W=W, tsl=tsl: e.matmul(
                            ps[:, :], lhsT=W[:, kc, ch * 128:(ch + 1) * 128], rhs=AT[:, kc, tsl],
                            start=(kc == 0), stop=(kc == KC - 1)) for kc in range(KC)]
                        p.pe(fns, reads=[wblkB[wi], ATb], writes=[psb_])
                        pp.append((ps, psb_))
                    (P1, P1B), (P2, P2B) = pp
                    tm, tmB = tmp[tt % 2], tmpB[tt % 2]
                    TT = lambda o, a, b, op: (lambda e: e.tensor_tensor(out=o, in0=a, in1=b, op=op))
                    p.op("dve", TT(tm[0], P1[:, :], cosT[:, tsl], ALU.mult), reads=[P1B, tabB], writes=[tmB[0]])
                    p.op("dve", TT(tm[3], P1[:, :], sinT[:, tsl], ALU.mult), reads=[P1B, tabB], writes=[tmB[3]])
                    p.op("dve", TT(tm[1], P2[:, :], sinT[:, tsl], ALU.mult), reads=[P2B, tabB], writes=[tmB[1]])
                    p.op("dve", TT(tm[2], P2[:, :], cosT[:, tsl], ALU.mult), reads=[P2B, tabB], writes=[tmB[2]])
                    p.op("dve", TT(s1[:, tsl], tm[0], tm[1], ALU.subtract), reads=[tmB[0], tmB[1]], pwrites=[s1B])
                    p.op("dve", TT(s2[:, tsl], tm[2], tm[3], ALU.add), reads=[tmB[2], tmB[3]], pwrites=[s2B])
                if blk < BLK_KA:
                    i = blk
                    for hl in range(4):
                        p.dma("sp", self.QTa[4 * i + hl, 0:32, :], s1[32 * hl:32 * hl + 32, :], reads=[s1B], pwrites=[self.buf("QTa")])
                        p.dma("sp", self.QTa[4 * i + hl, 32:64, :], s2[32 * hl:32 * hl + 32, :], reads=[s2B], pwrites=[self.buf("QTa")])
                elif blk == BLK_KA:
                    for g in range(4):
                        p.dma("sp", self.KTa[l][g, 0:32, tok0:tok0 + T], s1[32 * g:32 * g + 32, :], reads=[s1B], pwrites=[self.buf("KTa%d" % l)])
                        p.dma("sp", self.KTa[l][g, 32:64, tok0:tok0 + T], s2[32 * g:32 * g + 32, :], reads=[s2B], pwrites=[self.buf("KTa%d" % l)])
                else:
                    isq = blk < BLK_KB0
                    i = blk - (BLK_QB0 if isq else BLK_KB0)
                    for c in range(2):
                        if isq:
                            d1, d2, db = self.QTb[2 * i + c, 0:64, :], self.QTb[2 * i + c, 64:128, :], self.buf("QTb")
                        else:
                            d1 = self.KTb[l][2 * i + c, 0:64, tok0:tok0 + T]
                            d2 = self.KTb[l][2 * i + c, 64:128, tok0:tok0 + T]
                            db = self.buf("KTb%d" % l)
                        p.dma("sp", d1, s1[64 * c:64 * c + 64, :], reads=[s1B], pwrites=[db])
                        p.dma("sp", d2, s2[64 * c:64 * c + 64, :], reads=[s2B], pwrites=[db])
            elif blk < BLK_G0:
                if blk == BLK_VA:
                    vdst, vb_, c0 = self.Va[l], self.buf("Va%d" % l), 0
                else:
                    vdst, vb_, c0 = self.Vb[l], self.buf("Vb%d" % l), (blk - BLK_VB0) * 256
                for hh in range(2):
                    s, sB = next_stg()
                    s3 = s.rearrange("p (a n) -> p a n", a=8)
                    for j in range(8):
                        tb = hh * 8 + j
                        ps, psb_ = next_ps()
                        fns = [lambda e, kc=kc, ps=ps, tb=tb, W=W: e.matmul(
                            ps[:, 0:256], lhsT=AT[:, kc, tb * 128:(tb + 1) * 128], rhs=W[:, kc, :],
                            start=(kc == 0), stop=(kc == KC - 1)) for kc in range(KC)]
                        p.pe(fns, reads=[wblkB[wi], ATb], writes=[psb_])
                        p.op("act", lambda e, o=s3[:, j, :], a=ps[:, 0:256]: e.copy(out=o, in_=a), reads=[psb_], pwrites=[sB])
                    t0 = tok0 + hh * 1024
                    p.dma("sp", vdst[t0:t0 + 1024, c0:c0 + 256].rearrange("(a p) n -> p a n", p=128), s3,
                          reads=[sB], pwrites=[vb_])
            else:
                for ch in range(2):
                    s, sB = next_stg()
                    for tt in range(4):
                        tsl = slice(tt * 512, (tt + 1) * 512)
                        ps, psb_ = next_ps()
                        fns = [lambda e, kc=kc, ps=ps, ch=ch, W=W, tsl=tsl: e.matmul(
                            ps[:, :], lhsT=W[:, kc, ch * 128:(ch + 1) * 128], rhs=AT[:, kc, tsl],
                            start=(kc == 0), stop=(kc == KC - 1)) for kc in range(KC)]
                        p.pe(fns, reads=[wblkB[wi], ATb], writes=[psb_])
                        p.op("act", lambda e, o=s[:, tsl], a=ps[:, :]: e.activation(out=o, in_=a, func=AF.Sigmoid),
                             reads=[psb_], pwrites=[sB])
                    r0 = (blk - BLK_G0) * 256 + ch * 128
                    p.dma("sp", self.GT[r0:r0 + 128, :], s, reads=[sB], pwrites=[self.buf("GT")])

    def phase_B(self, l, h):
        p = self.p
        tok0 = h * T
        self.barrier()
        NKB = 17
        k0 = tok0 - 128
        QT = [self.alloc(8 * T * 2, BF16, [8, T], parts=64) for _ in range(2)]
        KT = [self.alloc(NKB * 128 * 2, BF16, parts=64) for _ in range(2)]
        QTB = [self.buf("B_QT%d" % i) for i in range(2)]
        KTB = [self.buf("B_KT%d" % i) for i in range(2)]
        V = self.alloc(NKB * 4 * 65 * 2, BF16, [NKB, 4, 65])
        VB = self.buf("B_V")
        PT = [[self.alloc(1024 * 2, BF16) for _ in range(2)] for _ in range(2)]
        PTB = [[self.buf("B_PT%d_%d" % (r, k)) for k in range(2)] for r in range(2)]
        ot = [self.alloc(512 * 2, BF16) for _ in range(2)]
        otB = [self.buf("B_ot%d" % i) for i in range(2)]
        OTs = [self.alloc(4 * T * 2, BF16, [4, T]) for _ in range(2)]
        OTsB = [self.buf("B_OTs%d" % i) for i in range(2)]
        esink = self.alloc(32 * 4, F32)
        esB = self.buf("B_esink")
        den = [self.alloc(8 * 4, F32) for _ in range(2)]
        denB = [self.buf("B_den%d" % i) for i in range(2)]
        psf = self.psf
        psB = [self.buf("ps%d" % i) for i in range(7)]
        scale = HA ** -0.5

        p.dma("sp", esink, self.sinks[l:l + 1, :].broadcast_to([128, NQA]), writes=[esB])
        p.op("act", lambda e: e.activation(out=esink, in_=esink, func=AF.Exp), reads=[esB], writes=[esB])
        p.op("dve", lambda e: e.memset(V[:, :, :, 64:65], 1.0), pwrites=[VB])
        kb_lo = 1 if h == 0 else 0
        for g in range(NKVA):
            src = self.Va[l][max(k0, 0):tok0 + T, g * 64:(g + 1) * 64].rearrange("(a p) n -> p a n", p=128)
            p.dma("sp", V[:, kb_lo:NKB, g, 0:64], src, reads=[self.buf("Va%d" % l)], pwrites=[VB])

        sring = [0]
        pend = []

        pend_t = []

        def emit_pv(item):
            g, j, kbs, r, gi = item
            Oa, Ob_ = psf[4], psf[5]
            for i in range(8):
                bank = Oa if i < 4 else Ob_
                out = bank[:, (i % 4) * 65:(i % 4) * 65 + 65]
                fns = []
                for n_, kb in enumerate(kbs):
                    kblk = j + kb
                    fns.append(lambda e, out=out, kb=kb, kblk=kblk, i=i, first=(n_ == 0), last=(n_ == len(kbs) - 1):
                               e.matmul(out, lhsT=PT[r][kb][:, i * 128:(i + 1) * 128], rhs=V[:, kblk, g, :],
                                        start=first, stop=last))
                p.pe(fns, reads=[PTB[r][kb] for kb in kbs] + [VB], pwrites=[psB[4 if i < 4 else 5]])
            flush_t()
            o = ot[r]
            for bi, bank in enumerate((Oa, Ob_)):
                O3 = bank[:, 0:260].rearrange("p (a n) -> p a n", a=4)
                dn = den[r][:, bi * 4:(bi + 1) * 4]
                hs = 8 * g + bi * 4
                p.op("dve", lambda e, dn=dn, O3=O3, hs=hs: e.tensor_tensor(
                    out=dn, in0=O3[:, :, 64], in1=esink[:, hs:hs + 4], op=ALU.add),
                    reads=[psB[4 + bi], esB], pwrites=[denB[r]])
                p.op("dve", lambda e, dn=dn: e.reciprocal(out=dn, in_=dn), reads=[denB[r]], pwrites=[denB[r]])
                o3 = o[:, bi * 256:(bi + 1) * 256].rearrange("p (a n) -> p a n", a=4)
                p.op("dve", lambda e, dn=dn, O3=O3, o3=o3: e.tensor_tensor(
                    out=o3, in0=O3[:, :, 0:64], in1=dn.unsqueeze(2).broadcast_to([128, 4, 64]), op=ALU.mult),
                    reads=[psB[4 + bi], denB[r]], pwrites=[otB[r]])
            pend_t.append((o, otB[r], OTs[gi], OTsB[gi], j * 128))

        def flush_t():
            while pend_t:
                a = pend_t.pop(0)
                self.transpose_tile(a[0], a[1], a[2], a[3], a[4], nchunks=4)

        def load_grp(g):
            if g >= NKVA:
                return
            gi = g % 2
            p.dma("sp", QT[gi], self.QTa[8 * g:8 * g + 8, :, :].rearrange("h d t -> d h t"),
                  reads=[self.buf("QTa")], writes=[QTB[gi]])
            if h == 0:
                p.dma("sp", KT[gi][:, 128:NKB * 128], self.KTa[l][g, :, 0:T], reads=[self.buf("KTa%d" % l)], writes=[KTB[gi]])
            else:
                p.dma("sp", KT[gi], self.KTa[l][g, :, k0:tok0 + T], reads=[self.buf("KTa%d" % l)], writes=[KTB[gi]])

        load_grp(0)
        for g in range(NKVA):
            gi = g % 2
            load_grp(g + 1)
            for j in range(T // 128):
                gq = h * 16 + j
                kbs = [1] if gq == 0 else [0, 1]
                r = j % 2
                for kb in kbs:
                    for hh in range(2):
                        bi = sring[0] % 4
                        sring[0] += 1
                        bank = psf[bi]
                        kcol = (j + kb) * 128
                        fns = [
                            lambda e, bank=bank, kcol=kcol, hh=hh, j=j, gi=gi: e.matmul(
                                bank[:, :], lhsT=KT[gi][:, kcol:kcol + 128],
                                rhs=QT[gi][:, 4 * hh:4 * hh + 4, j * 128:(j + 1) * 128], start=True, stop=False),
                            lambda e, bank=bank, kb=kb: e.matmul(
                                bank[:, :], lhsT=self.ident_sb, rhs=self.maskrep[:, kb, :], start=False, stop=True),
                        ]
                        p.pe(fns, reads=[KTB[gi], QTB[gi], self.buf("ident"), self.buf("maskrep")], writes=[psB[bi]])
                        p.op("act", lambda e, bank=bank, r=r, kb=kb, hh=hh: e.activation(
                            out=PT[r][kb][:, hh * 512:(hh + 1) * 512], in_=bank[:, :], func=AF.Exp, scale=scale),
                            reads=[psB[bi]], pwrites=[PTB[r][kb]])
                pend.append((g, j, kbs, r, gi))
                if len(pend) > 1:
                    emit_pv(pend.pop(0))
            while pend:
                emit_pv(pend.pop(0))
            flush_t()
            p.dma("sp", self.OTa[g * 512:(g + 1) * 512, :].rearrange("(c p) t -> p c t", p=128), OTs[gi],
                  reads=[OTsB[gi]], pwrites=[self.buf("OTa")])

    def phase_C(self, l, h):
        self.tp_two = False
        try:
            self._phase_C(l, h)
        finally:
            self.tp_two = True

    def _phase_C(self, l, h):
        p = self.p
        tok0 = h * T
        self.barrier()
        nk = tok0 + T
        nkb = nk // 128
        KT = [self.alloc(2 * SEQ * 2, BF16, [2, SEQ]) for _ in range(2)]
        QT = [self.alloc(2 * T * 2, BF16, [2, T]) for _ in range(2)]
        V = [self.alloc(32 * 257 * 2, BF16, [32, 257]) for _ in range(2)]
        KTB = [self.buf("C_KT%d" % i) for i in range(2)]
        QTB = [self.buf("C_QT%d" % i) for i in range(2)]
        VB = [self.buf("C_V%d" % i) for i in range(2)]
        PT = [self.alloc(512 * 2, BF16) for _ in range(3)]
        PTB = [self.buf("C_PT%d" % i) for i in range(3)]
        OTs = [self.alloc(2 * T * 2, BF16, [2, T]) for _ in range(2)]
        OTsB = [self.buf("C_OTs%d" % i) for i in range(2)]
        tf = [self.alloc(256 * 4, F32) for _ in range(2)]
        tfB = [self.buf("C_tf%d" % i) for i in range(2)]
        of = [self.alloc(256 * 4, F32) for _ in range(2)]
        ofB = [self.buf("C_of%d" % i) for i in range(2)]
        junk = self.alloc(256 * 4, F32)
        junkB = self.buf("C_junk")
        ob = [self.alloc(256 * 2, BF16) for _ in range(2)]
        obB = [self.buf("C_ob%d" % i) for i in range(2)]
        sm = [self.alloc(8 * 4, F32) for _ in range(2)]
        OC = [[[self.alloc(257 * 4, F32) for _ in range(2)] for _ in range(2)] for _ in range(2)]
        OCB = [[[self.buf("C_OC%d_%d_%d" % (s, c, q)) for q in range(2)] for c in range(2)] for s in range(2)]
        OFs = [self.alloc(16 * 256 * 4, F32, [16, 256]) for _ in range(2)]
        OFBs = [self.buf("C_OF%d" % i) for i in range(2)]
        sss = [self.alloc(64 * 4, F32) for _ in range(2)]
        ssBs = [self.buf("C_ss%d" % i) for i in range(2)]
        smB = [self.buf("C_sm%d" % i) for i in range(2)]
        lamt = self.alloc(4 * HB * 4, F32, [4, HB])
        lamB = self.buf("C_lamt")
        lsc = self.alloc(8 * 4, F32)
        lscB = self.buf("C_lsc")
        subg = self.alloc(256 * 4, F32)
        subgB = self.buf("C_subg")
        psf = self.psf
        psB = [self.buf("ps%d" % i) for i in range(7)]
        scale = HB ** -0.5
        lam_init = 0.8 - 0.6 * math.exp(-0.3 * l)

        p.dma("sp", lamt, self.lam[l:l + 1, :, :].broadcast_to([128, 4, HB]), writes=[lamB])
        p.dma("sp", subg, self.subg[l:l + 1, :].broadcast_to([128, 2 * HB]), writes=[subgB])
        for i in range(2):
            p.op("dve", lambda e, i=i: e.scalar_tensor_tensor(
                out=junk[:, 0:HB], in0=lamt[:, 2 * i, :], scalar=1.0, in1=lamt[:, 2 * i + 1, :],
                op0=ALU.mult, op1=ALU.mult, accum_out=lsc[:, i:i + 1]),
                reads=[lamB], writes=[junkB], pwrites=[lscB])
        p.op("act", lambda e: e.activation(out=lsc[:, 0:2], in_=lsc[:, 0:2], func=AF.Exp), reads=[lscB], writes=[lscB])
        p.op("dve", lambda e: e.scalar_tensor_tensor(
            out=lsc[:, 2:3], in0=lsc[:, 1:2], scalar=-lam_init, in1=lsc[:, 0:1], op0=ALU.add, op1=ALU.subtract),
            reads=[lscB], writes=[lscB])
        for i in range(2):
            p.op("dve", lambda e, i=i: e.memset(V[i][:, :, 256:257], 1.0), pwrites=[VB[i]])
        p.op("dve", lambda e: e.tensor_scalar(out=subg, in0=subg, scalar1=(1.0 - lam_init), scalar2=None, op0=ALU.mult),
             reads=[subgB], writes=[subgB])

        sring = [0]
        Oacc = [[psf[2], psf[3]], [psf[4], psf[5]]]
        OaccB = [[psB[2], psB[3]], [psB[4], psB[5]]]
        SB = [0, 1, 6]
        fin = [0]

        def load_head(hd):
            if hd >= NHB:
                return
            hi = hd % 2
            for c in range(2):
                p.dma("sp", KT[hi][:, c, 0:nk], self.KTb[l][2 * hd + c, :, 0:nk], reads=[self.buf("KTb%d" % l)], pwrites=[KTB[hi]])
                p.dma("sp", QT[hi][:, c, :], self.QTb[2 * hd + c, :, :], reads=[self.buf("QTb")], pwrites=[QTB[hi]])
            p.dma("sp", V[hi][:, 0:nkb, 0:256],
                  self.Vb[l][0:nk, hd * 256:(hd + 1) * 256].rearrange("(a p) n -> p a n", p=128),
                  reads=[self.buf("Vb%d" % l)], pwrites=[VB[hi]])

        ep_steps = []

        def run_steps(n):
            for _ in range(n):
                if ep_steps:
                    ep_steps.pop(0)()

        def epilogue(hd):
            run_steps(len(ep_steps))
            hi = hd % 2
            OF, OFB, ss, ssB = OFs[hi], OFBs[hi], sss[hi], ssBs[hi]

            def s0():
                p.op("dve", lambda e: e.tensor_scalar(out=ss[:, 16:32], in0=ss[:, 0:16], scalar1=1.0 / 256.0, scalar2=SUB_EPS,
                                                      op0=ALU.mult, op1=ALU.add), reads=[ssB], pwrites=[ssB])
                self.rsqrt(ss[:, 16:32], ss[:, 32:48], ss[:, 48:64], ssB)
            ep_steps.append(s0)
            for jb in range(T // 128):
                def sj(jb=jb):
                    f = jb % 2
                    p.op("dve", lambda e, f=f, jb=jb: e.scalar_tensor_tensor(
                        out=ob[f], in0=OF[:, jb, :], scalar=ss[:, 32 + jb:33 + jb], in1=subg, op0=ALU.mult, op1=ALU.mult),
                        reads=[OFB, ssB, subgB], writes=[obB[f]])
                    self.transpose_tile(ob[f], obB[f], OTs[hi], OTsB[hi], jb * 128, nchunks=2)
                ep_steps.append(sj)

            def sl():
                p.dma("sp", self.OTb[hd * 256:(hd + 1) * 256, :].rearrange("(c p) t -> p c t", p=128), OTs[hi],
                      reads=[OTsB[hi]], pwrites=[self.buf("OTb")])
            ep_steps.append(sl)

        load_head(0)
        for hd in range(NHB):
            hi = hd % 2
            OF, OFB, ss, ssB = OFs[hi], OFBs[hi], sss[hi], ssBs[hi]
            load_head(hd + 1)
            for jp in range(T // 256):
                gq0 = h * 16 + 2 * jp
                nkbs = gq0 + 2
                pend = []

                def emit_pv(item):
                    kb, pi = item
                    for c in range(2):
                        for qq in range(2):
                            if kb == gq0 + 1 and qq == 0:
                                continue
                            last = (kb == gq0 + qq)
                            p.pe([lambda e, c=c, qq=qq, kb=kb, pi=pi, last=last, hi=hi: e.matmul(
                                Oacc[c][qq][:, 0:257], lhsT=PT[pi][:, c * 256 + qq * 128:c * 256 + qq * 128 + 128],
                                rhs=V[hi][:, kb, :], start=(kb == 0), stop=last)],
                                reads=[PTB[pi], VB[hi]], pwrites=[OaccB[c][qq]])

                for kb in range(nkbs):
                    bi = SB[sring[0] % 3]
                    pi = sring[0] % 3
                    sring[0] += 1
                    bank = psf[bi]
                    masked = kb >= gq0
                    fns = []
                    for c in range(2):
                        fns.append(lambda e, bank=bank, c=c, kb=kb, masked=masked, hi=hi, jp=jp: e.matmul(
                            bank[:, c * 256:(c + 1) * 256], lhsT=KT[hi][:, c, kb * 128:(kb + 1) * 128],
                            rhs=QT[hi][:, c, jp * 256:(jp + 1) * 256], start=True, stop=(not masked)))
                        if masked:
                            mt = kb - gq0
                            fns.append(lambda e, bank=bank, c=c, mt=mt: e.matmul(
                                bank[:, c * 256:(c + 1) * 256], lhsT=self.ident_sb, rhs=self.maskpair[:, mt, :],
                                start=False, stop=True))
                    p.pe(fns, reads=[KTB[hi], QTB[hi], self.buf("ident"), self.buf("maskpair")], writes=[psB[bi]])
                    p.op("act", lambda e, bank=bank, pi=pi: e.activation(out=PT[pi], in_=bank[:, :], func=AF.Exp, scale=scale),
                         reads=[psB[bi]], writes=[PTB[pi]])
                    pend.append((kb, pi))
                    if len(pend) > 2:
                        emit_pv(pend.pop(0))
                    if kb == nkbs // 2:
                        run_steps(3)

                while pend:
                    emit_pv(pend.pop(0))
                oset = (hd * (T // 256) + jp) % 2
                for c in range(2):
                    for qq in range(2):
                        src_ = Oacc[c][qq][:, 0:257]
                        dst_ = OC[oset][c][qq]
                        p.op("dve", lambda e, o=dst_, a=src_: e.tensor_copy(out=o, in_=a), reads=[OaccB[c][qq]], writes=[OCB[oset][c][qq]])
                for qq in range(2):
                    f = fin[0] % 2
                    fin[0] += 1
                    O1, O2 = OC[oset][0][qq], OC[oset][1][qq]
                    O1B, O2B = OCB[oset][0][qq], OCB[oset][1][qq]
                    s_ = sm[f]
                    p.op("dve", lambda e, s_=s_, O1=O1: e.reciprocal(out=s_[:, 0:1], in_=O1[:, 256:257]), reads=[O1B], pwrites=[smB[f]])
                    p.op("dve", lambda e, s_=s_, O2=O2: e.reciprocal(out=s_[:, 1:2], in_=O2[:, 256:257]), reads=[O2B], pwrites=[smB[f]])
                    p.op("dve", lambda e, s_=s_: e.tensor_tensor(out=s_[:, 1:2], in0=s_[:, 1:2], in1=lsc[:, 2:3], op=ALU.mult),
                         reads=[smB[f], lscB], pwrites=[smB[f]])
                    p.op("dve", lambda e, s_=s_, O1=O1, f=f: e.tensor_scalar(
                        out=tf[f], in0=O1[:, 0:256], scalar1=s_[:, 0:1], scalar2=None, op0=ALU.mult),
                        reads=[O1B, smB[f]], writes=[tfB[f]])
                    jb = 2 * jp + qq
                    p.op("dve", lambda e, s_=s_, O2=O2, f=f, jb=jb, OF=OF: e.scalar_tensor_tensor(
                        out=OF[:, jb, :], in0=O2[:, 0:256], scalar=s_[:, 1:2], in1=tf[f], op0=ALU.mult, op1=ALU.add),
                        reads=[O2B, smB[f], tfB[f]], pwrites=[OFB])
                    p.op("dve", lambda e, jb=jb, OF=OF, ss=ss: e.scalar_tensor_tensor(
                        out=junk, in0=OF[:, jb, :], scalar=1.0, in1=OF[:, jb, :], op0=ALU.mult, op1=ALU.mult,
                        accum_out=ss[:, jb:jb + 1]), reads=[OFB], writes=[junkB], pwrites=[ssB])
                if jp == 0 and hd > 0:
                    epilogue(hd - 1)
        epilogue(NHB - 1)
        run_steps(len(ep_steps))

    def phase_D(self, l, h):
        p = self.p
        self.barrier()
        TS = 1024
        OA = self.alloc(KC * TS * 2, BF16, [KC, TS])
        OB = self.alloc(KC * TS * 2, BF16, [KC, TS])
        OAB, OBB = self.buf("D_OA"), self.buf("D_OB")
        wa = [self.alloc(KC * 128 * 2, BF16, [KC, 128]) for _ in range(3)]
        wb = [self.alloc(KC * 128 * 2, BF16, [KC, 128]) for _ in range(3)]
        waB = [self.buf("D_wa%d" % i) for i in range(3)]
        wbB = [self.buf("D_wb%d" % i) for i in range(3)]
        ga = [self.alloc(TS * 2, BF16) for _ in range(3)]
        gb = [self.alloc(TS * 2, BF16) for _ in range(3)]
        gaB = [self.buf("D_ga%d" % i) for i in range(3)]
        gbB = [self.buf("D_gb%d" % i) for i in range(3)]
        t1 = [self.alloc(512 * 4, F32) for _ in range(2)]
        t2 = [self.alloc(512 * 4, F32) for _ in range(2)]
        t1B = [self.buf("D_t1%d" % i) for i in range(2)]
        t2B = [self.buf("D_t2%d" % i) for i in range(2)]
        ms = [self.alloc(TS * 2, BF16) for _ in range(3)]
        msB = [self.buf("D_ms%d" % i) for i in range(3)]
        psf = self.psf
        psB = [self.buf("ps%d" % i) for i in range(7)]
        pr = [0]
        it = [0]
        NIT = (T // TS) * KC

        def load_it(k):
            if k >= NIT:
                return
            s_, c_ = k // KC, k % KC
            i_ = k % 3
            t_ = s_ * TS
            p.dma("sp", wa[i_], self.wb_pa[l][c_], reads=[self.wbuf("pa", l, 0)], writes=[waB[i_]])
            p.dma("sp", wb[i_], self.wb_pb[l][c_], reads=[self.wbuf("pb", l, 0)], writes=[wbB[i_]])
            p.dma("sp", ga[i_], self.GT[c_ * 128:(c_ + 1) * 128, t_:t_ + TS], reads=[self.buf("GT")], writes=[gaB[i_]])
            p.dma("sp", gb[i_], self.GT[D + c_ * 128:D + (c_ + 1) * 128, t_:t_ + TS], reads=[self.buf("GT")], writes=[gbB[i_]])

        load_it(0)
        load_it(1)
        for sub in range(T // TS):
            ts0 = sub * TS
            for c0 in range(0, KC, 4):
                p.dma("sp", OA[:, c0:c0 + 4, :], self.OTa[c0 * 128:(c0 + 4) * 128, ts0:ts0 + TS].rearrange("(c p) t -> p c t", p=128),
                      reads=[self.buf("OTa")], pwrites=[OAB])
                p.dma("sp", OB[:, c0:c0 + 4, :], self.OTb[c0 * 128:(c0 + 4) * 128, ts0:ts0 + TS].rearrange("(c p) t -> p c t", p=128),
                      reads=[self.buf("OTb")], pwrites=[OBB])
            for c in range(KC):
                i = it[0] % 3
                load_it(it[0] + 2)
                it[0] += 1
                for tt in range(TS // 512):
                    tsl = slice(tt * 512, (tt + 1) * 512)
                    banks = []
                    for (W, WB, O, OBf) in ((wa[i], waB[i], OA, OAB), (wb[i], wbB[i], OB, OBB)):
                        bi = pr[0] % 6
                        pr[0] += 1
                        ps = psf[bi]
                        fns = [lambda e, kc=kc, ps=ps, W=W, O=O, tsl=tsl: e.matmul(
                            ps[:, :], lhsT=W[:, kc, :], rhs=O[:, kc, tsl], start=(kc == 0), stop=(kc == KC - 1))
                            for kc in range(KC)]
                        p.pe(fns, reads=[WB, OBf], writes=[psB[bi]])
                        banks.append((ps, psB[bi]))
                    (pa, paB), (pb_, pbB) = banks
                    k = tt % 2
                    p.op("dve", lambda e, k=k, pa=pa, i=i, tsl=tsl: e.tensor_tensor(out=t1[k], in0=pa[:, :], in1=ga[i][:, tsl], op=ALU.mult),
                         reads=[paB, gaB[i]], writes=[t1B[k]])
                    p.op("dve", lambda e, k=k, pb_=pb_, i=i, tsl=tsl: e.tensor_tensor(out=t2[k], in0=pb_[:, :], in1=gb[i][:, tsl], op=ALU.mult),
                         reads=[pbB, gbB[i]], writes=[t2B[k]])
                    p.op("dve", lambda e, k=k, i=i, tsl=tsl: e.tensor_tensor(out=ms[i][:, tsl], in0=t1[k], in1=t2[k], op=ALU.add),
                         reads=[t1B[k], t2B[k]], pwrites=[msB[i]])
                p.dma("sp", self.MT[c * 128:(c + 1) * 128, ts0:ts0 + TS], ms[i], reads=[msB[i]], pwrites=[self.buf("MT")])

    def rsqrt(self, v, r, t, B, iters=2):
        p = self.p
        p.op("act", lambda e: e.activation(out=r, in_=v, func=AF.Sqrt), reads=[B], pwrites=[B])
        p.op("dve", lambda e: e.reciprocal(out=r, in_=r), reads=[B], pwrites=[B])
        for _ in range(iters):
            p.op("dve", lambda e: e.tensor_tensor(out=t, in0=r, in1=r, op=ALU.mult), reads=[B], pwrites=[B])
            p.op("dve", lambda e: e.tensor_tensor(out=t, in0=t, in1=v, op=ALU.mult), reads=[B], pwrites=[B])
            p.op("dve", lambda e: e.tensor_scalar(out=t, in0=t, scalar1=-0.5, scalar2=1.5, op0=ALU.mult, op1=ALU.add),
                 reads=[B], pwrites=[B])
            p.op("dve", lambda e: e.tensor_tensor(out=r, in0=r, in1=t, op=ALU.mult), reads=[B], pwrites=[B])

    def rsqrt1(self, v, r, t, B):
        p = self.p
        p.op("act", lambda e: e.activation(out=r, in_=v, func=AF.Sqrt), reads=[B], pwrites=[B])
        p.op("dve", lambda e: e.reciprocal(out=r, in_=r), reads=[B], pwrites=[B])
        for _ in range(2):
            p.op("dve", lambda e: e.scalar_tensor_tensor(out=t, in0=r, scalar=v, in1=r, op0=ALU.mult, op1=ALU.mult),
                 reads=[B], pwrites=[B])
            p.op("dve", lambda e: e.tensor_scalar(out=t, in0=t, scalar1=-0.5, scalar2=1.5, op0=ALU.mult, op1=ALU.add),
                 reads=[B], pwrites=[B])
            p.op("dve", lambda e: e.tensor_tensor(out=r, in0=r, in1=t, op=ALU.mult), reads=[B], pwrites=[B])

    def ln_tile(self, y, yB, gt, bt, gbB, st, stB, hbf, hbfB):
        p = self.p
        stats = st[:, 0:24].rearrange("p (a n) -> p a n", a=4)
        mv = st[:, 24:26]
        rstd = st[:, 26:27]
        nb = st[:, 27:28]
        for n in range(4):
            p.op("dve", lambda e, n=n: e.bn_stats(out=stats[:, n, :], in_=y[:, n * 512:(n + 1) * 512]), reads=[yB], pwrites=[stB])
        p.op("dve", lambda e: e.bn_aggr(out=mv, in_=st[:, 0:24]), reads=[stB], pwrites=[stB])
        p.op("dve", lambda e: e.tensor_scalar(out=st[:, 28:29], in0=st[:, 25:26], scalar1=LN_EPS, scalar2=None, op0=ALU.add),
             reads=[stB], pwrites=[stB])
        self.rsqrt1(st[:, 28:29], rstd, st[:, 29:30], stB)
        p.op("dve", lambda e: e.scalar_tensor_tensor(out=nb, in0=st[:, 24:25], scalar=-1.0, in1=rstd, op0=ALU.mult, op1=ALU.mult),
             reads=[stB], pwrites=[stB])
        p.op("act", lambda e: e.activation(out=y, in_=y, func=AF.Identity, bias=nb, scale=rstd), reads=[yB, stB], writes=[yB])
        p.op("dve", lambda e: e.tensor_tensor(out=y, in0=y, in1=gt, op=ALU.mult), reads=[yB, gbB], writes=[yB])
        p.op("dve", lambda e: e.tensor_tensor(out=y, in0=y, in1=bt, op=ALU.add), reads=[yB, gbB], writes=[yB])
        if hbf is not None:
            p.op("act", lambda e: e.copy(out=hbf, in_=y), reads=[yB], writes=[hbfB])

    def phase_E(self, l, h):
        p = self.p
        tok0 = h * T
        self.barrier()
        TS = 512
        Wo = self.alloc(KC * D * 2, BF16, [KC, D])
        WoB = self.buf("E_Wo")
        Ms = [self.alloc(KC * TS * 2, BF16, [KC, TS]) for _ in range(2)]
        MBs = [self.buf("E_M%d" % i) for i in range(2)]
        gt = self.alloc(D * 4, F32)
        bt = self.alloc(D * 4, F32)
        gbB = self.buf("E_gb")
        yt = [self.alloc(D * 4, F32) for _ in range(3)]
        ytB = [self.buf("E_y%d" % i) for i in range(3)]
        hb = [self.alloc(D * 2, BF16) for _ in range(2)]
        hbB = [self.buf("E_hb%d" % i) for i in range(2)]
        st = [self.alloc(32 * 4, F32) for _ in range(2)]
        stB = [self.buf("E_st%d" % i) for i in range(2)]
        HTs = self.alloc(KC * 256 * 2, BF16, [KC, 256])
        HTsB = self.buf("E_HTs")
        psf = self.psf
        psB = [self.buf("ps%d" % i) for i in range(7)]
        pr = [0]
        xsrc = self.x if l == 0 else self.Xres
        xoff = tok0 if l == 0 else 0
        xsrcB = [] if l == 0 else [self.buf("Xres")]
        for c0 in range(0, KC, 4):
            p.dma("sp", Wo[:, c0:c0 + 4, :], self.wb_out[l][:, c0:c0 + 4, :], reads=[self.wbuf("out", l, 0)], pwrites=[WoB])
        p.dma("sp", gt, self.ln1g[l:l + 1, :].broadcast_to([128, D]), pwrites=[gbB])
        p.dma("sp", bt, self.ln1b[l:l + 1, :].broadcast_to([128, D]), pwrites=[gbB])
        def load_y(k):
            if k < T // 128:
                p.dma("sp", yt[k % 3], xsrc[xoff + k * 128:xoff + (k + 1) * 128, :], reads=xsrcB, writes=[ytB[k % 3]])

        def load_m(s_):
            if s_ < T // TS:
                for c0 in range(0, KC, 8):
                    p.dma("sp", Ms[s_ % 2][:, c0:c0 + 8, :],
                          self.MT[c0 * 128:(c0 + 8) * 128, s_ * TS:(s_ + 1) * TS].rearrange("(c p) t -> p c t", p=128),
                          reads=[self.buf("MT")], pwrites=[MBs[s_ % 2]])

        load_m(0)
        NB_ = T // 128
        banks = {}

        def mm(gidx):
            sub, tb = divmod(gidx, TS // 128)
            if tb == 0:
                load_m(sub + 1)
            if gidx == 0:
                load_y(0)
            load_y(gidx + 1)
            M, MB = Ms[sub % 2], MBs[sub % 2]
            bl = []
            for n in range(4):
                bi = pr[0] % 6
                pr[0] += 1
                ps = psf[bi]
                fns = [lambda e, kc=kc, ps=ps, tb=tb, n=n, M=M: e.matmul(
                    ps[:, :], lhsT=M[:, kc, tb * 128:(tb + 1) * 128], rhs=Wo[:, kc, n * 512:(n + 1) * 512],
                    start=(kc == 0), stop=(kc == KC - 1)) for kc in range(KC)]
                p.pe(fns, reads=[MB, WoB], writes=[psB[bi]])
                bl.append(bi)
            banks[gidx] = bl

        def adds(gidx):
            y, yB = yt[gidx % 3], ytB[gidx % 3]
            for n, bi in enumerate(banks.pop(gidx)):
                ps = psf[bi]
                p.op("dve", lambda e, y=y, ps=ps, n=n: e.scalar_tensor_tensor(
                    out=y[:, n * 512:(n + 1) * 512], in0=y[:, n * 512:(n + 1) * 512], scalar=ALPHA, in1=ps[:, :],
                    op0=ALU.mult, op1=ALU.add), reads=[psB[bi]], writes=[yB])

        def tail(gidx):
            y, yB = yt[gidx % 3], ytB[gidx % 3]
            i = gidx % 2
            tl = gidx * 128
            self.ln_tile(y, yB, gt, bt, gbB, st[i], stB[i], hb[i], hbB[i])
            p.dma("sp", self.Hres[tl:tl + 128, :], y, reads=[yB], pwrites=[self.buf("Hres")])
            self.transpose_tile(hb[i], hbB[i], HTs, HTsB, (gidx % 2) * 128)
            if gidx % 2 == 1:
                t0 = tl - 128
                p.dma("sp", self.HTd.rearrange("(c p) t -> p c t", p=128)[:, :, t0:t0 + 256], HTs,
                      reads=[HTsB], pwrites=[self.buf("HTd")])

        for gidx in range(NB_):
            mm(gidx)
            if gidx > 0:
                tail(gidx - 1)
            adds(gidx)
        tail(NB_ - 1)

    def phase_F(self, l, h):
        p = self.p
        self.barrier()
        HT = self.alloc(KC * T * 2, BF16, [KC, T])
        HTB = self.buf("F_HT")
        wblk = [self.alloc(KC * 256 * 2, BF16, [KC, 256]) for _ in range(3)]
        wblkB = [self.buf("F_w%d" % i) for i in range(3)]
        U = [[self.alloc((T + 2) * 4, F32) for _ in range(2)] for _ in range(2)]
        UB = [[self.buf("F_U%d_%d" % (r, c)) for c in range(2)] for r in range(2)]
        cv = [[self.alloc(512 * 4, F32) for _ in range(2)] for _ in range(2)]
        cvB = [[self.buf("F_cv%d_%d" % (r, c)) for c in range(2)] for r in range(2)]
        sl = [self.alloc(512 * 4, F32) for _ in range(2)]
        slB = [self.buf("F_sl%d" % i) for i in range(2)]
        gs = [self.alloc(T * 2, BF16) for _ in range(2)]
        gsB = [self.buf("F_gs%d" % i) for i in range(2)]
        cw = self.alloc(88 * 3 * 4, F32, [88, 3])
        cb = self.alloc(88 * 4, F32)
        cwB = self.buf("F_cw")
        psf = self.psf
        psB = [self.buf("ps%d" % i) for i in range(7)]
        pr = [0]
        uh = self.uhalo
        uhB = self.buf("uhalo")
        for c0 in range(0, KC, 4):
            p.dma("sp", HT[:, c0:c0 + 4, :], self.HTd[c0 * 128:(c0 + 4) * 128, :].rearrange("(c p) t -> p c t", p=128),
                  reads=[self.buf("HTd")], pwrites=[HTB])
        p.dma("sp", cw, self.convw[l], pwrites=[cwB])
        p.dma("sp", cb, self.convb[l], pwrites=[cwB])
        def load_w(b):
            if b < DFF // 128:
                p.dma("sp", wblk[b % 3], self.wb_up[l][b], reads=[self.wbuf("up", l, b // 8)], writes=[wblkB[b % 3]])

        load_w(0)
        load_w(1)
        for blk in range(DFF // 128):
            wi = blk % 3
            W = wblk[wi]
            r = blk % 2
            load_w(blk + 2)
            for ch in range(2):
                p.op("act", lambda e, r=r, ch=ch, blk=blk: e.copy(out=U[r][ch][:, 0:2], in_=uh[:, l, 2 * blk + ch, :]),
                     reads=[uhB], pwrites=[UB[r][ch]])
            g_, gB_ = gs[r], gsB[r]
            for tt in range(4):
                tsl = slice(tt * 512, (tt + 1) * 512)
                k = tt % 2
                for ch in range(2):
                    bi = pr[0] % 6
                    pr[0] += 1
                    ps = psf[bi]
                    fns = [lambda e, kc=kc, ps=ps, ch=ch, W=W, tsl=tsl: e.matmul(
                        ps[:, :], lhsT=W[:, kc, ch * 128:(ch + 1) * 128], rhs=HT[:, kc, tsl],
                        start=(kc == 0), stop=(kc == KC - 1)) for kc in range(KC)]
                    p.pe(fns, reads=[wblkB[wi], HTB], writes=[psB[bi]])
                    u = U[r][ch]
                    p.op("act", lambda e, u=u, ps=ps, tt=tt: e.copy(out=u[:, 2 + tt * 512:2 + (tt + 1) * 512], in_=ps[:, :]),
                         reads=[psB[bi]], pwrites=[UB[r][ch]])
                    a = cv[k][ch]
                    ci = 2 * blk + ch
                    t0 = tt * 512
                    p.op("dve", lambda e, a=a, u=u, ci=ci, t0=t0: e.tensor_scalar(
                        out=a, in0=u[:, t0 + 2:t0 + 514], scalar1=cw[:, ci, 2:3], scalar2=cb[:, ci:ci + 1], op0=ALU.mult, op1=ALU.add),
                        reads=[UB[r][ch], cwB], writes=[cvB[k][ch]])
                    p.op("dve", lambda e, a=a, u=u, ci=ci, t0=t0: e.scalar_tensor_tensor(
                        out=a, in0=u[:, t0 + 1:t0 + 513], scalar=cw[:, ci, 1:2], in1=a, op0=ALU.mult, op1=ALU.add),
                        reads=[UB[r][ch], cwB, cvB[k][ch]], writes=[cvB[k][ch]])
                    p.op("dve", lambda e, a=a, u=u, ci=ci, t0=t0: e.scalar_tensor_tensor(
                        out=a, in0=u[:, t0:t0 + 512], scalar=cw[:, ci, 0:1], in1=a, op0=ALU.mult, op1=ALU.add),
                        reads=[UB[r][ch], cwB, cvB[k][ch]], writes=[cvB[k][ch]])
                p.op("act", lambda e, k=k: e.activation(out=sl[k], in_=cv[k][0], func=AF.Silu), reads=[cvB[k][0]], writes=[slB[k]])
                p.op("dve", lambda e, k=k, g_=g_, tsl=tsl: e.tensor_tensor(out=g_[:, tsl], in0=sl[k], in1=cv[k][1], op=ALU.mult),
                     reads=[slB[k], cvB[k][1]], pwrites=[gB_])
            p.dma("sp", self.GF[blk * 128:(blk + 1) * 128, :], g_, reads=[gB_], pwrites=[self.buf("GF")])
            if h == 0:
                for ch in range(2):
                    p.op("act", lambda e, r=r, ch=ch, blk=blk: e.copy(out=uh[:, l, 2 * blk + ch, :], in_=U[r][ch][:, T:T + 2]),
                         reads=[UB[r][ch]], pwrites=[uhB])

    def phase_G(self, l, h):
        p = self.p
        tok0 = h * T
        psf = self.psf
        psB = [self.buf("ps%d" % i) for i in range(7)]
        last = (l == self.nlayers - 1)
        for ps_i, (ka, kb_) in enumerate(((0, 24), (24, 44))):
            nkc = kb_ - ka
            self.barrier()
            Wd = self.alloc(nkc * D * 2, BF16, [nkc, D])
            WdB = self.buf("G_Wd%d" % ps_i)
            gT = [self.alloc(nkc * 256 * 2, BF16, [nkc, 256]) for _ in range(2)]
            gTB = [self.buf("G_gT%d_%d" % (ps_i, i)) for i in range(2)]
            yt = [self.alloc(D * 4, F32) for _ in range(3)]
            ytB = [self.buf("G_y%d_%d" % (ps_i, i)) for i in range(3)]
            if ps_i == 1:
                gt = self.alloc(D * 4, F32)
                bt = self.alloc(D * 4, F32)
                gbB = self.buf("G_gb")
                hb = [self.alloc(D * 2, BF16) for _ in range(2)]
                hbB = [self.buf("G_hb%d" % i) for i in range(2)]
                st = [self.alloc(32 * 4, F32) for _ in range(2)]
                stB = [self.buf("G_st%d" % i) for i in range(2)]
                XTs = self.alloc(KC * 256 * 2, BF16, [KC, 256])
                XTsB = self.buf("G_XTs")
                p.dma("sp", gt, self.ln2g[l:l + 1, :].broadcast_to([128, D]), pwrites=[gbB])
                p.dma("sp", bt, self.ln2b[l:l + 1, :].broadcast_to([128, D]), pwrites=[gbB])
            for c0 in range(0, nkc, 4):
                p.dma("sp", Wd[:, c0:c0 + 4, :], self.wb_down[l][:, ka + c0:ka + c0 + 4, :],
                      reads=[self.wbuf("down", l, (ka + c0) // 4)], pwrites=[WdB])
            pr = [0]

            def load_g(k, gT=gT, gTB=gTB, ka=ka, kb_=kb_):
                if k < T // 256:
                    p.dma("sp", gT[k % 2], self.GF[ka * 128:kb_ * 128, k * 256:(k + 1) * 256].rearrange("(c p) t -> p c t", p=128),
                          reads=[self.buf("GF")], writes=[gTB[k % 2]])

            def load_y(k, yt=yt, ytB=ytB, ps_i=ps_i):
                if k < T // 128:
                    if ps_i == 0:
                        p.dma("sp", yt[k % 3], self.Hres[k * 128:(k + 1) * 128, :], reads=[self.buf("Hres")], writes=[ytB[k % 3]])
                    else:
                        p.dma("sp", yt[k % 3], self.Y1[k * 128:(k + 1) * 128, :], reads=[self.buf("Y1")], writes=[ytB[k % 3]])

            load_g(0)
            load_y(0)
            banks = {}
            NB_ = T // 128

            def mm(tb, gT=gT, gTB=gTB, Wd=Wd, WdB=WdB, nkc=nkc, banks=banks, load_g=load_g, load_y=load_y, pr=pr):
                gi = (tb // 2) % 2
                if tb % 2 == 0:
                    load_g(tb // 2 + 1)
                load_y(tb + 1)
                bl = []
                for n in range(4):
                    bi = pr[0] % 6
                    pr[0] += 1
                    ps = psf[bi]
                    fns = [lambda e, kc=kc, ps=ps, tb=tb, n=n, g_=gT[gi], Wd=Wd, nkc=nkc: e.matmul(
                        ps[:, :], lhsT=g_[:, kc, (tb % 2) * 128:(tb % 2) * 128 + 128], rhs=Wd[:, kc, n * 512:(n + 1) * 512],
                        start=(kc == 0), stop=(kc == nkc - 1)) for kc in range(nkc)]
                    p.pe(fns, reads=[gTB[gi], WdB], writes=[psB[bi]])
                    bl.append(bi)
                banks[tb] = bl

            def adds(tb, yt=yt, ytB=ytB, banks=banks, ps_i=ps_i):
                y, yB = yt[tb % 3], ytB[tb % 3]
                sc = ALPHA if ps_i == 0 else 1.0
                for n, bi in enumerate(banks.pop(tb)):
                    ps = psf[bi]
                    p.op("dve", lambda e, y=y, ps=ps, n=n, sc=sc: e.scalar_tensor_tensor(
                        out=y[:, n * 512:(n + 1) * 512], in0=y[:, n * 512:(n + 1) * 512], scalar=sc, in1=ps[:, :],
                        op0=ALU.mult, op1=ALU.add), reads=[psB[bi]], writes=[yB])

            if ps_i == 0:
                def tail(tb, yt=yt, ytB=ytB):
                    y, yB = yt[tb % 3], ytB[tb % 3]
                    tl = tb * 128
                    p.dma("sp", self.Y1[tl:tl + 128, :], y, reads=[yB], pwrites=[self.buf("Y1")])
            else:
                def tail(tb, yt=yt, ytB=ytB, gt=gt, bt=bt, gbB=gbB, st=st, stB=stB, hb=hb, hbB=hbB, XTs=XTs, XTsB=XTsB):
                    y, yB = yt[tb % 3], ytB[tb % 3]
                    i = tb % 2
                    tl = tb * 128
                    self.ln_tile(y, yB, gt, bt, gbB, st[i], stB[i], None if last else hb[i], None if last else hbB[i])
                    if last:
                        p.dma("sp", self.y[tok0 + tl:tok0 + tl + 128, :], y, reads=[yB], pwrites=[self.buf("y_out")])
                    else:
                        p.dma("sp", self.Xres[tl:tl + 128, :], y, reads=[yB], pwrites=[self.buf("Xres")])
                        self.transpose_tile(hb[i], hbB[i], XTs, XTsB, (tb % 2) * 128)
                        if tb % 2 == 1:
                            t0 = tl - 128
                            p.dma("sp", self.XTd.rearrange("(c p) t -> p c t", p=128)[:, :, t0:t0 + 256], XTs,
                                  reads=[XTsB], pwrites=[self.buf("XTd")])

            for tb in range(NB_):
                mm(tb)
                if tb > 0:
                    tail(tb - 1)
                adds(tb)
            tail(NB_ - 1)

    def load_consts(self):
        p = self.p
        self.persist_end = 0
        self.aoff = 0
        self.ident_sb = self.alloc(128 * 2, BF16)
        self.masks_sb = self.alloc(4 * 128 * 2, BF16, [4, 128])
        p.dma("sp", self.ident_sb, self.ident, writes=[self.buf("ident")])
        p.dma("sp", self.masks_sb, self.masks, writes=[self.buf("masks")])
        self.maskrep = self.alloc(2 * 512 * 2, BF16, [2, 512])
        for kb, mi in ((0, 1), (1, 0)):
            for hh in range(4):
                p.op("dve", lambda e, kb=kb, mi=mi, hh=hh: e.tensor_copy(
                    out=self.maskrep[:, kb, hh * 128:(hh + 1) * 128], in_=self.masks_sb[:, mi, :]),
                    reads=[self.buf("masks")], pwrites=[self.buf("maskrep")])
        self.maskpair = self.alloc(2 * 256 * 2, BF16, [2, 256])
        for (t_, half, mi) in ((0, 0, 0), (0, 1, 3), (1, 0, 2), (1, 1, 0)):
            p.op("dve", lambda e, t_=t_, half=half, mi=mi: e.tensor_copy(
                out=self.maskpair[:, t_, half * 128:(half + 1) * 128], in_=self.masks_sb[:, mi, :]),
                reads=[self.buf("masks")], pwrites=[self.buf("maskpair")])
        self.uhalo = self.alloc(DEPTH * 88 * 2 * 4, F32, [DEPTH, 88, 2])
        p.op("dve", lambda e: e.memset(self.uhalo, 0.0), writes=[self.buf("uhalo")])
        self.persist_end = self.aoff

    def emit(self):
        self.conv_setup()
        self.load_consts()
        final = []
        for l in range(self.nlayers):
            self.convert_layer(l)
        for h in range(self.nhalves):
            for l in range(self.nlayers):
                for nm in "ABCDEFG":
                    getattr(self, "phase_" + nm)(l, h)
                    if self.stop == nm:
                        break
        allb = [b for b in self.bufs.values()]
        self.p.finish(allb)


_CACHE = {}


def _host_consts():
    if "c" in _CACHE:
        return _CACHE["c"]
    c = dict(_rope_tables())
    k = np.arange(128)[:, None]
    q = np.arange(128)[None, :]
    m = np.zeros((128, 4, 128), dtype=np.float32)
    m[:, 0, :] = np.where(k <= q, 0.0, NEG)
    m[:, 1, :] = np.where(k > q, 0.0, NEG)
    m[:, 2, :] = NEG
    c["masks"] = m.astype(ml_dtypes.bfloat16)
    c["ident"] = np.eye(128, dtype=np.float32).astype(ml_dtypes.bfloat16)
    _CACHE["c"] = c
    return c


def prepare_shared(inp):
    f = lambda a: np.ascontiguousarray(np.asarray(a, dtype=np.float32))
    sh = {}
    sh["w_in"] = np.ascontiguousarray(f(inp["w_in"])[:, :, _win_perm()])
    up = _wup_perm()
    sh["w_up"] = np.ascontiguousarray(f(inp["w_up"])[:, :, up])
    cw = f(inp["conv_w"])[:, :, up]
    sh["conv_w"] = np.ascontiguousarray(cw.reshape(DEPTH, 3, 88, 128).transpose(0, 3, 2, 1))
    cb = f(inp["conv_b"])[:, up]
    sh["conv_b"] = np.ascontiguousarray(cb.reshape(DEPTH, 88, 128).transpose(0, 2, 1))
    for k in ("w_proj_a", "w_proj_b", "w_out", "w_down", "sinks", "subln_g", "ln1_g", "ln1_b", "ln2_g", "ln2_b"):
        sh[k] = f(inp[k])
    sh["lam"] = np.ascontiguousarray(np.stack([f(inp["lambda_q1"]), f(inp["lambda_k1"]),
                                               f(inp["lambda_q2"]), f(inp["lambda_k2"])], axis=1))
    sh.update(_host_consts())
    return sh


def kernel(**inputs):
    x = np.asarray(inputs["x"], dtype=np.float32)
    sh = prepare_shared(inputs)
    kern = Kern()
    nc = kern.build()
    in_maps = []
    for c in range(NCORES):
        m = dict(sh)
        m["x"] = np.ascontiguousarray(x[c % BATCH])
        in_maps.append(m)
    res = run_bass_kernel_spmd(nc, in_maps, core_ids=list(range(NCORES)))
    out = np.stack([np.asarray(res.results[b]["y"], dtype=np.float32) for b in range(BATCH)], axis=0)
    return out
```

```python
import math
from contextlib import ExitStack

import numpy as np
import ml_dtypes

import concourse.bass as bass
import concourse.mybir as mybir
from concourse.bass_utils import run_bass_kernel_spmd

F32 = mybir.dt.float32
BF16 = mybir.dt.bfloat16
AF = mybir.ActivationFunctionType
ALU = mybir.AluOpType

D = 2048
SEQ = 4096
BATCH = 4
DEPTH = 2
HA, NQA, NKVA, GA = 64, 32, 4, 8
HB, NHB = 128, 8
DFF = 5632
INW = 12800
LN_EPS = 1e-5
SUB_EPS = 1e-5
ALPHA = (2 * DEPTH) ** 0.25
THETA = 10000.0
T = 2048
NH = SEQ // T
KC = D // 128
NEG = -30000.0

ENGS = ("pe", "act", "dve", "pool", "sp")
NCORES = 4


class Buf:
    __slots__ = ("name", "w", "r", "g")

    def __init__(self, name):
        self.name = name
        self.w = {}
        self.r = {}
        self.g = {}


def _merge(dst, src):
    for s, v in src.items():
        if dst.get(s, 0) < v:
            dst[s] = v


class Prog:
    DRING = 8

    def __init__(self, nc, stack):
        self.nc = nc
        self.st = stack
        self.q = {e: [] for e in ENGS}
        self.semh = {}
        self.cnt = {}
        self.seen = {e: {} for e in ENGS}
        for e in ("pe", "act", "dve", "pool"):
            self.newsem("P_" + e)
        self.dring = {}
        self.dpos = {}
        for e in ("sp", "act", "pool"):
            self.dring[e] = [self.newsem("D_%s_%d" % (e, i)) for i in range(self.DRING)]
            self.dpos[e] = 0
        self.nops = 0

    def newsem(self, name):
        self.semh[name] = self.st.enter_context(self.nc.semaphore(name))
        self.cnt[name] = 0
        return name

    def _need(self, eng, reads, writes, pwrites):
        need = {}
        for b in reads:
            _merge(need, b.w)
        for b in writes:
            _merge(need, b.g)
            _merge(need, b.w)
            _merge(need, b.r)
        for b in pwrites:
            if b.r:
                _merge(b.g, b.w)
                _merge(b.g, b.r)
                b.w = {}
                b.r = {}
            _merge(need, b.g)
        for s, v in need.items():
            if eng == "pe" and s == "P_pe":
                continue
            if self.seen[eng].get(s, 0) >= v:
                continue
            self.seen[eng][s] = v
            self.q[eng].append(("w", s, v))

    def _post(self, s, v, reads, writes, pwrites):
        for b in reads:
            if b.r.get(s, 0) < v:
                b.r[s] = v
        for b in writes:
            b.g = {}
            b.r = {}
            b.w = {s: v}
        for b in pwrites:
            if b.w.get(s, 0) < v:
                b.w[s] = v

    def op(self, eng, fn, reads=(), writes=(), pwrites=()):
        self._need(eng, reads, writes, pwrites)
        s = "P_" + eng
        self.cnt[s] += 1
        v = self.cnt[s]
        self.q[eng].append(("op", fn, s, 1))
        self._post(s, v, reads, writes, pwrites)
        self.nops += 1

    def pe(self, fns, reads=(), writes=(), pwrites=()):
        self._need("pe", reads, writes, pwrites)
        s = "P_pe"
        self.cnt[s] += 1
        v = self.cnt[s]
        for f in fns[:-1]:
            self.q["pe"].append(("op", f, None, 0))
        self.q["pe"].append(("op", fns[-1], s, 1))
        self._post(s, v, reads, writes, pwrites)
        self.nops += len(fns)

    def dma(self, eng, out, in_, reads=(), writes=(), pwrites=(), **kw):
        self._need(eng, reads, writes, pwrites)
        ring = self.dring[eng]
        s = ring[self.dpos[eng] % len(ring)]
        self.dpos[eng] += 1
        prev = self.cnt[s]
        if prev and self.seen[eng].get(s, 0) < prev:
            self.seen[eng][s] = prev
            self.q[eng].append(("w", s, prev))
        self.cnt[s] += 16
        v = self.cnt[s]
        self.q[eng].append(("op", lambda e, o=out, i=in_, k=kw: e.dma_start(out=o, in_=i, **k), s, 16))
        self._post(s, v, reads, writes, pwrites)
        self.nops += 1

    def finish(self, final_bufs):
        need = {}
        for b in final_bufs:
            _merge(need, b.w)
            _merge(need, b.g)
        for s, v in need.items():
            self.q["sp"].append(("w", s, v))

    def replay(self):
        nc = self.nc
        semh = self.semh

        def run(items, e):
            for it in items:
                if it[0] == "w":
                    e.wait_ge(semh[it[1]], it[2])
                else:
                    ins = it[1](e)
                    if it[2] is not None:
                        ins.then_inc(semh[it[2]], it[3])

        with nc.Block() as block:
            @block.sync
            def _(e):
                run(self.q["sp"], e)

            @block.tensor
            def _(e):
                run(self.q["pe"], e)

            @block.scalar
            def _(e):
                run(self.q["act"], e)

            @block.vector
            def _(e):
                run(self.q["dve"], e)

            @block.gpsimd
            def _(e):
                run(self.q["pool"], e)


def _win_perm():
    cols = []
    for i in range(8):
        for half in range(2):
            for hl in range(4):
                for d in range(32):
                    cols.append((4 * i + hl) * 64 + half * 32 + d)
    for half in range(2):
        for g in range(4):
            for d in range(32):
                cols.append(2048 + g * 64 + half * 32 + d)
    for base in (2560, 4608):
        for i in range(8):
            for half in range(2):
                for c in range(2):
                    for d in range(64):
                        cols.append(base + i * 256 + c * 128 + half * 64 + d)
    cols.extend(range(2304, 2560))
    cols.extend(range(6656, 8704))
    cols.extend(range(8704, 12800))
    assert len(cols) == INW and len(set(cols)) == INW
    return np.asarray(cols, dtype=np.int64)


def _wup_perm():
    cols = []
    for c in range(DFF // 128):
        cols.extend(range(c * 128, (c + 1) * 128))
        cols.extend(range(DFF + c * 128, DFF + (c + 1) * 128))
    return np.asarray(cols, dtype=np.int64)


def _rope_tables():
    out = {}
    t = np.arange(SEQ, dtype=np.float32)[None, :]
    for name, dim in (("A", HA), ("B", HB)):
        inv = (1.0 / (np.float32(THETA) ** (np.arange(0, dim, 2, dtype=np.float32) / np.float32(dim)))).astype(np.float32)
        p = np.arange(128) % (dim // 2)
        ang = (inv[p][:, None] * t).astype(np.float32)
        out["cos" + name] = np.cos(ang).astype(np.float32)
        out["sin" + name] = np.sin(ang).astype(np.float32)
    return out


BLK_QA0, BLK_KA, BLK_QB0, BLK_KB0, BLK_VA, BLK_VB0, BLK_G0 = 0, 8, 9, 17, 25, 26, 34
NBLK_IN = 50


class Kern:
    def __init__(self, debug=None, nlayers=DEPTH, nhalves=NH, stop=None):
        self.debug = debug or ()
        self.nlayers = nlayers
        self.nhalves = nhalves
        self.stop = stop
        self.nc = bass.Bass("TRN2", target_bir_lowering=False)
        self.bufs = {}

    def buf(self, name):
        b = self.bufs.get(name)
        if b is None:
            b = self.bufs[name] = Buf(name)
        return b

    def dram(self, name, shape, dt, kind="Internal"):
        if name in self.debug:
            kind = "ExternalOutput"
        return self.nc.dram_tensor(name, list(shape), dt, kind=kind).ap()

    def phase_begin(self):
        self.aoff = self.persist_end

    def alloc(self, nbytes, dt, shape=None, parts=128):
        nb0 = nbytes
        nbytes = (nbytes + 63) // 64 * 64
        off = self.aoff
        self.aoff += nbytes
        assert self.aoff <= self.ARENA, ("arena overflow", self.aoff)
        v = self.arena[0:parts, off // 4:(off + nbytes) // 4]
        if dt != F32:
            v = v.bitcast(dt)
            v = v[:, 0:nb0 // 2]
        else:
            v = v[:, 0:nb0 // 4]
        if shape is not None:
            names = " ".join("d%d" % i for i in range(len(shape)))
            kw = {"d%d" % i: int(s) for i, s in enumerate(shape)}
            v = v.rearrange("p (%s) -> p %s" % (names, names), **kw)
        return v

    def build(self):
        nc = self.nc
        with ExitStack() as st:
            self.st = st
            self.p = Prog(nc, st)
            self.ARENA = 168 * 1024
            self.arena = st.enter_context(nc.sbuf_tensor("arena", [128, self.ARENA // 4], F32))
            self.persist_end = 0
            self.aoff = 0
            self.tp_i = 0
            self.tp_two = True
            self.psf = [st.enter_context(nc.psum_tensor("psf%d" % i, [128, 512], F32)) for i in range(7)]
            self.psb = st.enter_context(nc.psum_tensor("psb", [128, 1024], BF16))
            self.psf6b = self.psf[6][:, :].bitcast(BF16)
            self.declare()
            self.emit()
            self.p.replay()
        return nc

    def declare(self):
        nc = self.nc
        ext = lambda n, s, dt=F32: nc.dram_tensor(n, list(s), dt, kind="ExternalInput").ap()
        self.x = ext("x", [SEQ, D])
        self.w_in = ext("w_in", [DEPTH, D, INW])
        self.w_pa = ext("w_proj_a", [DEPTH, D, D])
        self.w_pb = ext("w_proj_b", [DEPTH, D, D])
        self.w_out = ext("w_out", [DEPTH, D, D])
        self.w_up = ext("w_up", [DEPTH, D, 2 * DFF])
        self.w_down = ext("w_down", [DEPTH, DFF, D])
        self.sinks = ext("sinks", [DEPTH, NQA])
        self.lam = ext("lam", [DEPTH, 4, HB])
        self.subg = ext("subln_g", [DEPTH, 2 * HB])
        self.ln1g = ext("ln1_g", [DEPTH, D])
        self.ln1b = ext("ln1_b", [DEPTH, D])
        self.ln2g = ext("ln2_g", [DEPTH, D])
        self.ln2b = ext("ln2_b", [DEPTH, D])
        self.convw = ext("conv_w", [DEPTH, 128, 88, 3])
        self.convb = ext("conv_b", [DEPTH, 128, 88])
        self.cosA = ext("cosA", [128, SEQ])
        self.sinA = ext("sinA", [128, SEQ])
        self.cosB = ext("cosB", [128, SEQ])
        self.sinB = ext("sinB", [128, SEQ])
        self.masks = ext("masks", [128, 4, 128], BF16)
        self.ident = ext("ident", [128, 128], BF16)
        self.y = nc.dram_tensor("y", [SEQ, D], F32, kind="ExternalOutput").ap()

        self.wb_in = [self.dram("wb_in%d" % l, [NBLK_IN, 128, KC, 256], BF16) for l in range(DEPTH)]
        self.wb_pa = [self.dram("wb_pa%d" % l, [16, 128, KC, 128], BF16) for l in range(DEPTH)]
        self.wb_pb = [self.dram("wb_pb%d" % l, [16, 128, KC, 128], BF16) for l in range(DEPTH)]
        self.wb_out = [self.dram("wb_out%d" % l, [128, KC, D], BF16) for l in range(DEPTH)]
        self.wb_up = [self.dram("wb_up%d" % l, [44, 128, KC, 256], BF16) for l in range(DEPTH)]
        self.wb_down = [self.dram("wb_down%d" % l, [128, DFF // 128, D], BF16) for l in range(DEPTH)]
        self.QTa = self.dram("QTa", [NQA, HA, T], BF16)
        self.QTb = self.dram("QTb", [2 * NHB, HB, T], BF16)
        self.KTa = [self.dram("KTa%d" % l, [NKVA, HA, SEQ], BF16) for l in range(DEPTH)]
        self.KTb = [self.dram("KTb%d" % l, [2 * NHB, HB, SEQ], BF16) for l in range(DEPTH)]
        self.Va = [self.dram("Va%d" % l, [SEQ, NKVA * HA], BF16) for l in range(DEPTH)]
        self.Vb = [self.dram("Vb%d" % l, [SEQ, NHB * 2 * HB], BF16) for l in range(DEPTH)]
        self.GT = self.dram("GT", [2 * D, T], BF16)
        self.OTa = self.dram("OTa", [D, T], BF16)
        self.OTb = self.dram("OTb", [D, T], BF16)
        self.MT = self.dram("MT", [D, T], BF16)
        self.Hres = self.dram("Hres", [T, D], F32)
        self.GF = self.dram("GF", [DFF, T], BF16)
        self.Y1 = self.dram("Y1", [T, D], F32)
        self.Xres = self.dram("Xres", [T, D], F32)
        self.XTd = self.dram("XTd", [D, T], BF16)
        self.HTd = self.dram("HTd", [D, T], BF16)

    def barrier(self):
        p = self.p
        for eng in ("pe", "act", "dve", "sp"):
            for s, v in p.cnt.items():
                if s == "P_pool" or s.startswith("D_pool"):
                    continue
                if eng == "pe" and s == "P_pe":
                    continue
                if v and p.seen[eng].get(s, 0) < v:
                    p.seen[eng][s] = v
                    p.q[eng].append(("w", s, v))
        self.phase_begin()

    def wbuf(self, name, l, i):
        return self.buf("W_%s_%d_%d" % (name, l, i))

    def conv_setup(self):
        self.cv_f = [self.st.enter_context(self.nc.sbuf_tensor("cvf%d" % i, [128, 2048], F32)) for i in range(2)]
        self.cv_b = [self.st.enter_context(self.nc.sbuf_tensor("cvb%d" % i, [128, 2048], BF16)) for i in range(2)]
        self.cv_i = 0

    def conv_piece(self, src, dst, dst_shape3, wb):
        p = self.p
        i = self.cv_i % 2
        self.cv_i += 1
        n = src.shape[-1]
        f = self.cv_f[i][:, 0:n]
        b = self.cv_b[i][:, 0:n]
        bf_, bb_ = self.buf("cvf%d" % i), self.buf("cvb%d" % i)
        p.dma("pool", f, src, writes=[bf_])
        p.op("pool", lambda e, o=b, a=f: e.tensor_copy(out=o, in_=a), reads=[bf_], writes=[bb_])
        bsrc = b
        if dst_shape3 is not None:
            bsrc = b.rearrange("p (a n) -> p a n", a=dst_shape3)
        p.dma("pool", dst, bsrc, reads=[bb_], pwrites=[wb])

    def convert_layer(self, l):
        wsrc = self.w_in[l]
        for cg in range(7):
            c0 = cg * 2048
            n = min(2048, INW - c0)
            nb = n // 256
            for kc in range(KC):
                self.conv_piece(wsrc[kc * 128:(kc + 1) * 128, c0:c0 + n],
                                self.wb_in[l][8 * cg:8 * cg + nb, :, kc, :].rearrange("b p n -> p b n"),
                                nb, self.wbuf("in", l, cg))
        for nm, wsrc, wdst in (("pa", self.w_pa[l], self.wb_pa[l]), ("pb", self.w_pb[l], self.wb_pb[l])):
            for kc in range(KC):
                self.conv_piece(wsrc[kc * 128:(kc + 1) * 128, :],
                                wdst[:, :, kc, :].rearrange("c p n -> p c n"), 16, self.wbuf(nm, l, 0))
        for kc in range(KC):
            self.conv_piece(self.w_out[l][kc * 128:(kc + 1) * 128, :], self.wb_out[l][:, kc, :], None,
                            self.wbuf("out", l, 0))
        for cg in range(6):
            c0 = cg * 2048
            n = min(2048, 2 * DFF - c0)
            nb = n // 256
            for kc in range(KC):
                self.conv_piece(self.w_up[l][kc * 128:(kc + 1) * 128, c0:c0 + n],
                                self.wb_up[l][8 * cg:8 * cg + nb, :, kc, :].rearrange("b p n -> p b n"),
                                nb, self.wbuf("up", l, cg))
        for kc in range(DFF // 128):
            self.conv_piece(self.w_down[l][kc * 128:(kc + 1) * 128, :], self.wb_down[l][:, kc, :], None,
                            self.wbuf("down", l, kc // 4))

    def transpose_tile(self, src_bf, src_buf, dst3, dst_buf, tok_off, nchunks=16, pw=True):
        p = self.p
        ident, identB = self.ident_sb, self.buf("ident")
        for g0 in range(0, nchunks, 8):
            ng = min(8, nchunks - g0)
            if self.tp_two and self.tp_i % 2 == 1:
                psb, psbB = self.psf6b, self.buf("ps6")
            else:
                psb, psbB = self.psb, self.buf("psb")
            self.tp_i += 1
            fns = []
            for j in range(ng):
                c = g0 + j
                fns.append(lambda e, j=j, c=c, psb=psb: e.transpose(out=psb[:, j * 128:(j + 1) * 128],
                                                           in_=src_bf[:, c * 128:(c + 1) * 128], identity=ident))
            p.pe(fns, reads=[src_buf, identB], writes=[psbB])
            src = psb[:, 0:ng * 128].rearrange("p (c t) -> p c t", c=ng)
            dst = dst3[:, g0:g0 + ng, tok_off:tok_off + 128]
            kw = dict(reads=[psbB], pwrites=[dst_buf]) if pw else dict(reads=[psbB], writes=[dst_buf])
            eng = "act" if (g0 // 8) % 2 == 0 else "dve"
            if eng == "act":
                p.op("act", lambda e, o=dst, a=src: e.copy(out=o, in_=a), **kw)
            else:
                p.op("dve", lambda e, o=dst, a=src: e.tensor_copy(out=o, in_=a), **kw)

    def phase_A(self, l, h):
        p = self.p
        tok0 = h * T
        self.barrier()
        AT = self.alloc(KC * T * 2, BF16, [KC, T])
        ATb = self.buf("A_AT")
        wblk = [self.alloc(KC * 256 * 2, BF16, [KC, 256]) for _ in range(3)]
        wblkB = [self.buf("A_w%d" % i) for i in range(3)]
        tabs = {}
        for nm in ("cosA", "sinA", "cosB", "sinB"):
            tabs[nm] = self.alloc(T * 4, F32)
        tabB = self.buf("A_tabs")
        tmp = [[self.alloc(512 * 4, F32) for _ in range(4)] for _ in range(2)]
        tmpB = [[self.buf("A_tmp%d_%d" % (s, i)) for i in range(4)] for s in range(2)]
        stg = [self.alloc(T * 2, BF16) for _ in range(4)]
        stgB = [self.buf("A_stg%d" % i) for i in range(4)]
        nstg = [0]

        def next_stg():
            i = nstg[0] % 4
            nstg[0] += 1
            return stg[i], stgB[i]

        if l == 0:
            xf = [tabs["cosA"], tabs["sinA"]]
            xfB = [self.buf("A_xf%d" % i) for i in range(2)]
            xbv = tabs["cosB"].bitcast(BF16)
            xb = [xbv[:, 0:D], xbv[:, D:2 * D]]
            xbB = [self.buf("A_xb%d" % i) for i in range(2)]
            for tb in range(T // 128):
                i = tb % 2
                p.dma("sp", xf[i], self.x[tok0 + tb * 128:tok0 + (tb + 1) * 128, :], writes=[xfB[i]])
                if tb % 2 == 0:
                    p.op("dve", lambda e, o=xb[i], a=xf[i]: e.tensor_copy(out=o, in_=a), reads=[xfB[i]], writes=[xbB[i]])
                else:
                    p.op("act", lambda e, o=xb[i], a=xf[i]: e.copy(out=o, in_=a), reads=[xfB[i]], writes=[xbB[i]])
                self.transpose_tile(xb[i], xbB[i], AT, ATb, tb * 128)
        else:
            src = self.XTd.rearrange("(c p) t -> p c t", p=128)
            for c0 in range(0, KC, 4):
                p.dma("sp", AT[:, c0:c0 + 4, :], src[:, c0:c0 + 4, :], reads=[self.buf("XTd")], pwrites=[ATb])

        ov = {"cosA": ["A_xf0"], "sinA": ["A_xf1"], "cosB": ["A_xb0", "A_xb1"], "sinB": []}
        for nm in ("cosA", "sinA", "cosB", "sinB"):
            p.dma("sp", tabs[nm], getattr(self, nm)[:, tok0:tok0 + T], writes=[self.buf(n) for n in ov[nm]], pwrites=[tabB])

        psf = self.psf
        psB = [self.buf("ps%d" % i) for i in range(7)]
        pscur = [0]

        def next_ps():
            i = pscur[0] % 4
            pscur[0] += 1
            return psf[i], psB[i]

        wsrc = self.wb_in[l]

        def load_w(b):
            if b < NBLK_IN:
                p.dma("sp", wblk[b % 3], wsrc[b], reads=[self.wbuf("in", l, b // 8)], writes=[wblkB[b % 3]])

        load_w(0)
        load_w(1)
        for blk in range(NBLK_IN):
            wi = blk % 3
            W = wblk[wi]
            load_w(blk + 2)
            if blk < BLK_VA:
                if blk < BLK_KA:
                    cosT, sinT = tabs["cosA"], tabs["sinA"]
                elif blk == BLK_KA:
                    cosT, sinT = tabs["cosA"], tabs["sinA"]
                else:
                    cosT, sinT = tabs["cosB"], tabs["sinB"]
                s1, s1B = next_stg()
                s2, s2B = next_stg()
                for tt in range(4):
                    tsl = slice(tt * 512, (tt + 1) * 512)
                    pp = []
                    for ch in range(2):
                        ps, psb_ = next_ps()
                        fns = [lambda e, kc=kc, ps=ps, ch=ch, W=W, tsl=tsl: e.matmul(
                            ps[:, :], lhsT=W[:, kc, ch * 128:(ch + 1) * 128], rhs=AT[:, kc, tsl],
                            start=(kc == 0), stop=(kc == KC - 1)) for kc in range(KC)]
                        p.pe(fns, reads=[wblkB[wi], ATb], writes=[psb_])
                        pp.append((ps, psb_))
                    (P1, P1B), (P2, P2B) = pp
                    tm, tmB = tmp[tt % 2], tmpB[tt % 2]
                    TT = lambda o, a, b, op: (lambda e: e.tensor_tensor(out=o, in0=a, in1=b, op=op))
                    p.op("dve", TT(tm[0], P1[:, :], cosT[:, tsl], ALU.mult), reads=[P1B, tabB], writes=[tmB[0]])
                    p.op("dve", TT(tm[3], P1[:, :], sinT[:, tsl], ALU.mult), reads=[P1B, tabB], writes=[tmB[3]])
                    p.op("dve", TT(tm[1], P2[:, :], sinT[:, tsl], ALU.mult), reads=[P2B, tabB], writes=[tmB[1]])
                    p.op("dve", TT(tm[2], P2[:, :], cosT[:, tsl], ALU.mult), reads=[P2B, tabB], writes=[tmB[2]])
                    p.op("dve", TT(s1[:, tsl], tm[0], tm[1], ALU.subtract), reads=[tmB[0], tmB[1]], pwrites=[s1B])
                    p.op("dve", TT(s2[:, tsl], tm[2], tm[3], ALU.add), reads=[tmB[2], tmB[3]], pwrites=[s2B])
                if blk < BLK_KA:
                    i = blk
                    for hl in range(4):
                        p.dma("sp", self.QTa[4 * i + hl, 0:32, :], s1[32 * hl:32 * hl + 32, :], reads=[s1B], pwrites=[self.buf("QTa")])
                        p.dma("sp", self.QTa[4 * i + hl, 32:64, :], s2[32 * hl:32 * hl + 32, :], reads=[s2B], pwrites=[self.buf("QTa")])
                elif blk == BLK_KA:
                    for g in range(4):
                        p.dma("sp", self.KTa[l][g, 0:32, tok0:tok0 + T], s1[32 * g:32 * g + 32, :], reads=[s1B], pwrites=[self.buf("KTa%d" % l)])
                        p.dma("sp", self.KTa[l][g, 32:64, tok0:tok0 + T], s2[32 * g:32 * g + 32, :], reads=[s2B], pwrites=[self.buf("KTa%d" % l)])
                else:
                    isq = blk < BLK_KB0
                    i = blk - (BLK_QB0 if isq else BLK_KB0)
                    for c in range(2):
                        if isq:
                            d1, d2, db = self.QTb[2 * i + c, 0:64, :], self.QTb[2 * i + c, 64:128, :], self.buf("QTb")
                        else:
                            d1 = self.KTb[l][2 * i + c, 0:64, tok0:tok0 + T]
                            d2 = self.KTb[l][2 * i + c, 64:128, tok0:tok0 + T]
                            db = self.buf("KTb%d" % l)
                        p.dma("sp", d1, s1[64 * c:64 * c + 64, :], reads=[s1B], pwrites=[db])
                        p.dma("sp", d2, s2[64 * c:64 * c + 64, :], reads=[s2B], pwrites=[db])
            elif blk < BLK_G0:
                if blk == BLK_VA:
                    vdst, vb_, c0 = self.Va[l], self.buf("Va%d" % l), 0
                else:
                    vdst, vb_, c0 = self.Vb[l], self.buf("Vb%d" % l), (blk - BLK_VB0) * 256
                for hh in range(2):
                    s, sB = next_stg()
                    s3 = s.rearrange("p (a n) -> p a n", a=8)
                    for j in range(8):
                        tb = hh * 8 + j
                        ps, psb_ = next_ps()
                        fns = [lambda e, kc=kc, ps=ps, tb=tb, W=W: e.matmul(
                            ps[:, 0:256], lhsT=AT[:, kc, tb * 128:(tb + 1) * 128], rhs=W[:, kc, :],
                            start=(kc == 0), stop=(kc == KC - 1)) for kc in range(KC)]
                        p.pe(fns, reads=[wblkB[wi], ATb], writes=[psb_])
                        p.op("act", lambda e, o=s3[:, j, :], a=ps[:, 0:256]: e.copy(out=o, in_=a), reads=[psb_], pwrites=[sB])
                    t0 = tok0 + hh * 1024
                    p.dma("sp", vdst[t0:t0 + 1024, c0:c0 + 256].rearrange("(a p) n -> p a n", p=128), s3,
                          reads=[sB], pwrites=[vb_])
            else:
                for ch in range(2):
                    s, sB = next_stg()
                    for tt in range(4):
                        tsl = slice(tt * 512, (tt + 1) * 512)
                        ps, psb_ = next_ps()
                        fns = [lambda e, kc=kc, ps=ps, ch=ch, W=W, tsl=tsl: e.matmul(
                            ps[:, :], lhsT=W[:, kc, ch * 128:(ch + 1) * 128], rhs=AT[:, kc, tsl],
                            start=(kc == 0), stop=(kc == KC - 1)) for kc in range(KC)]
                        p.pe(fns, reads=[wblkB[wi], ATb], writes=[psb_])
                        p.op("act", lambda e, o=s[:, tsl], a=ps[:, :]: e.activation(out=o, in_=a, func=AF.Sigmoid),
                             reads=[psb_], pwrites=[sB])
                    r0 = (blk - BLK_G0) * 256 + ch * 128
                    p.dma("sp", self.GT[r0:r0 + 128, :], s, reads=[sB], pwrites=[self.buf("GT")])

    def phase_B(self, l, h):
        p = self.p
        tok0 = h * T
        self.barrier()
        NKB = 17
        k0 = tok0 - 128
        QT = [self.alloc(8 * T * 2, BF16, [8, T], parts=64) for _ in range(2)]
        KT = [self.alloc(NKB * 128 * 2, BF16, parts=64) for _ in range(2)]
        QTB = [self.buf("B_QT%d" % i) for i in range(2)]
        KTB = [self.buf("B_KT%d" % i) for i in range(2)]
        V = self.alloc(NKB * 4 * 65 * 2, BF16, [NKB, 4, 65])
        VB = self.buf("B_V")
        PT = [[self.alloc(1024 * 2, BF16) for _ in range(2)] for _ in range(2)]
        PTB = [[self.buf("B_PT%d_%d" % (r, k)) for k in range(2)] for r in range(2)]
        ot = [self.alloc(512 * 2, BF16) for _ in range(2)]
        otB = [self.buf("B_ot%d" % i) for i in range(2)]
        OTs = [self.alloc(4 * T * 2, BF16, [4, T]) for _ in range(2)]
        OTsB = [self.buf("B_OTs%d" % i) for i in range(2)]
        esink = self.alloc(32 * 4, F32)
        esB = self.buf("B_esink")
        den = [self.alloc(8 * 4, F32) for _ in range(2)]
        denB = [self.buf("B_den%d" % i) for i in range(2)]
        psf = self.psf
        psB = [self.buf("ps%d" % i) for i in range(7)]
        scale = HA ** -0.5

        p.dma("sp", esink, self.sinks[l:l + 1, :].broadcast_to([128, NQA]), writes=[esB])
        p.op("act", lambda e: e.activation(out=esink, in_=esink, func=AF.Exp), reads=[esB], writes=[esB])
        p.op("dve", lambda e: e.memset(V[:, :, :, 64:65], 1.0), pwrites=[VB])
        kb_lo = 1 if h == 0 else 0
        for g in range(NKVA):
            src = self.Va[l][max(k0, 0):tok0 + T, g * 64:(g + 1) * 64].rearrange("(a p) n -> p a n", p=128)
            p.dma("sp", V[:, kb_lo:NKB, g, 0:64], src, reads=[self.buf("Va%d" % l)], pwrites=[VB])

        sring = [0]
        pend = []

        pend_t = []

        def emit_pv(item):
            g, j, kbs, r, gi = item
            Oa, Ob_ = psf[4], psf[5]
            for i in range(8):
                bank = Oa if i < 4 else Ob_
                out = bank[:, (i % 4) * 65:(i % 4) * 65 + 65]
                fns = []
                for n_, kb in enumerate(kbs):
                    kblk = j + kb
                    fns.append(lambda e, out=out, kb=kb, kblk=kblk, i=i, first=(n_ == 0), last=(n_ == len(kbs) - 1):
                               e.matmul(out, lhsT=PT[r][kb][:, i * 128:(i + 1) * 128], rhs=V[:, kblk, g, :],
                                        start=first, stop=last))
                p.pe(fns, reads=[PTB[r][kb] for kb in kbs] + [VB], pwrites=[psB[4 if i < 4 else 5]])
            flush_t()
            o = ot[r]
            for bi, bank in enumerate((Oa, Ob_)):
                O3 = bank[:, 0:260].rearrange("p (a n) -> p a n", a=4)
                dn = den[r][:, bi * 4:(bi + 1) * 4]
                hs = 8 * g + bi * 4
                p.op("dve", lambda e, dn=dn, O3=O3, hs=hs: e.tensor_tensor(
                    out=dn, in0=O3[:, :, 64], in1=esink[:, hs:hs + 4], op=ALU.add),
                    reads=[psB[4 + bi], esB], pwrites=[denB[r]])
                p.op("dve", lambda e, dn=dn: e.reciprocal(out=dn, in_=dn), reads=[denB[r]], pwrites=[denB[r]])
                o3 = o[:, bi * 256:(bi + 1) * 256].rearrange("p (a n) -> p a n", a=4)
                p.op("dve", lambda e, dn=dn, O3=O3, o3=o3: e.tensor_tensor(
                    out=o3, in0=O3[:, :, 0:64], in1=dn.unsqueeze(2).broadcast_to([128, 4, 64]), op=ALU.mult),
                    reads=[psB[4 + bi], denB[r]], pwrites=[otB[r]])
            pend_t.append((o, otB[r], OTs[gi], OTsB[gi], j * 128))

        def flush_t():
            while pend_t:
                a = pend_t.pop(0)
                self.transpose_tile(a[0], a[1], a[2], a[3], a[4], nchunks=4)

        def load_grp(g):
            if g >= NKVA:
                return
            gi = g % 2
            p.dma("sp", QT[gi], self.QTa[8 * g:8 * g + 8, :, :].rearrange("h d t -> d h t"),
                  reads=[self.buf("QTa")], writes=[QTB[gi]])
            if h == 0:
                p.dma("sp", KT[gi][:, 128:NKB * 128], self.KTa[l][g, :, 0:T], reads=[self.buf("KTa%d" % l)], writes=[KTB[gi]])
            else:
                p.dma("sp", KT[gi], self.KTa[l][g, :, k0:tok0 + T], reads=[self.buf("KTa%d" % l)], writes=[KTB[gi]])

        load_grp(0)
        for g in range(NKVA):
            gi = g % 2
            load_grp(g + 1)
            for j in range(T // 128):
                gq = h * 16 + j
                kbs = [1] if gq == 0 else [0, 1]
                r = j % 2
                for kb in kbs:
                    for hh in range(2):
                        bi = sring[0] % 4
                        sring[0] += 1
                        bank = psf[bi]
                        kcol = (j + kb) * 128
                        fns = [
                            lambda e, bank=bank, kcol=kcol, hh=hh, j=j, gi=gi: e.matmul(
                                bank[:, :], lhsT=KT[gi][:, kcol:kcol + 128],
                                rhs=QT[gi][:, 4 * hh:4 * hh + 4, j * 128:(j + 1) * 128], start=True, stop=False),
                            lambda e, bank=bank, kb=kb: e.matmul(
                                bank[:, :], lhsT=self.ident_sb, rhs=self.maskrep[:, kb, :], start=False, stop=True),
                        ]
                        p.pe(fns, reads=[KTB[gi], QTB[gi], self.buf("ident"), self.buf("maskrep")], writes=[psB[bi]])
                        p.op("act", lambda e, bank=bank, r=r, kb=kb, hh=hh: e.activation(
                            out=PT[r][kb][:, hh * 512:(hh + 1) * 512], in_=bank[:, :], func=AF.Exp, scale=scale),
                            reads=[psB[bi]], pwrites=[PTB[r][kb]])
                pend.append((g, j, kbs, r, gi))
                if len(pend) > 1:
                    emit_pv(pend.pop(0))
            while pend:
                emit_pv(pend.pop(0))
            flush_t()
            p.dma("sp", self.OTa[g * 512:(g + 1) * 512, :].rearrange("(c p) t -> p c t", p=128), OTs[gi],
                  reads=[OTsB[gi]], pwrites=[self.buf("OTa")])

    def phase_C(self, l, h):
        self.tp_two = False
        try:
            self._phase_C(l, h)
        finally:
            self.tp_two = True

    def _phase_C(self, l, h):
        p = self.p
        tok0 = h * T
        self.barrier()
        nk = tok0 + T
        nkb = nk // 128
        KT = [self.alloc(2 * SEQ * 2, BF16, [2, SEQ]) for _ in range(2)]
        QT = [self.alloc(2 * T * 2, BF16, [2, T]) for _ in range(2)]
        V = [self.alloc(32 * 257 * 2, BF16, [32, 257]) for _ in range(2)]
        KTB = [self.buf("C_KT%d" % i) for i in range(2)]
        QTB = [self.buf("C_QT%d" % i) for i in range(2)]
        VB = [self.buf("C_V%d" % i) for i in range(2)]
        PT = [self.alloc(512 * 2, BF16) for _ in range(3)]
        PTB = [self.buf("C_PT%d" % i) for i in range(3)]
        OTs = [self.alloc(2 * T * 2, BF16, [2, T]) for _ in range(2)]
        OTsB = [self.buf("C_OTs%d" % i) for i in range(2)]
        tf = [self.alloc(256 * 4, F32) for _ in range(2)]
        tfB = [self.buf("C_tf%d" % i) for i in range(2)]
        of = [self.alloc(256 * 4, F32) for _ in range(2)]
        ofB = [self.buf("C_of%d" % i) for i in range(2)]
        junk = self.alloc(256 * 4, F32)
        junkB = self.buf("C_junk")
        ob = [self.alloc(256 * 2, BF16) for _ in range(2)]
        obB = [self.buf("C_ob%d" % i) for i in range(2)]
        sm = [self.alloc(8 * 4, F32) for _ in range(2)]
        OC = [[[self.alloc(257 * 4, F32) for _ in range(2)] for _ in range(2)] for _ in range(2)]
        OCB = [[[self.buf("C_OC%d_%d_%d" % (s, c, q)) for q in range(2)] for c in range(2)] for s in range(2)]
        OFs = [self.alloc(16 * 256 * 4, F32, [16, 256]) for _ in range(2)]
        OFBs = [self.buf("C_OF%d" % i) for i in range(2)]
        sss = [self.alloc(64 * 4, F32) for _ in range(2)]
        ssBs = [self.buf("C_ss%d" % i) for i in range(2)]
        smB = [self.buf("C_sm%d" % i) for i in range(2)]
        lamt = self.alloc(4 * HB * 4, F32, [4, HB])
        lamB = self.buf("C_lamt")
        lsc = self.alloc(8 * 4, F32)
        lscB = self.buf("C_lsc")
        subg = self.alloc(256 * 4, F32)
        subgB = self.buf("C_subg")
        psf = self.psf
        psB = [self.buf("ps%d" % i) for i in range(7)]
        scale = HB ** -0.5
        lam_init = 0.8 - 0.6 * math.exp(-0.3 * l)

        p.dma("sp", lamt, self.lam[l:l + 1, :, :].broadcast_to([128, 4, HB]), writes=[lamB])
        p.dma("sp", subg, self.subg[l:l + 1, :].broadcast_to([128, 2 * HB]), writes=[subgB])
        for i in range(2):
            p.op("dve", lambda e, i=i: e.scalar_tensor_tensor(
                out=junk[:, 0:HB], in0=lamt[:, 2 * i, :], scalar=1.0, in1=lamt[:, 2 * i + 1, :],
                op0=ALU.mult, op1=ALU.mult, accum_out=lsc[:, i:i + 1]),
                reads=[lamB], writes=[junkB], pwrites=[lscB])
        p.op("act", lambda e: e.activation(out=lsc[:, 0:2], in_=lsc[:, 0:2], func=AF.Exp), reads=[lscB], writes=[lscB])
        p.op("dve", lambda e: e.scalar_tensor_tensor(
            out=lsc[:, 2:3], in0=lsc[:, 1:2], scalar=-lam_init, in1=lsc[:, 0:1], op0=ALU.add, op1=ALU.subtract),
            reads=[lscB], writes=[lscB])
        for i in range(2):
            p.op("dve", lambda e, i=i: e.memset(V[i][:, :, 256:257], 1.0), pwrites=[VB[i]])
        p.op("dve", lambda e: e.tensor_scalar(out=subg, in0=subg, scalar1=(1.0 - lam_init), scalar2=None, op0=ALU.mult),
             reads=[subgB], writes=[subgB])

        sring = [0]
        Oacc = [[psf[2], psf[3]], [psf[4], psf[5]]]
        OaccB = [[psB[2], psB[3]], [psB[4], psB[5]]]
        SB = [0, 1, 6]
        fin = [0]

        def load_head(hd):
            if hd >= NHB:
                return
            hi = hd % 2
            for c in range(2):
                p.dma("sp", KT[hi][:, c, 0:nk], self.KTb[l][2 * hd + c, :, 0:nk], reads=[self.buf("KTb%d" % l)], pwrites=[KTB[hi]])
                p.dma("sp", QT[hi][:, c, :], self.QTb[2 * hd + c, :, :], reads=[self.buf("QTb")], pwrites=[QTB[hi]])
            p.dma("sp", V[hi][:, 0:nkb, 0:256],
                  self.Vb[l][0:nk, hd * 256:(hd + 1) * 256].rearrange("(a p) n -> p a n", p=128),
                  reads=[self.buf("Vb%d" % l)], pwrites=[VB[hi]])

        ep_steps = []

        def run_steps(n):
            for _ in range(n):
                if ep_steps:
                    ep_steps.pop(0)()

        def epilogue(hd):
            run_steps(len(ep_steps))
            hi = hd % 2
            OF, OFB, ss, ssB = OFs[hi], OFBs[hi], sss[hi], ssBs[hi]

            def s0():
                p.op("dve", lambda e: e.tensor_scalar(out=ss[:, 16:32], in0=ss[:, 0:16], scalar1=1.0 / 256.0, scalar2=SUB_EPS,
                                                      op0=ALU.mult, op1=ALU.add), reads=[ssB], pwrites=[ssB])
                self.rsqrt(ss[:, 16:32], ss[:, 32:48], ss[:, 48:64], ssB)
            ep_steps.append(s0)
            for jb in range(T // 128):
                def sj(jb=jb):
                    f = jb % 2
                    p.op("dve", lambda e, f=f, jb=jb: e.scalar_tensor_tensor(
                        out=ob[f], in0=OF[:, jb, :], scalar=ss[:, 32 + jb:33 + jb], in1=subg, op0=ALU.mult, op1=ALU.mult),
                        reads=[OFB, ssB, subgB], writes=[obB[f]])
                    self.transpose_tile(ob[f], obB[f], OTs[hi], OTsB[hi], jb * 128, nchunks=2)
                ep_steps.append(sj)

            def sl():
                p.dma("sp", self.OTb[hd * 256:(hd + 1) * 256, :].rearrange("(c p) t -> p c t", p=128), OTs[hi],
                      reads=[OTsB[hi]], pwrites=[self.buf("OTb")])
            ep_steps.append(sl)

        load_head(0)
        for hd in range(NHB):
            hi = hd % 2
            OF, OFB, ss, ssB = OFs[hi], OFBs[hi], sss[hi], ssBs[hi]
            load_head(hd + 1)
            for jp in range(T // 256):
                gq0 = h * 16 + 2 * jp
                nkbs = gq0 + 2
                pend = []

                def emit_pv(item):
                    kb, pi = item
                    for c in range(2):
                        for qq in range(2):
                            if kb == gq0 + 1 and qq == 0:
                                continue
                            last = (kb == gq0 + qq)
                            p.pe([lambda e, c=c, qq=qq, kb=kb, pi=pi, last=last, hi=hi: e.matmul(
                                Oacc[c][qq][:, 0:257], lhsT=PT[pi][:, c * 256 + qq * 128:c * 256 + qq * 128 + 128],
                                rhs=V[hi][:, kb, :], start=(kb == 0), stop=last)],
                                reads=[PTB[pi], VB[hi]], pwrites=[OaccB[c][qq]])

                ran = 0
                for kb in range(nkbs):
                    bi = SB[sring[0] % 3]
                    pi = sring[0] % 3
                    sring[0] += 1
                    bank = psf[bi]
                    masked = kb >= gq0
                    fns = []
                    for c in range(2):
                        fns.append(lambda e, bank=bank, c=c, kb=kb, masked=masked, hi=hi, jp=jp: e.matmul(
                            bank[:, c * 256:(c + 1) * 256], lhsT=KT[hi][:, c, kb * 128:(kb + 1) * 128],
                            rhs=QT[hi][:, c, jp * 256:(jp + 1) * 256], start=True, stop=(not masked)))
                        if masked:
                            mt = kb - gq0
                            fns.append(lambda e, bank=bank, c=c, mt=mt: e.matmul(
                                bank[:, c * 256:(c + 1) * 256], lhsT=self.ident_sb, rhs=self.maskpair[:, mt, :],
                                start=False, stop=True))
                    p.pe(fns, reads=[KTB[hi], QTB[hi], self.buf("ident"), self.buf("maskpair")], writes=[psB[bi]])
                    p.op("act", lambda e, bank=bank, pi=pi: e.activation(out=PT[pi], in_=bank[:, :], func=AF.Exp, scale=scale),
                         reads=[psB[bi]], writes=[PTB[pi]])
                    pend.append((kb, pi))
                    if len(pend) > 2:
                        emit_pv(pend.pop(0))
                    if nkbs >= 8 and kb in (nkbs // 4, nkbs // 2, (3 * nkbs) // 4):
                        run_steps(1)
                        ran += 1

                while pend:
                    emit_pv(pend.pop(0))
                run_steps(3 - ran)
                oset = (hd * (T // 256) + jp) % 2
                for c in range(2):
                    for qq in range(2):
                        src_ = Oacc[c][qq][:, 0:257]
                        dst_ = OC[oset][c][qq]
                        p.op("dve", lambda e, o=dst_, a=src_: e.tensor_copy(out=o, in_=a), reads=[OaccB[c][qq]], writes=[OCB[oset][c][qq]])
                for qq in range(2):
                    f = fin[0] % 2
                    fin[0] += 1
                    O1, O2 = OC[oset][0][qq], OC[oset][1][qq]
                    O1B, O2B = OCB[oset][0][qq], OCB[oset][1][qq]
                    s_ = sm[f]
                    p.op("dve", lambda e, s_=s_, O1=O1: e.reciprocal(out=s_[:, 0:1], in_=O1[:, 256:257]), reads=[O1B], pwrites=[smB[f]])
                    p.op("dve", lambda e, s_=s_, O2=O2: e.reciprocal(out=s_[:, 1:2], in_=O2[:, 256:257]), reads=[O2B], pwrites=[smB[f]])
                    p.op("dve", lambda e, s_=s_: e.tensor_tensor(out=s_[:, 1:2], in0=s_[:, 1:2], in1=lsc[:, 2:3], op=ALU.mult),
                         reads=[smB[f], lscB], pwrites=[smB[f]])
                    p.op("dve", lambda e, s_=s_, O1=O1, f=f: e.tensor_scalar(
                        out=tf[f], in0=O1[:, 0:256], scalar1=s_[:, 0:1], scalar2=None, op0=ALU.mult),
                        reads=[O1B, smB[f]], writes=[tfB[f]])
                    jb = 2 * jp + qq
                    p.op("dve", lambda e, s_=s_, O2=O2, f=f, jb=jb, OF=OF: e.scalar_tensor_tensor(
                        out=OF[:, jb, :], in0=O2[:, 0:256], scalar=s_[:, 1:2], in1=tf[f], op0=ALU.mult, op1=ALU.add),
                        reads=[O2B, smB[f], tfB[f]], pwrites=[OFB])
                    p.op("dve", lambda e, jb=jb, OF=OF, ss=ss: e.scalar_tensor_tensor(
                        out=junk, in0=OF[:, jb, :], scalar=1.0, in1=OF[:, jb, :], op0=ALU.mult, op1=ALU.mult,
                        accum_out=ss[:, jb:jb + 1]), reads=[OFB], writes=[junkB], pwrites=[ssB])
                if jp == 0 and hd > 0:
                    epilogue(hd - 1)
        epilogue(NHB - 1)
        run_steps(len(ep_steps))

    def phase_D(self, l, h):
        p = self.p
        self.barrier()
        TS = 1024
        OA = self.alloc(KC * TS * 2, BF16, [KC, TS])
        OB = self.alloc(KC * TS * 2, BF16, [KC, TS])
        OAB, OBB = self.buf("D_OA"), self.buf("D_OB")
        wa = [self.alloc(KC * 128 * 2, BF16, [KC, 128]) for _ in range(3)]
        wb = [self.alloc(KC * 128 * 2, BF16, [KC, 128]) for _ in range(3)]
        waB = [self.buf("D_wa%d" % i) for i in range(3)]
        wbB = [self.buf("D_wb%d" % i) for i in range(3)]
        ga = [self.alloc(TS * 2, BF16) for _ in range(3)]
        gb = [self.alloc(TS * 2, BF16) for _ in range(3)]
        gaB = [self.buf("D_ga%d" % i) for i in range(3)]
        gbB = [self.buf("D_gb%d" % i) for i in range(3)]
        t1 = [self.alloc(512 * 4, F32) for _ in range(2)]
        t2 = [self.alloc(512 * 4, F32) for _ in range(2)]
        t1B = [self.buf("D_t1%d" % i) for i in range(2)]
        t2B = [self.buf("D_t2%d" % i) for i in range(2)]
        ms = [self.alloc(TS * 2, BF16) for _ in range(3)]
        msB = [self.buf("D_ms%d" % i) for i in range(3)]
        psf = self.psf
        psB = [self.buf("ps%d" % i) for i in range(7)]
        pr = [0]
        it = [0]
        NIT = (T // TS) * KC

        def load_it(k):
            if k >= NIT:
                return
            s_, c_ = k // KC, k % KC
            i_ = k % 3
            t_ = s_ * TS
            p.dma("sp", wa[i_], self.wb_pa[l][c_], reads=[self.wbuf("pa", l, 0)], writes=[waB[i_]])
            p.dma("sp", wb[i_], self.wb_pb[l][c_], reads=[self.wbuf("pb", l, 0)], writes=[wbB[i_]])
            p.dma("sp", ga[i_], self.GT[c_ * 128:(c_ + 1) * 128, t_:t_ + TS], reads=[self.buf("GT")], writes=[gaB[i_]])
            p.dma("sp", gb[i_], self.GT[D + c_ * 128:D + (c_ + 1) * 128, t_:t_ + TS], reads=[self.buf("GT")], writes=[gbB[i_]])

        load_it(0)
        load_it(1)
        for sub in range(T // TS):
            ts0 = sub * TS
            for c0 in range(0, KC, 4):
                p.dma("sp", OA[:, c0:c0 + 4, :], self.OTa[c0 * 128:(c0 + 4) * 128, ts0:ts0 + TS].rearrange("(c p) t -> p c t", p=128),
                      reads=[self.buf("OTa")], pwrites=[OAB])
                p.dma("sp", OB[:, c0:c0 + 4, :], self.OTb[c0 * 128:(c0 + 4) * 128, ts0:ts0 + TS].rearrange("(c p) t -> p c t", p=128),
                      reads=[self.buf("OTb")], pwrites=[OBB])
            for c in range(KC):
                i = it[0] % 3
                load_it(it[0] + 2)
                it[0] += 1
                for tt in range(TS // 512):
                    tsl = slice(tt * 512, (tt + 1) * 512)
                    banks = []
                    for (W, WB, O, OBf) in ((wa[i], waB[i], OA, OAB), (wb[i], wbB[i], OB, OBB)):
                        bi = pr[0] % 6
                        pr[0] += 1
                        ps = psf[bi]
                        fns = [lambda e, kc=kc, ps=ps, W=W, O=O, tsl=tsl: e.matmul(
                            ps[:, :], lhsT=W[:, kc, :], rhs=O[:, kc, tsl], start=(kc == 0), stop=(kc == KC - 1))
                            for kc in range(KC)]
                        p.pe(fns, reads=[WB, OBf], writes=[psB[bi]])
                        banks.append((ps, psB[bi]))
                    (pa, paB), (pb_, pbB) = banks
                    k = tt % 2
                    p.op("dve", lambda e, k=k, pa=pa, i=i, tsl=tsl: e.tensor_tensor(out=t1[k], in0=pa[:, :], in1=ga[i][:, tsl], op=ALU.mult),
                         reads=[paB, gaB[i]], writes=[t1B[k]])
                    p.op("dve", lambda e, k=k, pb_=pb_, i=i, tsl=tsl: e.tensor_tensor(out=t2[k], in0=pb_[:, :], in1=gb[i][:, tsl], op=ALU.mult),
                         reads=[pbB, gbB[i]], writes=[t2B[k]])
                    p.op("dve", lambda e, k=k, i=i, tsl=tsl: e.tensor_tensor(out=ms[i][:, tsl], in0=t1[k], in1=t2[k], op=ALU.add),
                         reads=[t1B[k], t2B[k]], pwrites=[msB[i]])
                p.dma("sp", self.MT[c * 128:(c + 1) * 128, ts0:ts0 + TS], ms[i], reads=[msB[i]], pwrites=[self.buf("MT")])

    def rsqrt(self, v, r, t, B, iters=2):
        p = self.p
        p.op("act", lambda e: e.activation(out=r, in_=v, func=AF.Sqrt), reads=[B], pwrites=[B])
        p.op("dve", lambda e: e.reciprocal(out=r, in_=r), reads=[B], pwrites=[B])
        for _ in range(iters):
            p.op("dve", lambda e: e.tensor_tensor(out=t, in0=r, in1=r, op=ALU.mult), reads=[B], pwrites=[B])
            p.op("dve", lambda e: e.tensor_tensor(out=t, in0=t, in1=v, op=ALU.mult), reads=[B], pwrites=[B])
            p.op("dve", lambda e: e.tensor_scalar(out=t, in0=t, scalar1=-0.5, scalar2=1.5, op0=ALU.mult, op1=ALU.add),
                 reads=[B], pwrites=[B])
            p.op("dve", lambda e: e.tensor_tensor(out=r, in0=r, in1=t, op=ALU.mult), reads=[B], pwrites=[B])

    def rsqrt1(self, v, r, t, B, seed=True):
        p = self.p
        if seed:
            p.op("act", lambda e: e.activation(out=r, in_=v, func=AF.Sqrt), reads=[B], pwrites=[B])
        p.op("dve", lambda e: e.reciprocal(out=r, in_=r), reads=[B], pwrites=[B])
        for _ in range(2):
            p.op("dve", lambda e: e.scalar_tensor_tensor(out=t, in0=r, scalar=v, in1=r, op0=ALU.mult, op1=ALU.mult),
                 reads=[B], pwrites=[B])
            p.op("dve", lambda e: e.tensor_scalar(out=t, in0=t, scalar1=-0.5, scalar2=1.5, op0=ALU.mult, op1=ALU.add),
                 reads=[B], pwrites=[B])
            p.op("dve", lambda e: e.tensor_tensor(out=r, in0=r, in1=t, op=ALU.mult), reads=[B], pwrites=[B])

    def ln_tile(self, y, yB, gt, bt, gbB, st, stB, hbf, hbfB):
        p = self.p
        stats = st[:, 0:24].rearrange("p (a n) -> p a n", a=4)
        mv = st[:, 24:26]
        rstd = st[:, 26:27]
        nb = st[:, 27:28]
        for n in range(4):
            p.op("dve", lambda e, n=n: e.bn_stats(out=stats[:, n, :], in_=y[:, n * 512:(n + 1) * 512]), reads=[yB], pwrites=[stB])
        p.op("dve", lambda e: e.bn_aggr(out=mv, in_=st[:, 0:24]), reads=[stB], pwrites=[stB])
        p.op("dve", lambda e: e.tensor_scalar(out=st[:, 28:29], in0=st[:, 25:26], scalar1=LN_EPS, scalar2=None, op0=ALU.add),
             reads=[stB], pwrites=[stB])
        p.op("act", lambda e: e.activation(out=rstd, in_=st[:, 28:29], func=AF.Sqrt), reads=[stB], pwrites=[stB])
        p.op("dve", lambda e: e.scalar_tensor_tensor(out=y, in0=y, scalar=st[:, 24:25], in1=gt, op0=ALU.subtract, op1=ALU.mult),
             reads=[yB, stB, gbB], writes=[yB])
        self.rsqrt1(st[:, 28:29], rstd, st[:, 29:30], stB, seed=False)
        p.op("dve", lambda e: e.scalar_tensor_tensor(out=y, in0=y, scalar=rstd, in1=bt, op0=ALU.mult, op1=ALU.add),
             reads=[yB, stB, gbB], writes=[yB])
        if hbf is not None:
            p.op("act", lambda e: e.copy(out=hbf, in_=y), reads=[yB], writes=[hbfB])

    def phase_E(self, l, h):
        p = self.p
        tok0 = h * T
        self.barrier()
        TS = 512
        Wo = self.alloc(KC * D * 2, BF16, [KC, D])
        WoB = self.buf("E_Wo")
        Ms = [self.alloc(KC * TS * 2, BF16, [KC, TS]) for _ in range(2)]
        MBs = [self.buf("E_M%d" % i) for i in range(2)]
        gt = self.alloc(D * 4, F32)
        bt = self.alloc(D * 4, F32)
        gbB = self.buf("E_gb")
        yt = [self.alloc(D * 4, F32) for _ in range(3)]
        ytB = [self.buf("E_y%d" % i) for i in range(3)]
        hb = [self.alloc(D * 2, BF16) for _ in range(2)]
        hbB = [self.buf("E_hb%d" % i) for i in range(2)]
        st = [self.alloc(32 * 4, F32) for _ in range(2)]
        stB = [self.buf("E_st%d" % i) for i in range(2)]
        HTs = self.alloc(KC * 256 * 2, BF16, [KC, 256])
        HTsB = self.buf("E_HTs")
        psf = self.psf
        psB = [self.buf("ps%d" % i) for i in range(7)]
        pr = [0]
        xsrc = self.x if l == 0 else self.Xres
        xoff = tok0 if l == 0 else 0
        xsrcB = [] if l == 0 else [self.buf("Xres")]
        for c0 in range(0, KC, 4):
            p.dma("sp", Wo[:, c0:c0 + 4, :], self.wb_out[l][:, c0:c0 + 4, :], reads=[self.wbuf("out", l, 0)], pwrites=[WoB])
        p.dma("sp", gt, self.ln1g[l:l + 1, :].broadcast_to([128, D]), pwrites=[gbB])
        p.dma("sp", bt, self.ln1b[l:l + 1, :].broadcast_to([128, D]), pwrites=[gbB])
        def load_y(k):
            if k < T // 128:
                p.dma("sp", yt[k % 3], xsrc[xoff + k * 128:xoff + (k + 1) * 128, :], reads=xsrcB, writes=[ytB[k % 3]])

        def load_m(s_):
            if s_ < T // TS:
                for c0 in range(0, KC, 8):
                    p.dma("sp", Ms[s_ % 2][:, c0:c0 + 8, :],
                          self.MT[c0 * 128:(c0 + 8) * 128, s_ * TS:(s_ + 1) * TS].rearrange("(c p) t -> p c t", p=128),
                          reads=[self.buf("MT")], pwrites=[MBs[s_ % 2]])

        load_m(0)
        NB_ = T // 128
        banks = {}

        def mm(gidx):
            sub, tb = divmod(gidx, TS // 128)
            if tb == 0:
                load_m(sub + 1)
            if gidx == 0:
                load_y(0)
            load_y(gidx + 1)
            M, MB = Ms[sub % 2], MBs[sub % 2]
            bl = []
            for n in range(4):
                bi = pr[0] % 6
                pr[0] += 1
                ps = psf[bi]
                fns = [lambda e, kc=kc, ps=ps, tb=tb, n=n, M=M: e.matmul(
                    ps[:, :], lhsT=M[:, kc, tb * 128:(tb + 1) * 128], rhs=Wo[:, kc, n * 512:(n + 1) * 512],
                    start=(kc == 0), stop=(kc == KC - 1)) for kc in range(KC)]
                p.pe(fns, reads=[MB, WoB], writes=[psB[bi]])
                bl.append(bi)
            banks[gidx] = bl

        def adds(gidx):
            y, yB = yt[gidx % 3], ytB[gidx % 3]
            for n, bi in enumerate(banks.pop(gidx)):
                ps = psf[bi]
                p.op("dve", lambda e, y=y, ps=ps, n=n: e.scalar_tensor_tensor(
                    out=y[:, n * 512:(n + 1) * 512], in0=y[:, n * 512:(n + 1) * 512], scalar=ALPHA, in1=ps[:, :],
                    op0=ALU.mult, op1=ALU.add), reads=[psB[bi]], writes=[yB])

        def tail(gidx):
            y, yB = yt[gidx % 3], ytB[gidx % 3]
            i = gidx % 2
            tl = gidx * 128
            self.ln_tile(y, yB, gt, bt, gbB, st[i], stB[i], hb[i], hbB[i])
            p.dma("sp", self.Hres[tl:tl + 128, :], y, reads=[yB], pwrites=[self.buf("Hres")])
            self.transpose_tile(hb[i], hbB[i], HTs, HTsB, (gidx % 2) * 128)
            if gidx % 2 == 1:
                t0 = tl - 128
                p.dma("sp", self.HTd.rearrange("(c p) t -> p c t", p=128)[:, :, t0:t0 + 256], HTs,
                      reads=[HTsB], pwrites=[self.buf("HTd")])

        for gidx in range(NB_):
            mm(gidx)
            if gidx > 0:
                tail(gidx - 1)
            adds(gidx)
        tail(NB_ - 1)

    def phase_F(self, l, h):
        p = self.p
        self.barrier()
        HT = self.alloc(KC * T * 2, BF16, [KC, T])
        HTB = self.buf("F_HT")
        wblk = [self.alloc(KC * 256 * 2, BF16, [KC, 256]) for _ in range(3)]
        wblkB = [self.buf("F_w%d" % i) for i in range(3)]
        U = [[self.alloc((T + 2) * 4, F32) for _ in range(2)] for _ in range(2)]
        UB = [[self.buf("F_U%d_%d" % (r, c)) for c in range(2)] for r in range(2)]
        cv = [[self.alloc(512 * 4, F32) for _ in range(2)] for _ in range(2)]
        cvB = [[self.buf("F_cv%d_%d" % (r, c)) for c in range(2)] for r in range(2)]
        sl = [self.alloc(512 * 4, F32) for _ in range(2)]
        slB = [self.buf("F_sl%d" % i) for i in range(2)]
        gs = [self.alloc(T * 2, BF16) for _ in range(2)]
        gsB = [self.buf("F_gs%d" % i) for i in range(2)]
        cw = self.alloc(88 * 3 * 4, F32, [88, 3])
        cb = self.alloc(88 * 4, F32)
        cwB = self.buf("F_cw")
        psf = self.psf
        psB = [self.buf("ps%d" % i) for i in range(7)]
        pr = [0]
        uh = self.uhalo
        uhB = self.buf("uhalo")
        for c0 in range(0, KC, 4):
            p.dma("sp", HT[:, c0:c0 + 4, :], self.HTd[c0 * 128:(c0 + 4) * 128, :].rearrange("(c p) t -> p c t", p=128),
                  reads=[self.buf("HTd")], pwrites=[HTB])
        p.dma("sp", cw, self.convw[l], pwrites=[cwB])
        p.dma("sp", cb, self.convb[l], pwrites=[cwB])
        def load_w(b):
            if b < DFF // 128:
                p.dma("sp", wblk[b % 3], self.wb_up[l][b], reads=[self.wbuf("up", l, b // 8)], writes=[wblkB[b % 3]])

        load_w(0)
        load_w(1)
        for blk in range(DFF // 128):
            wi = blk % 3
            W = wblk[wi]
            r = blk % 2
            load_w(blk + 2)
            for ch in range(2):
                p.op("act", lambda e, r=r, ch=ch, blk=blk: e.copy(out=U[r][ch][:, 0:2], in_=uh[:, l, 2 * blk + ch, :]),
                     reads=[uhB], pwrites=[UB[r][ch]])
            g_, gB_ = gs[r], gsB[r]
            for tt in range(4):
                tsl = slice(tt * 512, (tt + 1) * 512)
                k = tt % 2
                for ch in range(2):
                    bi = pr[0] % 6
                    pr[0] += 1
                    ps = psf[bi]
                    fns = [lambda e, kc=kc, ps=ps, ch=ch, W=W, tsl=tsl: e.matmul(
                        ps[:, :], lhsT=W[:, kc, ch * 128:(ch + 1) * 128], rhs=HT[:, kc, tsl],
                        start=(kc == 0), stop=(kc == KC - 1)) for kc in range(KC)]
                    p.pe(fns, reads=[wblkB[wi], HTB], writes=[psB[bi]])
                    u = U[r][ch]
                    p.op("act", lambda e, u=u, ps=ps, tt=tt: e.copy(out=u[:, 2 + tt * 512:2 + (tt + 1) * 512], in_=ps[:, :]),
                         reads=[psB[bi]], pwrites=[UB[r][ch]])
                    a = cv[k][ch]
                    ci = 2 * blk + ch
                    t0 = tt * 512
                    p.op("dve", lambda e, a=a, u=u, ci=ci, t0=t0: e.tensor_scalar(
                        out=a, in0=u[:, t0 + 2:t0 + 514], scalar1=cw[:, ci, 2:3], scalar2=cb[:, ci:ci + 1], op0=ALU.mult, op1=ALU.add),
                        reads=[UB[r][ch], cwB], writes=[cvB[k][ch]])
                    p.op("dve", lambda e, a=a, u=u, ci=ci, t0=t0: e.scalar_tensor_tensor(
                        out=a, in0=u[:, t0 + 1:t0 + 513], scalar=cw[:, ci, 1:2], in1=a, op0=ALU.mult, op1=ALU.add),
                        reads=[UB[r][ch], cwB, cvB[k][ch]], writes=[cvB[k][ch]])
                    p.op("dve", lambda e, a=a, u=u, ci=ci, t0=t0: e.scalar_tensor_tensor(
                        out=a, in0=u[:, t0:t0 + 512], scalar=cw[:, ci, 0:1], in1=a, op0=ALU.mult, op1=ALU.add),
                        reads=[UB[r][ch], cwB, cvB[k][ch]], writes=[cvB[k][ch]])
                p.op("act", lambda e, k=k: e.activation(out=sl[k], in_=cv[k][0], func=AF.Silu), reads=[cvB[k][0]], writes=[slB[k]])
                p.op("dve", lambda e, k=k, g_=g_, tsl=tsl: e.tensor_tensor(out=g_[:, tsl], in0=sl[k], in1=cv[k][1], op=ALU.mult),
                     reads=[slB[k], cvB[k][1]], pwrites=[gB_])
            p.dma("sp", self.GF[blk * 128:(blk + 1) * 128, :], g_, reads=[gB_], pwrites=[self.buf("GF")])
            if h == 0:
                for ch in range(2):
                    p.op("act", lambda e, r=r, ch=ch, blk=blk: e.copy(out=uh[:, l, 2 * blk + ch, :], in_=U[r][ch][:, T:T + 2]),
                         reads=[UB[r][ch]], pwrites=[uhB])

    def phase_G(self, l, h):
        p = self.p
        tok0 = h * T
        psf = self.psf
        psB = [self.buf("ps%d" % i) for i in range(7)]
        last = (l == self.nlayers - 1)
        for ps_i, (ka, kb_) in enumerate(((0, 24), (24, 44))):
            nkc = kb_ - ka
            self.barrier()
            Wd = self.alloc(nkc * D * 2, BF16, [nkc, D])
            WdB = self.buf("G_Wd%d" % ps_i)
            gT = [self.alloc(nkc * 256 * 2, BF16, [nkc, 256]) for _ in range(2)]
            gTB = [self.buf("G_gT%d_%d" % (ps_i, i)) for i in range(2)]
            yt = [self.alloc(D * 4, F32) for _ in range(3)]
            ytB = [self.buf("G_y%d_%d" % (ps_i, i)) for i in range(3)]
            if ps_i == 1:
                gt = self.alloc(D * 4, F32)
                bt = self.alloc(D * 4, F32)
                gbB = self.buf("G_gb")
                hb = [self.alloc(D * 2, BF16) for _ in range(2)]
                hbB = [self.buf("G_hb%d" % i) for i in range(2)]
                st = [self.alloc(32 * 4, F32) for _ in range(2)]
                stB = [self.buf("G_st%d" % i) for i in range(2)]
                XTs = self.alloc(KC * 256 * 2, BF16, [KC, 256])
                XTsB = self.buf("G_XTs")
                p.dma("sp", gt, self.ln2g[l:l + 1, :].broadcast_to([128, D]), pwrites=[gbB])
                p.dma("sp", bt, self.ln2b[l:l + 1, :].broadcast_to([128, D]), pwrites=[gbB])
            for c0 in range(0, nkc, 4):
                p.dma("sp", Wd[:, c0:c0 + 4, :], self.wb_down[l][:, ka + c0:ka + c0 + 4, :],
                      reads=[self.wbuf("down", l, (ka + c0) // 4)], pwrites=[WdB])
            pr = [0]

            def load_g(k, gT=gT, gTB=gTB, ka=ka, kb_=kb_):
                if k < T // 256:
                    p.dma("sp", gT[k % 2], self.GF[ka * 128:kb_ * 128, k * 256:(k + 1) * 256].rearrange("(c p) t -> p c t", p=128),
                          reads=[self.buf("GF")], writes=[gTB[k % 2]])

            def load_y(k, yt=yt, ytB=ytB, ps_i=ps_i):
                if k < T // 128:
                    if ps_i == 0:
                        p.dma("sp", yt[k % 3], self.Hres[k * 128:(k + 1) * 128, :], reads=[self.buf("Hres")], writes=[ytB[k % 3]])
                    else:
                        p.dma("sp", yt[k % 3], self.Y1[k * 128:(k + 1) * 128, :], reads=[self.buf("Y1")], writes=[ytB[k % 3]])

            load_g(0)
            load_y(0)
            banks = {}
            NB_ = T // 128

            def mm(tb, gT=gT, gTB=gTB, Wd=Wd, WdB=WdB, nkc=nkc, banks=banks, load_g=load_g, load_y=load_y, pr=pr):
                gi = (tb // 2) % 2
                if tb % 2 == 0:
                    load_g(tb // 2 + 1)
                load_y(tb + 1)
                bl = []
                for n in range(4):
                    bi = pr[0] % 6
                    pr[0] += 1
                    ps = psf[bi]
                    fns = [lambda e, kc=kc, ps=ps, tb=tb, n=n, g_=gT[gi], Wd=Wd, nkc=nkc: e.matmul(
                        ps[:, :], lhsT=g_[:, kc, (tb % 2) * 128:(tb % 2) * 128 + 128], rhs=Wd[:, kc, n * 512:(n + 1) * 512],
                        start=(kc == 0), stop=(kc == nkc - 1)) for kc in range(nkc)]
                    p.pe(fns, reads=[gTB[gi], WdB], writes=[psB[bi]])
                    bl.append(bi)
                banks[tb] = bl

            def adds(tb, yt=yt, ytB=ytB, banks=banks, ps_i=ps_i):
                y, yB = yt[tb % 3], ytB[tb % 3]
                sc = ALPHA if ps_i == 0 else 1.0
                for n, bi in enumerate(banks.pop(tb)):
                    ps = psf[bi]
                    p.op("dve", lambda e, y=y, ps=ps, n=n, sc=sc: e.scalar_tensor_tensor(
                        out=y[:, n * 512:(n + 1) * 512], in0=y[:, n * 512:(n + 1) * 512], scalar=sc, in1=ps[:, :],
                        op0=ALU.mult, op1=ALU.add), reads=[psB[bi]], writes=[yB])

            if ps_i == 0:
                def tail(tb, yt=yt, ytB=ytB):
                    y, yB = yt[tb % 3], ytB[tb % 3]
                    tl = tb * 128
                    p.dma("sp", self.Y1[tl:tl + 128, :], y, reads=[yB], pwrites=[self.buf("Y1")])
            else:
                def tail(tb, yt=yt, ytB=ytB, gt=gt, bt=bt, gbB=gbB, st=st, stB=stB, hb=hb, hbB=hbB, XTs=XTs, XTsB=XTsB):
                    y, yB = yt[tb % 3], ytB[tb % 3]
                    i = tb % 2
                    tl = tb * 128
                    self.ln_tile(y, yB, gt, bt, gbB, st[i], stB[i], None if last else hb[i], None if last else hbB[i])
                    if last:
                        p.dma("sp", self.y[tok0 + tl:tok0 + tl + 128, :], y, reads=[yB], pwrites=[self.buf("y_out")])
                    else:
                        p.dma("sp", self.Xres[tl:tl + 128, :], y, reads=[yB], pwrites=[self.buf("Xres")])
                        self.transpose_tile(hb[i], hbB[i], XTs, XTsB, (tb % 2) * 128)
                        if tb % 2 == 1:
                            t0 = tl - 128
                            p.dma("sp", self.XTd.rearrange("(c p) t -> p c t", p=128)[:, :, t0:t0 + 256], XTs,
                                  reads=[XTsB], pwrites=[self.buf("XTd")])

            for tb in range(NB_):
                mm(tb)
                if tb > 0:
                    tail(tb - 1)
                adds(tb)
            tail(NB_ - 1)

    def load_consts(self):
        p = self.p
        self.persist_end = 0
        self.aoff = 0
        self.ident_sb = self.alloc(128 * 2, BF16)
        self.masks_sb = self.alloc(4 * 128 * 2, BF16, [4, 128])
        p.dma("sp", self.ident_sb, self.ident, writes=[self.buf("ident")])
        p.dma("sp", self.masks_sb, self.masks, writes=[self.buf("masks")])
        self.maskrep = self.alloc(2 * 512 * 2, BF16, [2, 512])
        for kb, mi in ((0, 1), (1, 0)):
            for hh in range(4):
                p.op("dve", lambda e, kb=kb, mi=mi, hh=hh: e.tensor_copy(
                    out=self.maskrep[:, kb, hh * 128:(hh + 1) * 128], in_=self.masks_sb[:, mi, :]),
                    reads=[self.buf("masks")], pwrites=[self.buf("maskrep")])
        self.maskpair = self.alloc(2 * 256 * 2, BF16, [2, 256])
        for (t_, half, mi) in ((0, 0, 0), (0, 1, 3), (1, 0, 2), (1, 1, 0)):
            p.op("dve", lambda e, t_=t_, half=half, mi=mi: e.tensor_copy(
                out=self.maskpair[:, t_, half * 128:(half + 1) * 128], in_=self.masks_sb[:, mi, :]),
                reads=[self.buf("masks")], pwrites=[self.buf("maskpair")])
        self.uhalo = self.alloc(DEPTH * 88 * 2 * 4, F32, [DEPTH, 88, 2])
        p.op("dve", lambda e: e.memset(self.uhalo, 0.0), writes=[self.buf("uhalo")])
        self.persist_end = self.aoff

    def emit(self):
        self.conv_setup()
        self.load_consts()
        final = []
        for l in range(self.nlayers):
            self.convert_layer(l)
        for h in range(self.nhalves):
            for l in range(self.nlayers):
                for nm in "ABCDEFG":
                    getattr(self, "phase_" + nm)(l, h)
                    if self.stop == nm:
                        break
        allb = [b for b in self.bufs.values()]
        self.p.finish(allb)


_CACHE = {}


def _host_consts():
    if "c" in _CACHE:
        return _CACHE["c"]
    c = dict(_rope_tables())
    k = np.arange(128)[:, None]
    q = np.arange(128)[None, :]
    m = np.zeros((128, 4, 128), dtype=np.float32)
    m[:, 0, :] = np.where(k <= q, 0.0, NEG)
    m[:, 1, :] = np.where(k > q, 0.0, NEG)
    m[:, 2, :] = NEG
    c["masks"] = m.astype(ml_dtypes.bfloat16)
    c["ident"] = np.eye(128, dtype=np.float32).astype(ml_dtypes.bfloat16)
    _CACHE["c"] = c
    return c


def prepare_shared(inp):
    f = lambda a: np.ascontiguousarray(np.asarray(a, dtype=np.float32))
    sh = {}
    sh["w_in"] = np.ascontiguousarray(f(inp["w_in"])[:, :, _win_perm()])
    up = _wup_perm()
    sh["w_up"] = np.ascontiguousarray(f(inp["w_up"])[:, :, up])
    cw = f(inp["conv_w"])[:, :, up]
    sh["conv_w"] = np.ascontiguousarray(cw.reshape(DEPTH, 3, 88, 128).transpose(0, 3, 2, 1))
    cb = f(inp["conv_b"])[:, up]
    sh["conv_b"] = np.ascontiguousarray(cb.reshape(DEPTH, 88, 128).transpose(0, 2, 1))
    for k in ("w_proj_a", "w_proj_b", "w_out", "w_down", "sinks", "subln_g", "ln1_g", "ln1_b", "ln2_g", "ln2_b"):
        sh[k] = f(inp[k])
    sh["lam"] = np.ascontiguousarray(np.stack([f(inp["lambda_q1"]), f(inp["lambda_k1"]),
                                               f(inp["lambda_q2"]), f(inp["lambda_k2"])], axis=1))
    sh.update(_host_consts())
    return sh


def kernel(**inputs):
    x = np.asarray(inputs["x"], dtype=np.float32)
    sh = prepare_shared(inputs)
    kern = Kern()
    nc = kern.build()
    in_maps = []
    for c in range(NCORES):
        m = dict(sh)
        m["x"] = np.ascontiguousarray(x[c % BATCH])
        in_maps.append(m)
    res = run_bass_kernel_spmd(nc, in_maps, core_ids=list(range(NCORES)))
    out = np.stack([np.asarray(res.results[b]["y"], dtype=np.float32) for b in range(BATCH)], axis=0)
    return out
```

```python
import math
from contextlib import ExitStack

import numpy as np
import ml_dtypes

import concourse.bass as bass
import concourse.mybir as mybir
from concourse.bass_utils import run_bass_kernel_spmd

F32 = mybir.dt.float32
BF16 = mybir.dt.bfloat16
AF = mybir.ActivationFunctionType
ALU = mybir.AluOpType

D = 2048
SEQ = 4096
BATCH = 4
DEPTH = 2
HA, NQA, NKVA, GA = 64, 32, 4, 8
HB, NHB = 128, 8
DFF = 5632
INW = 12800
LN_EPS = 1e-5
SUB_EPS = 1e-5
ALPHA = (2 * DEPTH) ** 0.25
THETA = 10000.0
T = 2048
NH = SEQ // T
KC = D // 128
NEG = -30000.0

ENGS = ("pe", "act", "dve", "pool", "sp")
NCORES = 8


class Buf:
    __slots__ = ("name", "w", "r", "g")

    def __init__(self, name):
        self.name = name
        self.w = {}
        self.r = {}
        self.g = {}


def _merge(dst, src):
    for s, v in src.items():
        if dst.get(s, 0) < v:
            dst[s] = v


class Prog:
    DRING = 8

    def __init__(self, nc, stack):
        self.nc = nc
        self.st = stack
        self.q = {e: [] for e in ENGS}
        self.semh = {}
        self.cnt = {}
        self.seen = {e: {} for e in ENGS}
        for e in ("pe", "act", "dve", "pool"):
            self.newsem("P_" + e)
        self.dring = {}
        self.dpos = {}
        for e in ("sp", "act", "pool"):
            self.dring[e] = [self.newsem("D_%s_%d" % (e, i)) for i in range(self.DRING)]
            self.dpos[e] = 0
        self.nops = 0

    def newsem(self, name):
        self.semh[name] = self.st.enter_context(self.nc.semaphore(name))
        self.cnt[name] = 0
        return name

    def _need(self, eng, reads, writes, pwrites):
        need = {}
        for b in reads:
            _merge(need, b.w)
        for b in writes:
            _merge(need, b.g)
            _merge(need, b.w)
            _merge(need, b.r)
        for b in pwrites:
            if b.r:
                _merge(b.g, b.w)
                _merge(b.g, b.r)
                b.w = {}
                b.r = {}
            _merge(need, b.g)
        for s, v in need.items():
            if eng == "pe" and s == "P_pe":
                continue
            if self.seen[eng].get(s, 0) >= v:
                continue
            self.seen[eng][s] = v
            self.q[eng].append(("w", s, v))

    def _post(self, s, v, reads, writes, pwrites):
        for b in reads:
            if b.r.get(s, 0) < v:
                b.r[s] = v
        for b in writes:
            b.g = {}
            b.r = {}
            b.w = {s: v}
        for b in pwrites:
            if b.w.get(s, 0) < v:
                b.w[s] = v

    def op(self, eng, fn, reads=(), writes=(), pwrites=()):
        self._need(eng, reads, writes, pwrites)
        s = "P_" + eng
        self.cnt[s] += 1
        v = self.cnt[s]
        self.q[eng].append(("op", fn, s, 1))
        self._post(s, v, reads, writes, pwrites)
        self.nops += 1

    def pe(self, fns, reads=(), writes=(), pwrites=()):
        self._need("pe", reads, writes, pwrites)
        s = "P_pe"
        self.cnt[s] += 1
        v = self.cnt[s]
        for f in fns[:-1]:
            self.q["pe"].append(("op", f, None, 0))
        self.q["pe"].append(("op", fns[-1], s, 1))
        self._post(s, v, reads, writes, pwrites)
        self.nops += len(fns)

    def dma(self, eng, out, in_, reads=(), writes=(), pwrites=(), **kw):
        self._need(eng, reads, writes, pwrites)
        ring = self.dring[eng]
        s = ring[self.dpos[eng] % len(ring)]
        self.dpos[eng] += 1
        prev = self.cnt[s]
        if prev and self.seen[eng].get(s, 0) < prev:
            self.seen[eng][s] = prev
            self.q[eng].append(("w", s, prev))
        self.cnt[s] += 16
        v = self.cnt[s]
        self.q[eng].append(("op", lambda e, o=out, i=in_, k=kw: e.dma_start(out=o, in_=i, **k), s, 16))
        self._post(s, v, reads, writes, pwrites)
        self.nops += 1

    def finish(self, final_bufs):
        need = {}
        for b in final_bufs:
            _merge(need, b.w)
            _merge(need, b.g)
        for s, v in need.items():
            self.q["sp"].append(("w", s, v))

    def replay(self):
        nc = self.nc
        semh = self.semh

        def run(items, e):
            for it in items:
                if it[0] == "w":
                    e.wait_ge(semh[it[1]], it[2])
                else:
                    ins = it[1](e)
                    if it[2] is not None:
                        ins.then_inc(semh[it[2]], it[3])

        with nc.Block() as block:
            @block.sync
            def _(e):
                run(self.q["sp"], e)

            @block.tensor
            def _(e):
                run(self.q["pe"], e)

            @block.scalar
            def _(e):
                run(self.q["act"], e)

            @block.vector
            def _(e):
                run(self.q["dve"], e)

            @block.gpsimd
            def _(e):
                run(self.q["pool"], e)


def _win_perm():
    cols = []
    for i in range(8):
        for half in range(2):
            for hl in range(4):
                for d in range(32):
                    cols.append((4 * i + hl) * 64 + half * 32 + d)
    for half in range(2):
        for g in range(4):
            for d in range(32):
                cols.append(2048 + g * 64 + half * 32 + d)
    for base in (2560, 4608):
        for i in range(8):
            for half in range(2):
                for c in range(2):
                    for d in range(64):
                        cols.append(base + i * 256 + c * 128 + half * 64 + d)
    cols.extend(range(2304, 2560))
    cols.extend(range(6656, 8704))
    cols.extend(range(8704, 12800))
    assert len(cols) == INW and len(set(cols)) == INW
    return np.asarray(cols, dtype=np.int64)


def _wup_perm():
    cols = []
    for c in range(DFF // 128):
        cols.extend(range(c * 128, (c + 1) * 128))
        cols.extend(range(DFF + c * 128, DFF + (c + 1) * 128))
    return np.asarray(cols, dtype=np.int64)


def _rope_tables():
    out = {}
    t = np.arange(SEQ, dtype=np.float32)[None, :]
    for name, dim in (("A", HA), ("B", HB)):
        inv = (1.0 / (np.float32(THETA) ** (np.arange(0, dim, 2, dtype=np.float32) / np.float32(dim)))).astype(np.float32)
        p = np.arange(128) % (dim // 2)
        ang = (inv[p][:, None] * t).astype(np.float32)
        out["cos" + name] = np.cos(ang).astype(np.float32)
        out["sin" + name] = np.sin(ang).astype(np.float32)
    return out


BLK_QA0, BLK_KA, BLK_QB0, BLK_KB0, BLK_VA, BLK_VB0, BLK_G0 = 0, 8, 9, 17, 25, 26, 34
NBLK_IN = 50


class Kern:
    def __init__(self, debug=None, nlayers=DEPTH, nhalves=NH, stop=None):
        self.debug = debug or ()
        self.nlayers = nlayers
        self.nhalves = nhalves
        self.stop = stop
        self.nc = bass.Bass("TRN2", target_bir_lowering=False)
        self.bufs = {}

    def buf(self, name):
        b = self.bufs.get(name)
        if b is None:
            b = self.bufs[name] = Buf(name)
        return b

    def dram(self, name, shape, dt, kind="Internal"):
        if name in self.debug:
            kind = "ExternalOutput"
        return self.nc.dram_tensor(name, list(shape), dt, kind=kind).ap()

    def phase_begin(self):
        self.aoff = self.persist_end

    def alloc(self, nbytes, dt, shape=None, parts=128):
        nb0 = nbytes
        nbytes = (nbytes + 63) // 64 * 64
        off = self.aoff
        self.aoff += nbytes
        assert self.aoff <= self.ARENA, ("arena overflow", self.aoff)
        v = self.arena[0:parts, off // 4:(off + nbytes) // 4]
        if dt != F32:
            v = v.bitcast(dt)
            v = v[:, 0:nb0 // 2]
        else:
            v = v[:, 0:nb0 // 4]
        if shape is not None:
            names = " ".join("d%d" % i for i in range(len(shape)))
            kw = {"d%d" % i: int(s) for i, s in enumerate(shape)}
            v = v.rearrange("p (%s) -> p %s" % (names, names), **kw)
        return v

    def build(self):
        nc = self.nc
        with ExitStack() as st:
            self.st = st
            self.p = Prog(nc, st)
            self.ARENA = 168 * 1024
            self.arena = st.enter_context(nc.sbuf_tensor("arena", [128, self.ARENA // 4], F32))
            self.persist_end = 0
            self.aoff = 0
            self.tp_i = 0
            self.tp_two = True
            self.psf = [st.enter_context(nc.psum_tensor("psf%d" % i, [128, 512], F32)) for i in range(7)]
            self.psb = st.enter_context(nc.psum_tensor("psb", [128, 1024], BF16))
            self.psf6b = self.psf[6][:, :].bitcast(BF16)
            self.declare()
            self.emit()
            self.p.replay()
        return nc

    def declare(self):
        nc = self.nc
        ext = lambda n, s, dt=F32: nc.dram_tensor(n, list(s), dt, kind="ExternalInput").ap()
        self.x = ext("x", [SEQ, D])
        self.w_in = ext("w_in", [DEPTH, D, INW])
        self.w_pa = ext("w_proj_a", [DEPTH, D, D])
        self.w_pb = ext("w_proj_b", [DEPTH, D, D])
        self.w_out = ext("w_out", [DEPTH, D, D])
        self.w_up = ext("w_up", [DEPTH, D, 2 * DFF])
        self.w_down = ext("w_down", [DEPTH, DFF, D])
        self.sinks = ext("sinks", [DEPTH, NQA])
        self.lam = ext("lam", [DEPTH, 4, HB])
        self.subg = ext("subln_g", [DEPTH, 2 * HB])
        self.ln1g = ext("ln1_g", [DEPTH, D])
        self.ln1b = ext("ln1_b", [DEPTH, D])
        self.ln2g = ext("ln2_g", [DEPTH, D])
        self.ln2b = ext("ln2_b", [DEPTH, D])
        self.convw = ext("conv_w", [DEPTH, 128, 88, 3])
        self.convb = ext("conv_b", [DEPTH, 128, 88])
        self.cosA = ext("cosA", [128, SEQ])
        self.sinA = ext("sinA", [128, SEQ])
        self.cosB = ext("cosB", [128, SEQ])
        self.sinB = ext("sinB", [128, SEQ])
        self.masks = ext("masks", [128, 4, 128], BF16)
        self.ident = ext("ident", [128, 128], BF16)
        self.y = nc.dram_tensor("y", [SEQ, D], F32, kind="ExternalOutput").ap()

        self.wb_in = [self.dram("wb_in%d" % l, [NBLK_IN, 128, KC, 256], BF16) for l in range(DEPTH)]
        self.wb_pa = [self.dram("wb_pa%d" % l, [16, 128, KC, 128], BF16) for l in range(DEPTH)]
        self.wb_pb = [self.dram("wb_pb%d" % l, [16, 128, KC, 128], BF16) for l in range(DEPTH)]
        self.wb_out = [self.dram("wb_out%d" % l, [128, KC, D], BF16) for l in range(DEPTH)]
        self.wb_up = [self.dram("wb_up%d" % l, [44, 128, KC, 256], BF16) for l in range(DEPTH)]
        self.wb_down = [self.dram("wb_down%d" % l, [128, DFF // 128, D], BF16) for l in range(DEPTH)]
        self.QTa = self.dram("QTa", [NQA, HA, T], BF16)
        self.QTb = self.dram("QTb", [2 * NHB, HB, T], BF16)
        self.KTa = [self.dram("KTa%d" % l, [NKVA, HA, SEQ], BF16) for l in range(DEPTH)]
        self.KTb = [self.dram("KTb%d" % l, [2 * NHB, HB, SEQ], BF16) for l in range(DEPTH)]
        self.Va = [self.dram("Va%d" % l, [SEQ, NKVA * HA], BF16) for l in range(DEPTH)]
        self.Vb = [self.dram("Vb%d" % l, [SEQ, NHB * 2 * HB], BF16) for l in range(DEPTH)]
        self.GT = self.dram("GT", [2 * D, T], BF16)
        self.OTa = self.dram("OTa", [D, T], BF16)
        self.OTb = self.dram("OTb", [D, T], BF16)
        self.MT = self.dram("MT", [D, T], BF16)
        self.Hres = self.dram("Hres", [T, D], F32)
        self.GF = self.dram("GF", [DFF, T], BF16)
        self.Y1 = self.dram("Y1", [T, D], F32)
        self.Xres = self.dram("Xres", [T, D], F32)
        self.XTd = self.dram("XTd", [D, T], BF16)
        self.HTd = self.dram("HTd", [D, T], BF16)

    def barrier(self):
        p = self.p
        for eng in ("pe", "act", "dve", "sp"):
            for s, v in p.cnt.items():
                if s == "P_pool" or s.startswith("D_pool"):
                    continue
                if eng == "pe" and s == "P_pe":
                    continue
                if v and p.seen[eng].get(s, 0) < v:
                    p.seen[eng][s] = v
                    p.q[eng].append(("w", s, v))
        self.phase_begin()

    def wbuf(self, name, l, i):
        return self.buf("W_%s_%d_%d" % (name, l, i))

    def conv_setup(self):
        self.cv_f = [self.st.enter_context(self.nc.sbuf_tensor("cvf%d" % i, [128, 2048], F32)) for i in range(2)]
        self.cv_b = [self.st.enter_context(self.nc.sbuf_tensor("cvb%d" % i, [128, 2048], BF16)) for i in range(2)]
        self.cv_i = 0

    def conv_piece(self, src, dst, dst_shape3, wb):
        p = self.p
        i = self.cv_i % 2
        self.cv_i += 1
        n = src.shape[-1]
        f = self.cv_f[i][:, 0:n]
        b = self.cv_b[i][:, 0:n]
        bf_, bb_ = self.buf("cvf%d" % i), self.buf("cvb%d" % i)
        p.dma("pool", f, src, writes=[bf_])
        p.op("pool", lambda e, o=b, a=f: e.tensor_copy(out=o, in_=a), reads=[bf_], writes=[bb_])
        bsrc = b
        if dst_shape3 is not None:
            bsrc = b.rearrange("p (a n) -> p a n", a=dst_shape3)
        p.dma("pool", dst, bsrc, reads=[bb_], pwrites=[wb])

    def convert_layer(self, l):
        wsrc = self.w_in[l]
        for cg in range(7):
            c0 = cg * 2048
            n = min(2048, INW - c0)
            nb = n // 256
            for kc in range(KC):
                self.conv_piece(wsrc[kc * 128:(kc + 1) * 128, c0:c0 + n],
                                self.wb_in[l][8 * cg:8 * cg + nb, :, kc, :].rearrange("b p n -> p b n"),
                                nb, self.wbuf("in", l, cg))
        for nm, wsrc, wdst in (("pa", self.w_pa[l], self.wb_pa[l]), ("pb", self.w_pb[l], self.wb_pb[l])):
            for kc in range(KC):
                self.conv_piece(wsrc[kc * 128:(kc + 1) * 128, :],
                                wdst[:, :, kc, :].rearrange("c p n -> p c n"), 16, self.wbuf(nm, l, 0))
        for kc in range(KC):
            self.conv_piece(self.w_out[l][kc * 128:(kc + 1) * 128, :], self.wb_out[l][:, kc, :], None,
                            self.wbuf("out", l, 0))
        for cg in range(6):
            c0 = cg * 2048
            n = min(2048, 2 * DFF - c0)
            nb = n // 256
            for kc in range(KC):
                self.conv_piece(self.w_up[l][kc * 128:(kc + 1) * 128, c0:c0 + n],
                                self.wb_up[l][8 * cg:8 * cg + nb, :, kc, :].rearrange("b p n -> p b n"),
                                nb, self.wbuf("up", l, cg))
        for kc in range(DFF // 128):
            self.conv_piece(self.w_down[l][kc * 128:(kc + 1) * 128, :], self.wb_down[l][:, kc, :], None,
                            self.wbuf("down", l, kc // 4))

    def transpose_tile(self, src_bf, src_buf, dst3, dst_buf, tok_off, nchunks=16, pw=True):
        p = self.p
        ident, identB = self.ident_sb, self.buf("ident")
        for g0 in range(0, nchunks, 8):
            ng = min(8, nchunks - g0)
            if self.tp_two and self.tp_i % 2 == 1:
                psb, psbB = self.psf6b, self.buf("ps6")
            else:
                psb, psbB = self.psb, self.buf("psb")
            self.tp_i += 1
            fns = []
            for j in range(ng):
                c = g0 + j
                fns.append(lambda e, j=j, c=c, psb=psb: e.transpose(out=psb[:, j * 128:(j + 1) * 128],
                                                           in_=src_bf[:, c * 128:(c + 1) * 128], identity=ident))
            p.pe(fns, reads=[src_buf, identB], writes=[psbB])
            src = psb[:, 0:ng * 128].rearrange("p (c t) -> p c t", c=ng)
            dst = dst3[:, g0:g0 + ng, tok_off:tok_off + 128]
            kw = dict(reads=[psbB], pwrites=[dst_buf]) if pw else dict(reads=[psbB], writes=[dst_buf])
            eng = "act" if (g0 // 8) % 2 == 0 else "dve"
            if eng == "act":
                p.op("act", lambda e, o=dst, a=src: e.copy(out=o, in_=a), **kw)
            else:
                p.op("dve", lambda e, o=dst, a=src: e.tensor_copy(out=o, in_=a), **kw)

    def phase_A(self, l, h):
        p = self.p
        tok0 = h * T
        self.barrier()
        AT = self.alloc(KC * T * 2, BF16, [KC, T])
        ATb = self.buf("A_AT")
        wblk = [self.alloc(KC * 256 * 2, BF16, [KC, 256]) for _ in range(3)]
        wblkB = [self.buf("A_w%d" % i) for i in range(3)]
        tabs = {}
        for nm in ("cosA", "sinA", "cosB", "sinB"):
            tabs[nm] = self.alloc(T * 4, F32)
        tabB = self.buf("A_tabs")
        tmp = [[self.alloc(512 * 4, F32) for _ in range(4)] for _ in range(2)]
        tmpB = [[self.buf("A_tmp%d_%d" % (s, i)) for i in range(4)] for s in range(2)]
        stg = [self.alloc(T * 2, BF16) for _ in range(4)]
        stgB = [self.buf("A_stg%d" % i) for i in range(4)]
        nstg = [0]

        def next_stg():
            i = nstg[0] % 4
            nstg[0] += 1
            return stg[i], stgB[i]

        if l == 0:
            xf = [tabs["cosA"], tabs["sinA"]]
            xfB = [self.buf("A_xf%d" % i) for i in range(2)]
            xbv = tabs["cosB"].bitcast(BF16)
            xb = [xbv[:, 0:D], xbv[:, D:2 * D]]
            xbB = [self.buf("A_xb%d" % i) for i in range(2)]
            for tb in range(T // 128):
                i = tb % 2
                p.dma("sp", xf[i], self.x[tok0 + tb * 128:tok0 + (tb + 1) * 128, :], writes=[xfB[i]])
                if tb % 2 == 0:
                    p.op("dve", lambda e, o=xb[i], a=xf[i]: e.tensor_copy(out=o, in_=a), reads=[xfB[i]], writes=[xbB[i]])
                else:
                    p.op("act", lambda e, o=xb[i], a=xf[i]: e.copy(out=o, in_=a), reads=[xfB[i]], writes=[xbB[i]])
                self.transpose_tile(xb[i], xbB[i], AT, ATb, tb * 128)
        else:
            src = self.XTd.rearrange("(c p) t -> p c t", p=128)
            for c0 in range(0, KC, 4):
                p.dma("sp", AT[:, c0:c0 + 4, :], src[:, c0:c0 + 4, :], reads=[self.buf("XTd")], pwrites=[ATb])

        ov = {"cosA": ["A_xf0"], "sinA": ["A_xf1"], "cosB": ["A_xb0", "A_xb1"], "sinB": []}
        for nm in ("cosA", "sinA", "cosB", "sinB"):
            p.dma("sp", tabs[nm], getattr(self, nm)[:, tok0:tok0 + T], writes=[self.buf(n) for n in ov[nm]], pwrites=[tabB])

        psf = self.psf
        psB = [self.buf("ps%d" % i) for i in range(7)]
        pscur = [0]

        def next_ps():
            i = pscur[0] % 4
            pscur[0] += 1
            return psf[i], psB[i]

        wsrc = self.wb_in[l]

        def load_w(b):
            if b < NBLK_IN:
                p.dma("sp", wblk[b % 3], wsrc[b], reads=[self.wbuf("in", l, b // 8)], writes=[wblkB[b % 3]])

        load_w(0)
        load_w(1)
        for blk in range(NBLK_IN):
            wi = blk % 3
            W = wblk[wi]
            load_w(blk + 2)
            if blk < BLK_VA:
                if blk < BLK_KA:
                    cosT, sinT = tabs["cosA"], tabs["sinA"]
                elif blk == BLK_KA:
                    cosT, sinT = tabs["cosA"], tabs["sinA"]
                else:
                    cosT, sinT = tabs["cosB"], tabs["sinB"]
                s1, s1B = next_stg()
                s2, s2B = next_stg()
                for tt in range(4):
                    tsl = slice(tt * 512, (tt + 1) * 512)
                    pp = []
                    for ch in range(2):
                        ps, psb_ = next_ps()
                        fns = [lambda e, kc=kc, ps=ps, ch=ch, W=W, tsl=tsl: e.matmul(
                            ps[:, :], lhsT=W[:, kc, ch * 128:(ch + 1) * 128], rhs=AT[:, kc, tsl],
                            start=(kc == 0), stop=(kc == KC - 1)) for kc in range(KC)]
                        p.pe(fns, reads=[wblkB[wi], ATb], writes=[psb_])
                        pp.append((ps, psb_))
                    (P1, P1B), (P2, P2B) = pp
                    tm, tmB = tmp[tt % 2], tmpB[tt % 2]
                    TT = lambda o, a, b, op: (lambda e: e.tensor_tensor(out=o, in0=a, in1=b, op=op))
                    p.op("dve", TT(tm[0], P1[:, :], cosT[:, tsl], ALU.mult), reads=[P1B, tabB], writes=[tmB[0]])
                    p.op("dve", TT(tm[3], P1[:, :], sinT[:, tsl], ALU.mult), reads=[P1B, tabB], writes=[tmB[3]])
                    p.op("dve", TT(tm[1], P2[:, :], sinT[:, tsl], ALU.mult), reads=[P2B, tabB], writes=[tmB[1]])
                    p.op("dve", TT(tm[2], P2[:, :], cosT[:, tsl], ALU.mult), reads=[P2B, tabB], writes=[tmB[2]])
                    p.op("dve", TT(s1[:, tsl], tm[0], tm[1], ALU.subtract), reads=[tmB[0], tmB[1]], pwrites=[s1B])
                    p.op("dve", TT(s2[:, tsl], tm[2], tm[3], ALU.add), reads=[tmB[2], tmB[3]], pwrites=[s2B])
                if blk < BLK_KA:
                    i = blk
                    for hl in range(4):
                        p.dma("sp", self.QTa[4 * i + hl, 0:32, :], s1[32 * hl:32 * hl + 32, :], reads=[s1B], pwrites=[self.buf("QTa")])
                        p.dma("sp", self.QTa[4 * i + hl, 32:64, :], s2[32 * hl:32 * hl + 32, :], reads=[s2B], pwrites=[self.buf("QTa")])
                elif blk == BLK_KA:
                    for g in range(4):
                        p.dma("sp", self.KTa[l][g, 0:32, tok0:tok0 + T], s1[32 * g:32 * g + 32, :], reads=[s1B], pwrites=[self.buf("KTa%d" % l)])
                        p.dma("sp", self.KTa[l][g, 32:64, tok0:tok0 + T], s2[32 * g:32 * g + 32, :], reads=[s2B], pwrites=[self.buf("KTa%d" % l)])
                else:
                    isq = blk < BLK_KB0
                    i = blk - (BLK_QB0 if isq else BLK_KB0)
                    for c in range(2):
                        if isq:
                            d1, d2, db = self.QTb[2 * i + c, 0:64, :], self.QTb[2 * i + c, 64:128, :], self.buf("QTb")
                        else:
                            d1 = self.KTb[l][2 * i + c, 0:64, tok0:tok0 + T]
                            d2 = self.KTb[l][2 * i + c, 64:128, tok0:tok0 + T]
                            db = self.buf("KTb%d" % l)
                        p.dma("sp", d1, s1[64 * c:64 * c + 64, :], reads=[s1B], pwrites=[db])
                        p.dma("sp", d2, s2[64 * c:64 * c + 64, :], reads=[s2B], pwrites=[db])
            elif blk < BLK_G0:
                if blk == BLK_VA:
                    vdst, vb_, c0 = self.Va[l], self.buf("Va%d" % l), 0
                else:
                    vdst, vb_, c0 = self.Vb[l], self.buf("Vb%d" % l), (blk - BLK_VB0) * 256
                for hh in range(2):
                    s, sB = next_stg()
                    s3 = s.rearrange("p (a n) -> p a n", a=8)
                    for j in range(8):
                        tb = hh * 8 + j
                        ps, psb_ = next_ps()
                        fns = [lambda e, kc=kc, ps=ps, tb=tb, W=W: e.matmul(
                            ps[:, 0:256], lhsT=AT[:, kc, tb * 128:(tb + 1) * 128], rhs=W[:, kc, :],
                            start=(kc == 0), stop=(kc == KC - 1)) for kc in range(KC)]
                        p.pe(fns, reads=[wblkB[wi], ATb], writes=[psb_])
                        p.op("act", lambda e, o=s3[:, j, :], a=ps[:, 0:256]: e.copy(out=o, in_=a), reads=[psb_], pwrites=[sB])
                    t0 = tok0 + hh * 1024
                    p.dma("sp", vdst[t0:t0 + 1024, c0:c0 + 256].rearrange("(a p) n -> p a n", p=128), s3,
                          reads=[sB], pwrites=[vb_])
            else:
                for ch in range(2):
                    s, sB = next_stg()
                    for tt in range(4):
                        tsl = slice(tt * 512, (tt + 1) * 512)
                        ps, psb_ = next_ps()
                        fns = [lambda e, kc=kc, ps=ps, ch=ch, W=W, tsl=tsl: e.matmul(
                            ps[:, :], lhsT=W[:, kc, ch * 128:(ch + 1) * 128], rhs=AT[:, kc, tsl],
                            start=(kc == 0), stop=(kc == KC - 1)) for kc in range(KC)]
                        p.pe(fns, reads=[wblkB[wi], ATb], writes=[psb_])
                        p.op("act", lambda e, o=s[:, tsl], a=ps[:, :]: e.activation(out=o, in_=a, func=AF.Sigmoid),
                             reads=[psb_], pwrites=[sB])
                    r0 = (blk - BLK_G0) * 256 + ch * 128
                    p.dma("sp", self.GT[r0:r0 + 128, :], s, reads=[sB], pwrites=[self.buf("GT")])

    def phase_B(self, l, h):
        p = self.p
        tok0 = h * T
        self.barrier()
        NKB = 17
        k0 = tok0 - 128
        QT = [self.alloc(8 * T * 2, BF16, [8, T], parts=64) for _ in range(2)]
        KT = [self.alloc(NKB * 128 * 2, BF16, parts=64) for _ in range(2)]
        QTB = [self.buf("B_QT%d" % i) for i in range(2)]
        KTB = [self.buf("B_KT%d" % i) for i in range(2)]
        V = self.alloc(NKB * 4 * 65 * 2, BF16, [NKB, 4, 65])
        VB = self.buf("B_V")
        PT = [[self.alloc(1024 * 2, BF16) for _ in range(2)] for _ in range(2)]
        PTB = [[self.buf("B_PT%d_%d" % (r, k)) for k in range(2)] for r in range(2)]
        ot = [self.alloc(512 * 2, BF16) for _ in range(2)]
        otB = [self.buf("B_ot%d" % i) for i in range(2)]
        OTs = [self.alloc(4 * T * 2, BF16, [4, T]) for _ in range(2)]
        OTsB = [self.buf("B_OTs%d" % i) for i in range(2)]
        esink = self.alloc(32 * 4, F32)
        esB = self.buf("B_esink")
        den = [self.alloc(8 * 4, F32) for _ in range(2)]
        denB = [self.buf("B_den%d" % i) for i in range(2)]
        psf = self.psf
        psB = [self.buf("ps%d" % i) for i in range(7)]
        scale = HA ** -0.5

        p.dma("sp", esink, self.sinks[l:l + 1, :].broadcast_to([128, NQA]), writes=[esB])
        p.op("act", lambda e: e.activation(out=esink, in_=esink, func=AF.Exp), reads=[esB], writes=[esB])
        p.op("dve", lambda e: e.memset(V[:, :, :, 64:65], 1.0), pwrites=[VB])
        kb_lo = 1 if h == 0 else 0
        for g in range(NKVA):
            src = self.Va[l][max(k0, 0):tok0 + T, g * 64:(g + 1) * 64].rearrange("(a p) n -> p a n", p=128)
            p.dma("sp", V[:, kb_lo:NKB, g, 0:64], src, reads=[self.buf("Va%d" % l)], pwrites=[VB])

        sring = [0]
        pend = []

        pend_t = []

        def emit_pv(item):
            g, j, kbs, r, gi = item
            Oa, Ob_ = psf[4], psf[5]
            for i in range(8):
                bank = Oa if i < 4 else Ob_
                out = bank[:, (i % 4) * 65:(i % 4) * 65 + 65]
                fns = []
                for n_, kb in enumerate(kbs):
                    kblk = j + kb
                    fns.append(lambda e, out=out, kb=kb, kblk=kblk, i=i, first=(n_ == 0), last=(n_ == len(kbs) - 1):
                               e.matmul(out, lhsT=PT[r][kb][:, i * 128:(i + 1) * 128], rhs=V[:, kblk, g, :],
                                        start=first, stop=last))
                p.pe(fns, reads=[PTB[r][kb] for kb in kbs] + [VB], pwrites=[psB[4 if i < 4 else 5]])
            flush_t()
            o = ot[r]
            for bi, bank in enumerate((Oa, Ob_)):
                O3 = bank[:, 0:260].rearrange("p (a n) -> p a n", a=4)
                dn = den[r][:, bi * 4:(bi + 1) * 4]
                hs = 8 * g + bi * 4
                p.op("dve", lambda e, dn=dn, O3=O3, hs=hs: e.tensor_tensor(
                    out=dn, in0=O3[:, :, 64], in1=esink[:, hs:hs + 4], op=ALU.add),
                    reads=[psB[4 + bi], esB], pwrites=[denB[r]])
                p.op("dve", lambda e, dn=dn: e.reciprocal(out=dn, in_=dn), reads=[denB[r]], pwrites=[denB[r]])
                o3 = o[:, bi * 256:(bi + 1) * 256].rearrange("p (a n) -> p a n", a=4)
                p.op("dve", lambda e, dn=dn, O3=O3, o3=o3: e.tensor_tensor(
                    out=o3, in0=O3[:, :, 0:64], in1=dn.unsqueeze(2).broadcast_to([128, 4, 64]), op=ALU.mult),
                    reads=[psB[4 + bi], denB[r]], pwrites=[otB[r]])
            pend_t.append((o, otB[r], OTs[gi], OTsB[gi], j * 128))

        def flush_t():
            while pend_t:
                a = pend_t.pop(0)
                self.transpose_tile(a[0], a[1], a[2], a[3], a[4], nchunks=4)

        def load_grp(g):
            if g >= NKVA:
                return
            gi = g % 2
            p.dma("sp", QT[gi], self.QTa[8 * g:8 * g + 8, :, :].rearrange("h d t -> d h t"),
                  reads=[self.buf("QTa")], writes=[QTB[gi]])
            if h == 0:
                p.dma("sp", KT[gi][:, 128:NKB * 128], self.KTa[l][g, :, 0:T], reads=[self.buf("KTa%d" % l)], writes=[KTB[gi]])
            else:
                p.dma("sp", KT[gi], self.KTa[l][g, :, k0:tok0 + T], reads=[self.buf("KTa%d" % l)], writes=[KTB[gi]])

        load_grp(0)
        for g in range(NKVA):
            gi = g % 2
            load_grp(g + 1)
            for j in range(T // 128):
                gq = h * 16 + j
                kbs = [1] if gq == 0 else [0, 1]
                r = j % 2
                for kb in kbs:
                    for hh in range(2):
                        bi = sring[0] % 4
                        sring[0] += 1
                        bank = psf[bi]
                        kcol = (j + kb) * 128
                        fns = [
                            lambda e, bank=bank, kcol=kcol, hh=hh, j=j, gi=gi: e.matmul(
                                bank[:, :], lhsT=KT[gi][:, kcol:kcol + 128],
                                rhs=QT[gi][:, 4 * hh:4 * hh + 4, j * 128:(j + 1) * 128], start=True, stop=False),
                            lambda e, bank=bank, kb=kb: e.matmul(
                                bank[:, :], lhsT=self.ident_sb, rhs=self.maskrep[:, kb, :], start=False, stop=True),
                        ]
                        p.pe(fns, reads=[KTB[gi], QTB[gi], self.buf("ident"), self.buf("maskrep")], writes=[psB[bi]])
                        p.op("act", lambda e, bank=bank, r=r, kb=kb, hh=hh: e.activation(
                            out=PT[r][kb][:, hh * 512:(hh + 1) * 512], in_=bank[:, :], func=AF.Exp, scale=scale),
                            reads=[psB[bi]], pwrites=[PTB[r][kb]])
                pend.append((g, j, kbs, r, gi))
                if len(pend) > 1:
                    emit_pv(pend.pop(0))
            while pend:
                emit_pv(pend.pop(0))
            flush_t()
            p.dma("sp", self.OTa[g * 512:(g + 1) * 512, :].rearrange("(c p) t -> p c t", p=128), OTs[gi],
                  reads=[OTsB[gi]], pwrites=[self.buf("OTa")])

    def phase_C(self, l, h):
        self.tp_two = False
        try:
            self._phase_C(l, h)
        finally:
            self.tp_two = True

    def _phase_C(self, l, h):
        p = self.p
        tok0 = h * T
        self.barrier()
        nk = tok0 + T
        nkb = nk // 128
        KT = [self.alloc(2 * SEQ * 2, BF16, [2, SEQ]) for _ in range(2)]
        QT = [self.alloc(2 * T * 2, BF16, [2, T]) for _ in range(2)]
        V = [self.alloc(32 * 257 * 2, BF16, [32, 257]) for _ in range(2)]
        KTB = [self.buf("C_KT%d" % i) for i in range(2)]
        QTB = [self.buf("C_QT%d" % i) for i in range(2)]
        VB = [self.buf("C_V%d" % i) for i in range(2)]
        PT = [self.alloc(512 * 2, BF16) for _ in range(3)]
        PTB = [self.buf("C_PT%d" % i) for i in range(3)]
        OTs = [self.alloc(2 * T * 2, BF16, [2, T]) for _ in range(2)]
        OTsB = [self.buf("C_OTs%d" % i) for i in range(2)]
        tf = [self.alloc(256 * 4, F32) for _ in range(2)]
        tfB = [self.buf("C_tf%d" % i) for i in range(2)]
        of = [self.alloc(256 * 4, F32) for _ in range(2)]
        ofB = [self.buf("C_of%d" % i) for i in range(2)]
        junk = self.alloc(256 * 4, F32)
        junkB = self.buf("C_junk")
        ob = [self.alloc(256 * 2, BF16) for _ in range(2)]
        obB = [self.buf("C_ob%d" % i) for i in range(2)]
        sm = [self.alloc(8 * 4, F32) for _ in range(2)]
        OC = [[[self.alloc(257 * 4, F32) for _ in range(2)] for _ in range(2)] for _ in range(2)]
        OCB = [[[self.buf("C_OC%d_%d_%d" % (s, c, q)) for q in range(2)] for c in range(2)] for s in range(2)]
        OFs = [self.alloc(16 * 256 * 4, F32, [16, 256]) for _ in range(2)]
        OFBs = [self.buf("C_OF%d" % i) for i in range(2)]
        sss = [self.alloc(64 * 4, F32) for _ in range(2)]
        ssBs = [self.buf("C_ss%d" % i) for i in range(2)]
        smB = [self.buf("C_sm%d" % i) for i in range(2)]
        lamt = self.alloc(4 * HB * 4, F32, [4, HB])
        lamB = self.buf("C_lamt")
        lsc = self.alloc(8 * 4, F32)
        lscB = self.buf("C_lsc")
        subg = self.alloc(256 * 4, F32)
        subgB = self.buf("C_subg")
        psf = self.psf
        psB = [self.buf("ps%d" % i) for i in range(7)]
        scale = HB ** -0.5
        lam_init = 0.8 - 0.6 * math.exp(-0.3 * l)

        p.dma("sp", lamt, self.lam[l:l + 1, :, :].broadcast_to([128, 4, HB]), writes=[lamB])
        p.dma("sp", subg, self.subg[l:l + 1, :].broadcast_to([128, 2 * HB]), writes=[subgB])
        for i in range(2):
            p.op("dve", lambda e, i=i: e.scalar_tensor_tensor(
                out=junk[:, 0:HB], in0=lamt[:, 2 * i, :], scalar=1.0, in1=lamt[:, 2 * i + 1, :],
                op0=ALU.mult, op1=ALU.mult, accum_out=lsc[:, i:i + 1]),
                reads=[lamB], writes=[junkB], pwrites=[lscB])
        p.op("act", lambda e: e.activation(out=lsc[:, 0:2], in_=lsc[:, 0:2], func=AF.Exp), reads=[lscB], writes=[lscB])
        p.op("dve", lambda e: e.scalar_tensor_tensor(
            out=lsc[:, 2:3], in0=lsc[:, 1:2], scalar=-lam_init, in1=lsc[:, 0:1], op0=ALU.add, op1=ALU.subtract),
            reads=[lscB], writes=[lscB])
        for i in range(2):
            p.op("dve", lambda e, i=i: e.memset(V[i][:, :, 256:257], 1.0), pwrites=[VB[i]])
        p.op("dve", lambda e: e.tensor_scalar(out=subg, in0=subg, scalar1=(1.0 - lam_init), scalar2=None, op0=ALU.mult),
             reads=[subgB], writes=[subgB])

        sring = [0]
        Oacc = [[psf[2], psf[3]], [psf[4], psf[5]]]
        OaccB = [[psB[2], psB[3]], [psB[4], psB[5]]]
        SB = [0, 1, 6]
        fin = [0]

        def load_head(hd):
            if hd >= NHB:
                return
            hi = hd % 2
            for c in range(2):
                p.dma("sp", KT[hi][:, c, 0:nk], self.KTb[l][2 * hd + c, :, 0:nk], reads=[self.buf("KTb%d" % l)], pwrites=[KTB[hi]])
                p.dma("sp", QT[hi][:, c, :], self.QTb[2 * hd + c, :, :], reads=[self.buf("QTb")], pwrites=[QTB[hi]])
            p.dma("sp", V[hi][:, 0:nkb, 0:256],
                  self.Vb[l][0:nk, hd * 256:(hd + 1) * 256].rearrange("(a p) n -> p a n", p=128),
                  reads=[self.buf("Vb%d" % l)], pwrites=[VB[hi]])

        ep_steps = []

        def run_steps(n):
            for _ in range(n):
                if ep_steps:
                    ep_steps.pop(0)()

        def epilogue(hd):
            run_steps(len(ep_steps))
            hi = hd % 2
            OF, OFB, ss, ssB = OFs[hi], OFBs[hi], sss[hi], ssBs[hi]

            def s0():
                p.op("dve", lambda e: e.tensor_scalar(out=ss[:, 16:32], in0=ss[:, 0:16], scalar1=1.0 / 256.0, scalar2=SUB_EPS,
                                                      op0=ALU.mult, op1=ALU.add), reads=[ssB], pwrites=[ssB])
                self.rsqrt(ss[:, 16:32], ss[:, 32:48], ss[:, 48:64], ssB)
            ep_steps.append(s0)
            for jb in range(T // 128):
                def sj(jb=jb):
                    f = jb % 2
                    p.op("dve", lambda e, f=f, jb=jb: e.scalar_tensor_tensor(
                        out=ob[f], in0=OF[:, jb, :], scalar=ss[:, 32 + jb:33 + jb], in1=subg, op0=ALU.mult, op1=ALU.mult),
                        reads=[OFB, ssB, subgB], writes=[obB[f]])
                    self.transpose_tile(ob[f], obB[f], OTs[hi], OTsB[hi], jb * 128, nchunks=2)
                ep_steps.append(sj)

            def sl():
                p.dma("sp", self.OTb[hd * 256:(hd + 1) * 256, :].rearrange("(c p) t -> p c t", p=128), OTs[hi],
                      reads=[OTsB[hi]], pwrites=[self.buf("OTb")])
            ep_steps.append(sl)

        load_head(0)
        for hd in range(NHB):
            hi = hd % 2
            OF, OFB, ss, ssB = OFs[hi], OFBs[hi], sss[hi], ssBs[hi]
            load_head(hd + 1)
            for jp in range(T // 256):
                gq0 = h * 16 + 2 * jp
                nkbs = gq0 + 2
                pend = []

                def emit_pv(item):
                    kb, pi = item
                    for c in range(2):
                        for qq in range(2):
                            if kb == gq0 + 1 and qq == 0:
                                continue
                            last = (kb == gq0 + qq)
                            p.pe([lambda e, c=c, qq=qq, kb=kb, pi=pi, last=last, hi=hi: e.matmul(
                                Oacc[c][qq][:, 0:257], lhsT=PT[pi][:, c * 256 + qq * 128:c * 256 + qq * 128 + 128],
                                rhs=V[hi][:, kb, :], start=(kb == 0), stop=last)],
                                reads=[PTB[pi], VB[hi]], pwrites=[OaccB[c][qq]])

                ran = 0
                for kb in range(nkbs):
                    bi = SB[sring[0] % 3]
                    pi = sring[0] % 3
                    sring[0] += 1
                    bank = psf[bi]
                    masked = kb >= gq0
                    fns = []
                    for c in range(2):
                        fns.append(lambda e, bank=bank, c=c, kb=kb, masked=masked, hi=hi, jp=jp: e.matmul(
                            bank[:, c * 256:(c + 1) * 256], lhsT=KT[hi][:, c, kb * 128:(kb + 1) * 128],
                            rhs=QT[hi][:, c, jp * 256:(jp + 1) * 256], start=True, stop=(not masked)))
                        if masked:
                            mt = kb - gq0
                            fns.append(lambda e, bank=bank, c=c, mt=mt: e.matmul(
                                bank[:, c * 256:(c + 1) * 256], lhsT=self.ident_sb, rhs=self.maskpair[:, mt, :],
                                start=False, stop=True))
                    p.pe(fns, reads=[KTB[hi], QTB[hi], self.buf("ident"), self.buf("maskpair")], writes=[psB[bi]])
                    p.op("act", lambda e, bank=bank, pi=pi: e.activation(out=PT[pi], in_=bank[:, :], func=AF.Exp, scale=scale),
                         reads=[psB[bi]], writes=[PTB[pi]])
                    pend.append((kb, pi))
                    if len(pend) > 2:
                        emit_pv(pend.pop(0))
                    if nkbs >= 8 and kb in (nkbs // 4, nkbs // 2, (3 * nkbs) // 4):
                        run_steps(1)
                        ran += 1

                while pend:
                    emit_pv(pend.pop(0))
                run_steps(3 - ran)
                oset = (hd * (T // 256) + jp) % 2
                for c in range(2):
                    for qq in range(2):
                        src_ = Oacc[c][qq][:, 0:257]
                        dst_ = OC[oset][c][qq]
                        p.op("dve", lambda e, o=dst_, a=src_: e.tensor_copy(out=o, in_=a), reads=[OaccB[c][qq]], writes=[OCB[oset][c][qq]])
                for qq in range(2):
                    f = fin[0] % 2
                    fin[0] += 1
                    O1, O2 = OC[oset][0][qq], OC[oset][1][qq]
                    O1B, O2B = OCB[oset][0][qq], OCB[oset][1][qq]
                    s_ = sm[f]
                    p.op("dve", lambda e, s_=s_, O1=O1: e.reciprocal(out=s_[:, 0:1], in_=O1[:, 256:257]), reads=[O1B], pwrites=[smB[f]])
                    p.op("dve", lambda e, s_=s_, O2=O2: e.reciprocal(out=s_[:, 1:2], in_=O2[:, 256:257]), reads=[O2B], pwrites=[smB[f]])
                    p.op("dve", lambda e, s_=s_: e.tensor_tensor(out=s_[:, 1:2], in0=s_[:, 1:2], in1=lsc[:, 2:3], op=ALU.mult),
                         reads=[smB[f], lscB], pwrites=[smB[f]])
                    p.op("dve", lambda e, s_=s_, O1=O1, f=f: e.tensor_scalar(
                        out=tf[f], in0=O1[:, 0:256], scalar1=s_[:, 0:1], scalar2=None, op0=ALU.mult),
                        reads=[O1B, smB[f]], writes=[tfB[f]])
                    jb = 2 * jp + qq
                    p.op("dve", lambda e, s_=s_, O2=O2, f=f, jb=jb, OF=OF: e.scalar_tensor_tensor(
                        out=OF[:, jb, :], in0=O2[:, 0:256], scalar=s_[:, 1:2], in1=tf[f], op0=ALU.mult, op1=ALU.add),
                        reads=[O2B, smB[f], tfB[f]], pwrites=[OFB])
                    p.op("dve", lambda e, jb=jb, OF=OF, ss=ss: e.scalar_tensor_tensor(
                        out=junk, in0=OF[:, jb, :], scalar=1.0, in1=OF[:, jb, :], op0=ALU.mult, op1=ALU.mult,
                        accum_out=ss[:, jb:jb + 1]), reads=[OFB], writes=[junkB], pwrites=[ssB])
                if jp == 0 and hd > 0:
                    epilogue(hd - 1)
        epilogue(NHB - 1)
        run_steps(len(ep_steps))

    def phase_D(self, l, h):
        p = self.p
        self.barrier()
        TS = 1024
        OA = self.alloc(KC * TS * 2, BF16, [KC, TS])
        OB = self.alloc(KC * TS * 2, BF16, [KC, TS])
        OAB, OBB = self.buf("D_OA"), self.buf("D_OB")
        wa = [self.alloc(KC * 128 * 2, BF16, [KC, 128]) for _ in range(3)]
        wb = [self.alloc(KC * 128 * 2, BF16, [KC, 128]) for _ in range(3)]
        waB = [self.buf("D_wa%d" % i) for i in range(3)]
        wbB = [self.buf("D_wb%d" % i) for i in range(3)]
        ga = [self.alloc(TS * 2, BF16) for _ in range(3)]
        gb = [self.alloc(TS * 2, BF16) for _ in range(3)]
        gaB = [self.buf("D_ga%d" % i) for i in range(3)]
        gbB = [self.buf("D_gb%d" % i) for i in range(3)]
        t1 = [self.alloc(512 * 4, F32) for _ in range(2)]
        t2 = [self.alloc(512 * 4, F32) for _ in range(2)]
        t1B = [self.buf("D_t1%d" % i) for i in range(2)]
        t2B = [self.buf("D_t2%d" % i) for i in range(2)]
        ms = [self.alloc(TS * 2, BF16) for _ in range(3)]
        msB = [self.buf("D_ms%d" % i) for i in range(3)]
        psf = self.psf
        psB = [self.buf("ps%d" % i) for i in range(7)]
        pr = [0]
        it = [0]
        NIT = (T // TS) * KC

        def load_it(k):
            if k >= NIT:
                return
            s_, c_ = k // KC, k % KC
            i_ = k % 3
            t_ = s_ * TS
            p.dma("sp", wa[i_], self.wb_pa[l][c_], reads=[self.wbuf("pa", l, 0)], writes=[waB[i_]])
            p.dma("sp", wb[i_], self.wb_pb[l][c_], reads=[self.wbuf("pb", l, 0)], writes=[wbB[i_]])
            p.dma("sp", ga[i_], self.GT[c_ * 128:(c_ + 1) * 128, t_:t_ + TS], reads=[self.buf("GT")], writes=[gaB[i_]])
            p.dma("sp", gb[i_], self.GT[D + c_ * 128:D + (c_ + 1) * 128, t_:t_ + TS], reads=[self.buf("GT")], writes=[gbB[i_]])

        load_it(0)
        load_it(1)
        for sub in range(T // TS):
            ts0 = sub * TS
            for c0 in range(0, KC, 4):
                p.dma("sp", OA[:, c0:c0 + 4, :], self.OTa[c0 * 128:(c0 + 4) * 128, ts0:ts0 + TS].rearrange("(c p) t -> p c t", p=128),
                      reads=[self.buf("OTa")], pwrites=[OAB])
                p.dma("sp", OB[:, c0:c0 + 4, :], self.OTb[c0 * 128:(c0 + 4) * 128, ts0:ts0 + TS].rearrange("(c p) t -> p c t", p=128),
                      reads=[self.buf("OTb")], pwrites=[OBB])
            for c in range(KC):
                i = it[0] % 3
                load_it(it[0] + 2)
                it[0] += 1
                for tt in range(TS // 512):
                    tsl = slice(tt * 512, (tt + 1) * 512)
                    banks = []
                    for (W, WB, O, OBf) in ((wa[i], waB[i], OA, OAB), (wb[i], wbB[i], OB, OBB)):
                        bi = pr[0] % 6
                        pr[0] += 1
                        ps = psf[bi]
                        fns = [lambda e, kc=kc, ps=ps, W=W, O=O, tsl=tsl: e.matmul(
                            ps[:, :], lhsT=W[:, kc, :], rhs=O[:, kc, tsl], start=(kc == 0), stop=(kc == KC - 1))
                            for kc in range(KC)]
                        p.pe(fns, reads=[WB, OBf], writes=[psB[bi]])
                        banks.append((ps, psB[bi]))
                    (pa, paB), (pb_, pbB) = banks
                    k = tt % 2
                    p.op("dve", lambda e, k=k, pa=pa, i=i, tsl=tsl: e.tensor_tensor(out=t1[k], in0=pa[:, :], in1=ga[i][:, tsl], op=ALU.mult),
                         reads=[paB, gaB[i]], writes=[t1B[k]])
                    p.op("dve", lambda e, k=k, pb_=pb_, i=i, tsl=tsl: e.tensor_tensor(out=t2[k], in0=pb_[:, :], in1=gb[i][:, tsl], op=ALU.mult),
                         reads=[pbB, gbB[i]], writes=[t2B[k]])
                    p.op("dve", lambda e, k=k, i=i, tsl=tsl: e.tensor_tensor(out=ms[i][:, tsl], in0=t1[k], in1=t2[k], op=ALU.add),
                         reads=[t1B[k], t2B[k]], pwrites=[msB[i]])
                p.dma("sp", self.MT[c * 128:(c + 1) * 128, ts0:ts0 + TS], ms[i], reads=[msB[i]], pwrites=[self.buf("MT")])

    def rsqrt(self, v, r, t, B, iters=2):
        p = self.p
        p.op("act", lambda e: e.activation(out=r, in_=v, func=AF.Sqrt), reads=[B], pwrites=[B])
        p.op("dve", lambda e: e.reciprocal(out=r, in_=r), reads=[B], pwrites=[B])
        for _ in range(iters):
            p.op("dve", lambda e: e.tensor_tensor(out=t, in0=r, in1=r, op=ALU.mult), reads=[B], pwrites=[B])
            p.op("dve", lambda e: e.tensor_tensor(out=t, in0=t, in1=v, op=ALU.mult), reads=[B], pwrites=[B])
            p.op("dve", lambda e: e.tensor_scalar(out=t, in0=t, scalar1=-0.5, scalar2=1.5, op0=ALU.mult, op1=ALU.add),
                 reads=[B], pwrites=[B])
            p.op("dve", lambda e: e.tensor_tensor(out=r, in0=r, in1=t, op=ALU.mult), reads=[B], pwrites=[B])

    def rsqrt1(self, v, r, t, B, seed=True):
        p = self.p
        if seed:
            p.op("act", lambda e: e.activation(out=r, in_=v, func=AF.Sqrt), reads=[B], pwrites=[B])
        p.op("dve", lambda e: e.reciprocal(out=r, in_=r), reads=[B], pwrites=[B])
        for _ in range(2):
            p.op("dve", lambda e: e.scalar_tensor_tensor(out=t, in0=r, scalar=v, in1=r, op0=ALU.mult, op1=ALU.mult),
                 reads=[B], pwrites=[B])
            p.op("dve", lambda e: e.tensor_scalar(out=t, in0=t, scalar1=-0.5, scalar2=1.5, op0=ALU.mult, op1=ALU.add),
                 reads=[B], pwrites=[B])
            p.op("dve", lambda e: e.tensor_tensor(out=r, in0=r, in1=t, op=ALU.mult), reads=[B], pwrites=[B])

    def ln_tile(self, y, yB, gt, bt, gbB, st, stB, hbf, hbfB):
        p = self.p
        stats = st[:, 0:24].rearrange("p (a n) -> p a n", a=4)
        mv = st[:, 24:26]
        rstd = st[:, 26:27]
        nb = st[:, 27:28]
        for n in range(4):
            p.op("dve", lambda e, n=n: e.bn_stats(out=stats[:, n, :], in_=y[:, n * 512:(n + 1) * 512]), reads=[yB], pwrites=[stB])
        p.op("dve", lambda e: e.bn_aggr(out=mv, in_=st[:, 0:24]), reads=[stB], pwrites=[stB])
        p.op("dve", lambda e: e.tensor_scalar(out=st[:, 28:29], in0=st[:, 25:26], scalar1=LN_EPS, scalar2=None, op0=ALU.add),
             reads=[stB], pwrites=[stB])
        p.op("act", lambda e: e.activation(out=rstd, in_=st[:, 28:29], func=AF.Sqrt), reads=[stB], pwrites=[stB])
        p.op("dve", lambda e: e.scalar_tensor_tensor(out=y, in0=y, scalar=st[:, 24:25], in1=gt, op0=ALU.subtract, op1=ALU.mult),
             reads=[yB, stB, gbB], writes=[yB])
        self.rsqrt1(st[:, 28:29], rstd, st[:, 29:30], stB, seed=False)
        p.op("dve", lambda e: e.scalar_tensor_tensor(out=y, in0=y, scalar=rstd, in1=bt, op0=ALU.mult, op1=ALU.add),
             reads=[yB, stB, gbB], writes=[yB])
        if hbf is not None:
            p.op("act", lambda e: e.copy(out=hbf, in_=y), reads=[yB], writes=[hbfB])

    def phase_E(self, l, h):
        p = self.p
        tok0 = h * T
        self.barrier()
        TS = 512
        Wo = self.alloc(KC * D * 2, BF16, [KC, D])
        WoB = self.buf("E_Wo")
        Ms = [self.alloc(KC * TS * 2, BF16, [KC, TS]) for _ in range(2)]
        MBs = [self.buf("E_M%d" % i) for i in range(2)]
        gt = self.alloc(D * 4, F32)
        bt = self.alloc(D * 4, F32)
        gbB = self.buf("E_gb")
        yt = [self.alloc(D * 4, F32) for _ in range(3)]
        ytB = [self.buf("E_y%d" % i) for i in range(3)]
        hb = [self.alloc(D * 2, BF16) for _ in range(2)]
        hbB = [self.buf("E_hb%d" % i) for i in range(2)]
        st = [self.alloc(32 * 4, F32) for _ in range(2)]
        stB = [self.buf("E_st%d" % i) for i in range(2)]
        HTs = self.alloc(KC * 256 * 2, BF16, [KC, 256])
        HTsB = self.buf("E_HTs")
        psf = self.psf
        psB = [self.buf("ps%d" % i) for i in range(7)]
        pr = [0]
        xsrc = self.x if l == 0 else self.Xres
        xoff = tok0 if l == 0 else 0
        xsrcB = [] if l == 0 else [self.buf("Xres")]
        for c0 in range(0, KC, 4):
            p.dma("sp", Wo[:, c0:c0 + 4, :], self.wb_out[l][:, c0:c0 + 4, :], reads=[self.wbuf("out", l, 0)], pwrites=[WoB])
        p.dma("sp", gt, self.ln1g[l:l + 1, :].broadcast_to([128, D]), pwrites=[gbB])
        p.dma("sp", bt, self.ln1b[l:l + 1, :].broadcast_to([128, D]), pwrites=[gbB])
        def load_y(k):
            if k < T // 128:
                p.dma("sp", yt[k % 3], xsrc[xoff + k * 128:xoff + (k + 1) * 128, :], reads=xsrcB, writes=[ytB[k % 3]])

        def load_m(s_):
            if s_ < T // TS:
                for c0 in range(0, KC, 8):
                    p.dma("sp", Ms[s_ % 2][:, c0:c0 + 8, :],
                          self.MT[c0 * 128:(c0 + 8) * 128, s_ * TS:(s_ + 1) * TS].rearrange("(c p) t -> p c t", p=128),
                          reads=[self.buf("MT")], pwrites=[MBs[s_ % 2]])

        load_m(0)
        NB_ = T // 128
        banks = {}

        def mm(gidx):
            sub, tb = divmod(gidx, TS // 128)
            if tb == 0:
                load_m(sub + 1)
            if gidx == 0:
                load_y(0)
            load_y(gidx + 1)
            M, MB = Ms[sub % 2], MBs[sub % 2]
            bl = []
            for n in range(4):
                bi = pr[0] % 6
                pr[0] += 1
                ps = psf[bi]
                fns = [lambda e, kc=kc, ps=ps, tb=tb, n=n, M=M: e.matmul(
                    ps[:, :], lhsT=M[:, kc, tb * 128:(tb + 1) * 128], rhs=Wo[:, kc, n * 512:(n + 1) * 512],
                    start=(kc == 0), stop=(kc == KC - 1)) for kc in range(KC)]
                p.pe(fns, reads=[MB, WoB], writes=[psB[bi]])
                bl.append(bi)
            banks[gidx] = bl

        def adds(gidx):
            y, yB = yt[gidx % 3], ytB[gidx % 3]
            for n, bi in enumerate(banks.pop(gidx)):
                ps = psf[bi]
                p.op("dve", lambda e, y=y, ps=ps, n=n: e.scalar_tensor_tensor(
                    out=y[:, n * 512:(n + 1) * 512], in0=y[:, n * 512:(n + 1) * 512], scalar=ALPHA, in1=ps[:, :],
                    op0=ALU.mult, op1=ALU.add), reads=[psB[bi]], writes=[yB])

        def tail(gidx):
            y, yB = yt[gidx % 3], ytB[gidx % 3]
            i = gidx % 2
            tl = gidx * 128
            self.ln_tile(y, yB, gt, bt, gbB, st[i], stB[i], hb[i], hbB[i])
            p.dma("sp", self.Hres[tl:tl + 128, :], y, reads=[yB], pwrites=[self.buf("Hres")])
            self.transpose_tile(hb[i], hbB[i], HTs, HTsB, (gidx % 2) * 128)
            if gidx % 2 == 1:
                t0 = tl - 128
                p.dma("sp", self.HTd.rearrange("(c p) t -> p c t", p=128)[:, :, t0:t0 + 256], HTs,
                      reads=[HTsB], pwrites=[self.buf("HTd")])

        for gidx in range(NB_):
            mm(gidx)
            if gidx > 0:
                tail(gidx - 1)
            adds(gidx)
        tail(NB_ - 1)

    def phase_F(self, l, h):
        p = self.p
        self.barrier()
        HT = self.alloc(KC * T * 2, BF16, [KC, T])
        HTB = self.buf("F_HT")
        wblk = [self.alloc(KC * 256 * 2, BF16, [KC, 256]) for _ in range(3)]
        wblkB = [self.buf("F_w%d" % i) for i in range(3)]
        U = [[self.alloc((T + 2) * 4, F32) for _ in range(2)] for _ in range(2)]
        UB = [[self.buf("F_U%d_%d" % (r, c)) for c in range(2)] for r in range(2)]
        cv = [[self.alloc(512 * 4, F32) for _ in range(2)] for _ in range(2)]
        cvB = [[self.buf("F_cv%d_%d" % (r, c)) for c in range(2)] for r in range(2)]
        sl = [self.alloc(512 * 4, F32) for _ in range(2)]
        slB = [self.buf("F_sl%d" % i) for i in range(2)]
        gs = [self.alloc(T * 2, BF16) for _ in range(2)]
        gsB = [self.buf("F_gs%d" % i) for i in range(2)]
        cw = self.alloc(88 * 3 * 4, F32, [88, 3])
        cb = self.alloc(88 * 4, F32)
        cwB = self.buf("F_cw")
        psf = self.psf
        psB = [self.buf("ps%d" % i) for i in range(7)]
        pr = [0]
        uh = self.uhalo
        uhB = self.buf("uhalo")
        for c0 in range(0, KC, 4):
            p.dma("sp", HT[:, c0:c0 + 4, :], self.HTd[c0 * 128:(c0 + 4) * 128, :].rearrange("(c p) t -> p c t", p=128),
                  reads=[self.buf("HTd")], pwrites=[HTB])
        p.dma("sp", cw, self.convw[l], pwrites=[cwB])
        p.dma("sp", cb, self.convb[l], pwrites=[cwB])
        def load_w(b):
            if b < DFF // 128:
                p.dma("sp", wblk[b % 3], self.wb_up[l][b], reads=[self.wbuf("up", l, b // 8)], writes=[wblkB[b % 3]])

        load_w(0)
        load_w(1)
        for blk in range(DFF // 128):
            wi = blk % 3
            W = wblk[wi]
            r = blk % 2
            load_w(blk + 2)
            for ch in range(2):
                p.op("act", lambda e, r=r, ch=ch, blk=blk: e.copy(out=U[r][ch][:, 0:2], in_=uh[:, l, 2 * blk + ch, :]),
                     reads=[uhB], pwrites=[UB[r][ch]])
            g_, gB_ = gs[r], gsB[r]
            for tt in range(4):
                tsl = slice(tt * 512, (tt + 1) * 512)
                k = tt % 2
                for ch in range(2):
                    bi = pr[0] % 6
                    pr[0] += 1
                    ps = psf[bi]
                    fns = [lambda e, kc=kc, ps=ps, ch=ch, W=W, tsl=tsl: e.matmul(
                        ps[:, :], lhsT=W[:, kc, ch * 128:(ch + 1) * 128], rhs=HT[:, kc, tsl],
                        start=(kc == 0), stop=(kc == KC - 1)) for kc in range(KC)]
                    p.pe(fns, reads=[wblkB[wi], HTB], writes=[psB[bi]])
                    u = U[r][ch]
                    p.op("act", lambda e, u=u, ps=ps, tt=tt: e.copy(out=u[:, 2 + tt * 512:2 + (tt + 1) * 512], in_=ps[:, :]),
                         reads=[psB[bi]], pwrites=[UB[r][ch]])
                    a = cv[k][ch]
                    ci = 2 * blk + ch
                    t0 = tt * 512
                    p.op("dve", lambda e, a=a, u=u, ci=ci, t0=t0: e.tensor_scalar(
                        out=a, in0=u[:, t0 + 2:t0 + 514], scalar1=cw[:, ci, 2:3], scalar2=cb[:, ci:ci + 1], op0=ALU.mult, op1=ALU.add),
                        reads=[UB[r][ch], cwB], writes=[cvB[k][ch]])
                    p.op("dve", lambda e, a=a, u=u, ci=ci, t0=t0: e.scalar_tensor_tensor(
                        out=a, in0=u[:, t0 + 1:t0 + 513], scalar=cw[:, ci, 1:2], in1=a, op0=ALU.mult, op1=ALU.add),
                        reads=[UB[r][ch], cwB, cvB[k][ch]], writes=[cvB[k][ch]])
                    p.op("dve", lambda e, a=a, u=u, ci=ci, t0=t0: e.scalar_tensor_tensor(
                        out=a, in0=u[:, t0:t0 + 512], scalar=cw[:, ci, 0:1], in1=a, op0=ALU.mult, op1=ALU.add),
                        reads=[UB[r][ch], cwB, cvB[k][ch]], writes=[cvB[k][ch]])
                p.op("act", lambda e, k=k: e.activation(out=sl[k], in_=cv[k][0], func=AF.Silu), reads=[cvB[k][0]], writes=[slB[k]])
                p.op("dve", lambda e, k=k, g_=g_, tsl=tsl: e.tensor_tensor(out=g_[:, tsl], in0=sl[k], in1=cv[k][1], op=ALU.mult),
                     reads=[slB[k], cvB[k][1]], pwrites=[gB_])
            p.dma("sp", self.GF[blk * 128:(blk + 1) * 128, :], g_, reads=[gB_], pwrites=[self.buf("GF")])
            if h == 0:
                for ch in range(2):
                    p.op("act", lambda e, r=r, ch=ch, blk=blk: e.copy(out=uh[:, l, 2 * blk + ch, :], in_=U[r][ch][:, T:T + 2]),
                         reads=[UB[r][ch]], pwrites=[uhB])

    def phase_G(self, l, h):
        p = self.p
        tok0 = h * T
        psf = self.psf
        psB = [self.buf("ps%d" % i) for i in range(7)]
        last = (l == self.nlayers - 1)
        for ps_i, (ka, kb_) in enumerate(((0, 24), (24, 44))):
            nkc = kb_ - ka
            self.barrier()
            Wd = self.alloc(nkc * D * 2, BF16, [nkc, D])
            WdB = self.buf("G_Wd%d" % ps_i)
            gT = [self.alloc(nkc * 256 * 2, BF16, [nkc, 256]) for _ in range(2)]
            gTB = [self.buf("G_gT%d_%d" % (ps_i, i)) for i in range(2)]
            yt = [self.alloc(D * 4, F32) for _ in range(3)]
            ytB = [self.buf("G_y%d_%d" % (ps_i, i)) for i in range(3)]
            if ps_i == 1:
                gt = self.alloc(D * 4, F32)
                bt = self.alloc(D * 4, F32)
                gbB = self.buf("G_gb")
                hb = [self.alloc(D * 2, BF16) for _ in range(2)]
                hbB = [self.buf("G_hb%d" % i) for i in range(2)]
                st = [self.alloc(32 * 4, F32) for _ in range(2)]
                stB = [self.buf("G_st%d" % i) for i in range(2)]
                XTs = self.alloc(KC * 256 * 2, BF16, [KC, 256])
                XTsB = self.buf("G_XTs")
                p.dma("sp", gt, self.ln2g[l:l + 1, :].broadcast_to([128, D]), pwrites=[gbB])
                p.dma("sp", bt, self.ln2b[l:l + 1, :].broadcast_to([128, D]), pwrites=[gbB])
            for c0 in range(0, nkc, 4):
                p.dma("sp", Wd[:, c0:c0 + 4, :], self.wb_down[l][:, ka + c0:ka + c0 + 4, :],
                      reads=[self.wbuf("down", l, (ka + c0) // 4)], pwrites=[WdB])
            pr = [0]

            def load_g(k, gT=gT, gTB=gTB, ka=ka, kb_=kb_):
                if k < T // 256:
                    p.dma("sp", gT[k % 2], self.GF[ka * 128:kb_ * 128, k * 256:(k + 1) * 256].rearrange("(c p) t -> p c t", p=128),
                          reads=[self.buf("GF")], writes=[gTB[k % 2]])

            def load_y(k, yt=yt, ytB=ytB, ps_i=ps_i):
                if k < T // 128:
                    if ps_i == 0:
                        p.dma("sp", yt[k % 3], self.Hres[k * 128:(k + 1) * 128, :], reads=[self.buf("Hres")], writes=[ytB[k % 3]])
                    else:
                        p.dma("sp", yt[k % 3], self.Y1[k * 128:(k + 1) * 128, :], reads=[self.buf("Y1")], writes=[ytB[k % 3]])

            load_g(0)
            load_y(0)
            banks = {}
            NB_ = T // 128

            def mm(tb, gT=gT, gTB=gTB, Wd=Wd, WdB=WdB, nkc=nkc, banks=banks, load_g=load_g, load_y=load_y, pr=pr):
                gi = (tb // 2) % 2
                if tb % 2 == 0:
                    load_g(tb // 2 + 1)
                load_y(tb + 1)
                bl = []
                for n in range(4):
                    bi = pr[0] % 6
                    pr[0] += 1
                    ps = psf[bi]
                    fns = [lambda e, kc=kc, ps=ps, tb=tb, n=n, g_=gT[gi], Wd=Wd, nkc=nkc: e.matmul(
                        ps[:, :], lhsT=g_[:, kc, (tb % 2) * 128:(tb % 2) * 128 + 128], rhs=Wd[:, kc, n * 512:(n + 1) * 512],
                        start=(kc == 0), stop=(kc == nkc - 1)) for kc in range(nkc)]
                    p.pe(fns, reads=[gTB[gi], WdB], writes=[psB[bi]])
                    bl.append(bi)
                banks[tb] = bl

            def adds(tb, yt=yt, ytB=ytB, banks=banks, ps_i=ps_i):
                y, yB = yt[tb % 3], ytB[tb % 3]
                sc = ALPHA if ps_i == 0 else 1.0
                for n, bi in enumerate(banks.pop(tb)):
                    ps = psf[bi]
                    p.op("dve", lambda e, y=y, ps=ps, n=n, sc=sc: e.scalar_tensor_tensor(
                        out=y[:, n * 512:(n + 1) * 512], in0=y[:, n * 512:(n + 1) * 512], scalar=sc, in1=ps[:, :],
                        op0=ALU.mult, op1=ALU.add), reads=[psB[bi]], writes=[yB])

            if ps_i == 0:
                def tail(tb, yt=yt, ytB=ytB):
                    y, yB = yt[tb % 3], ytB[tb % 3]
                    tl = tb * 128
                    p.dma("sp", self.Y1[tl:tl + 128, :], y, reads=[yB], pwrites=[self.buf("Y1")])
            else:
                def tail(tb, yt=yt, ytB=ytB, gt=gt, bt=bt, gbB=gbB, st=st, stB=stB, hb=hb, hbB=hbB, XTs=XTs, XTsB=XTsB):
                    y, yB = yt[tb % 3], ytB[tb % 3]
                    i = tb % 2
                    tl = tb * 128
                    self.ln_tile(y, yB, gt, bt, gbB, st[i], stB[i], None if last else hb[i], None if last else hbB[i])
                    if last:
                        p.dma("sp", self.y[tok0 + tl:tok0 + tl + 128, :], y, reads=[yB], pwrites=[self.buf("y_out")])
                    else:
                        p.dma("sp", self.Xres[tl:tl + 128, :], y, reads=[yB], pwrites=[self.buf("Xres")])
                        self.transpose_tile(hb[i], hbB[i], XTs, XTsB, (tb % 2) * 128)
                        if tb % 2 == 1:
                            t0 = tl - 128
                            p.dma("sp", self.XTd.rearrange("(c p) t -> p c t", p=128)[:, :, t0:t0 + 256], XTs,
                                  reads=[XTsB], pwrites=[self.buf("XTd")])

            for tb in range(NB_):
                mm(tb)
                if tb > 0:
                    tail(tb - 1)
                adds(tb)
            tail(NB_ - 1)

    def load_consts(self):
        p = self.p
        self.persist_end = 0
        self.aoff = 0
        self.ident_sb = self.alloc(128 * 2, BF16)
        self.masks_sb = self.alloc(4 * 128 * 2, BF16, [4, 128])
        p.dma("sp", self.ident_sb, self.ident, writes=[self.buf("ident")])
        p.dma("sp", self.masks_sb, self.masks, writes=[self.buf("masks")])
        self.maskrep = self.alloc(2 * 512 * 2, BF16, [2, 512])
        for kb, mi in ((0, 1), (1, 0)):
            for hh in range(4):
                p.op("dve", lambda e, kb=kb, mi=mi, hh=hh: e.tensor_copy(
                    out=self.maskrep[:, kb, hh * 128:(hh + 1) * 128], in_=self.masks_sb[:, mi, :]),
                    reads=[self.buf("masks")], pwrites=[self.buf("maskrep")])
        self.maskpair = self.alloc(2 * 256 * 2, BF16, [2, 256])
        for (t_, half, mi) in ((0, 0, 0), (0, 1, 3), (1, 0, 2), (1, 1, 0)):
            p.op("dve", lambda e, t_=t_, half=half, mi=mi: e.tensor_copy(
                out=self.maskpair[:, t_, half * 128:(half + 1) * 128], in_=self.masks_sb[:, mi, :]),
                reads=[self.buf("masks")], pwrites=[self.buf("maskpair")])
        self.uhalo = self.alloc(DEPTH * 88 * 2 * 4, F32, [DEPTH, 88, 2])
        p.op("dve", lambda e: e.memset(self.uhalo, 0.0), writes=[self.buf("uhalo")])
        self.persist_end = self.aoff

    def emit(self):
        self.conv_setup()
        self.load_consts()
        final = []
        for l in range(self.nlayers):
            self.convert_layer(l)
        for h in range(self.nhalves):
            for l in range(self.nlayers):
                for nm in "ABCDEFG":
                    getattr(self, "phase_" + nm)(l, h)
                    if self.stop == nm:
                        break
        allb = [b for b in self.bufs.values()]
        self.p.finish(allb)


_CACHE = {}


def _host_consts():
    if "c" in _CACHE:
        return _CACHE["c"]
    c = dict(_rope_tables())
    k = np.arange(128)[:, None]
    q = np.arange(128)[None, :]
    m = np.zeros((128, 4, 128), dtype=np.float32)
    m[:, 0, :] = np.where(k <= q, 0.0, NEG)
    m[:, 1, :] = np.where(k > q, 0.0, NEG)
    m[:, 2, :] = NEG
    c["masks"] = m.astype(ml_dtypes.bfloat16)
    c["ident"] = np.eye(128, dtype=np.float32).astype(ml_dtypes.bfloat16)
    _CACHE["c"] = c
    return c


def prepare_shared(inp):
    f = lambda a: np.ascontiguousarray(np.asarray(a, dtype=np.float32))
    sh = {}
    sh["w_in"] = np.ascontiguousarray(f(inp["w_in"])[:, :, _win_perm()])
    up = _wup_perm()
    sh["w_up"] = np.ascontiguousarray(f(inp["w_up"])[:, :, up])
    cw = f(inp["conv_w"])[:, :, up]
    sh["conv_w"] = np.ascontiguousarray(cw.reshape(DEPTH, 3, 88, 128).transpose(0, 3, 2, 1))
    cb = f(inp["conv_b"])[:, up]
    sh["conv_b"] = np.ascontiguousarray(cb.reshape(DEPTH, 88, 128).transpose(0, 2, 1))
    for k in ("w_proj_a", "w_proj_b", "w_out", "w_down", "sinks", "subln_g", "ln1_g", "ln1_b", "ln2_g", "ln2_b"):
        sh[k] = f(inp[k])
    sh["lam"] = np.ascontiguousarray(np.stack([f(inp["lambda_q1"]), f(inp["lambda_k1"]),
                                               f(inp["lambda_q2"]), f(inp["lambda_k2"])], axis=1))
    sh.update(_host_consts())
    return sh


def kernel(**inputs):
    x = np.asarray(inputs["x"], dtype=np.float32)
    sh = prepare_shared(inputs)
    kern = Kern()
    nc = kern.build()
    work = {0: 0, 1: 1, 4: 2, 5: 3}
    zero = {k: np.zeros_like(v) for k, v in sh.items()}
    zero["x"] = np.zeros_like(x[0])
    in_maps = []
    for c in range(NCORES):
        if c in work:
            m = dict(sh)
            m["x"] = np.ascontiguousarray(x[work[c]])
        else:
            m = zero
        in_maps.append(m)
    res = run_bass_kernel_spmd(nc, in_maps, core_ids=list(range(NCORES)))
    inv = {b: c for c, b in work.items()}
    out = np.stack([np.asarray(res.results[inv[b]]["y"], dtype=np.float32) for b in range(BATCH)], axis=0)
    return out
```
